# Optimizing a Trainium2 kernel written in Bass

```python
import math
import jax, jax.numpy as jnp
from jax import lax
import numpy as np

D_MODEL = 4096
BATCH = 4
SEQ = 4096
DEPTH = 1

D_MIX = D_MODEL
D_ATTN = D_MIX // 2
D_SSM = D_MIX - D_ATTN
HEAD_DIM = 64
N_Q_HEADS = D_ATTN // HEAD_DIM
N_KV_HEADS = max(1, N_Q_HEADS // 8)
Q_PER_KV = N_Q_HEADS // N_KV_HEADS
D_KV = N_KV_HEADS * HEAD_DIM
WINDOW = 128
BLOCK = WINDOW
ROPE_THETA = 10000.0
SSM_GROUP = 16
N_SSM_GROUPS = D_SSM // SSM_GROUP
STATE = 64
DT_MIN = 1e-3
DT_MAX = 1e-1
D_FF = 4 * D_MODEL
N_MOD = 6
EPS = 1e-6
D_IN = D_ATTN + 2 * D_KV + D_SSM

kernel_name = "hymba_s5_swa_sink_adaln_block"


def rmsnorm(x, g):
    xf = x.astype(jnp.float32)
    y = xf * lax.rsqrt(jnp.mean(xf * xf, axis=-1, keepdims=True) + EPS)
    return (y * g.astype(jnp.float32)).astype(x.dtype)


def rope(x):
    s = x.shape[1]
    half = x.shape[-1] // 2
    inv_freq = ROPE_THETA ** (-jnp.arange(half, dtype=jnp.float32) / half)
    ang = jnp.arange(s, dtype=jnp.float32)[:, None] * inv_freq[None, :]
    cos = jnp.cos(ang)[None, :, None, :]
    sin = jnp.sin(ang)[None, :, None, :]
    xf = x.astype(jnp.float32)
    x1, x2 = xf[..., :half], xf[..., half:]
    out = jnp.concatenate([x1 * cos - x2 * sin, x2 * cos + x1 * sin], axis=-1)
    return out.astype(x.dtype)


def sliding_window_attention(q, k, v, sinks):
    b, s = q.shape[0], q.shape[1]
    nb = s // BLOCK
    qb = q.reshape(b, nb, BLOCK, N_KV_HEADS, Q_PER_KV, HEAD_DIM).astype(jnp.float32)
    kb = k.reshape(b, nb, BLOCK, N_KV_HEADS, HEAD_DIM).astype(jnp.float32)
    vb = v.reshape(b, nb, BLOCK, N_KV_HEADS, HEAD_DIM).astype(jnp.float32)
    kk = jnp.concatenate([jnp.concatenate([jnp.zeros_like(kb[:, :1]), kb[:, :-1]], axis=1), kb], axis=2)
    vv = jnp.concatenate([jnp.concatenate([jnp.zeros_like(vb[:, :1]), vb[:, :-1]], axis=1), vb], axis=2)
    scores = jnp.einsum('bnqhgd,bnkhd->bnhgqk', qb, kk) * (HEAD_DIM ** -0.5)
    qi = jnp.arange(BLOCK)[:, None] + BLOCK
    kj = jnp.arange(2 * BLOCK)[None, :]
    rel = qi - kj
    band = (rel >= 0) & (rel < WINDOW)
    key_ok = (jnp.arange(nb)[:, None] > 0) | (jnp.arange(2 * BLOCK)[None, :] >= BLOCK)
    mask = band[None, :, :] & key_ok[:, None, :]
    scores = jnp.where(mask[None, :, None, None], scores, jnp.float32(-1e30))
    sink = sinks.astype(jnp.float32).reshape(N_KV_HEADS, Q_PER_KV)[None, None, :, :, None, None]
    m = jnp.maximum(jnp.max(scores, axis=-1, keepdims=True), sink)
    p = jnp.exp(scores - m)
    probs = p / (jnp.sum(p, axis=-1, keepdims=True) + jnp.exp(sink - m))
    out = jnp.einsum('bnhgqk,bnkhd->bnqhgd', probs, vv)
    return out.reshape(b, s, N_Q_HEADS * HEAD_DIM).astype(q.dtype)


def _scan_op(e1, e2):
    a1, b1 = e1
    a2, b2 = e2
    return a2 * a1, a2 * b1 + b2


def s5_mixer(u, lam_re, lam_im, log_step, b_re, b_im, c_re, c_im, d_skip, w_glu, b_glu):
    bsz, s = u.shape[0], u.shape[1]
    f32 = jnp.float32
    uf = u.astype(f32).reshape(bsz, s, N_SSM_GROUPS, SSM_GROUP)
    step = jnp.exp(log_step.astype(f32))[:, None]
    lam = lax.complex(lam_re.astype(f32), lam_im.astype(f32))
    lam_bar = jnp.exp(lam * step)
    coef = (lam_bar - 1.0) / lam
    b_bar = coef[..., None] * lax.complex(b_re.astype(f32), b_im.astype(f32))
    bu = jnp.einsum('bsgh,gph->sbgp', uf.astype(jnp.complex64), b_bar)
    a = jnp.broadcast_to(lam_bar[None, None], (s, 1, N_SSM_GROUPS, STATE))
    _, states = lax.associative_scan(_scan_op, (a, bu), axis=0)
    c_mat = lax.complex(c_re.astype(f32), c_im.astype(f32))
    y = jnp.real(jnp.einsum('sbgp,ghp->bsgh', states, c_mat))
    y = y + d_skip.astype(f32).reshape(N_SSM_GROUPS, SSM_GROUP) * uf
    y = jax.nn.gelu(y.reshape(bsz, s, D_SSM), approximate=False)
    out = y * jax.nn.sigmoid(y @ w_glu.astype(f32) + b_glu.astype(f32))
    return out.astype(u.dtype)


def setup_inputs(seed: int = 0) -> dict:
    key = jax.random.key(seed)
    ks = jax.random.split(key, 24)
    f32 = jnp.float32
    nrm = lambda k, shape, sc: jax.random.normal(k, shape, f32) * sc
    inputs = {
        "x": nrm(ks[0], (BATCH, SEQ, D_MODEL), 1.0),
        "c": nrm(ks[1], (BATCH, D_MODEL), 1.0),
        "w_ada": nrm(ks[2], (DEPTH, D_MODEL, N_MOD * D_MODEL), 0.5 * D_MODEL ** -0.5),
        "b_ada": nrm(ks[3], (DEPTH, N_MOD * D_MODEL), 0.01),
        "norm1_g": 1.0 + nrm(ks[4], (DEPTH, D_MODEL), 0.02),
        "w_in": nrm(ks[5], (DEPTH, D_MODEL, D_IN), D_MODEL ** -0.5),
        "sinks": nrm(ks[6], (DEPTH, N_Q_HEADS), 0.5),
        "ssm_lam_re": -0.5 + nrm(ks[7], (DEPTH, N_SSM_GROUPS, STATE), 0.01),
        "ssm_lam_im": jnp.pi * jnp.arange(STATE, dtype=f32)[None, None, :] + nrm(ks[8], (DEPTH, N_SSM_GROUPS, STATE), 0.01),
        "ssm_log_step": jax.random.uniform(ks[9], (DEPTH, N_SSM_GROUPS), f32, math.log(DT_MIN), math.log(DT_MAX)),
        "ssm_b_re": nrm(ks[10], (DEPTH, N_SSM_GROUPS, STATE, SSM_GROUP), (2 * SSM_GROUP) ** -0.5),
        "ssm_b_im": nrm(ks[11], (DEPTH, N_SSM_GROUPS, STATE, SSM_GROUP), (2 * SSM_GROUP) ** -0.5),
        "ssm_c_re": nrm(ks[12], (DEPTH, N_SSM_GROUPS, SSM_GROUP, STATE), (2 * STATE) ** -0.5),
        "ssm_c_im": nrm(ks[13], (DEPTH, N_SSM_GROUPS, SSM_GROUP, STATE), (2 * STATE) ** -0.5),
        "ssm_d": nrm(ks[14], (DEPTH, D_SSM), 1.0),
        "w_glu": nrm(ks[15], (DEPTH, D_SSM, D_SSM), D_SSM ** -0.5),
        "b_glu": nrm(ks[16], (DEPTH, D_SSM), 0.01),
        "attn_out_g": 1.0 + nrm(ks[17], (DEPTH, D_ATTN), 0.02),
        "ssm_out_g": 1.0 + nrm(ks[18], (DEPTH, D_SSM), 0.02),
        "w_out": nrm(ks[19], (DEPTH, D_MIX, D_MODEL), D_MIX ** -0.5),
        "norm2_g": 1.0 + nrm(ks[20], (DEPTH, D_MODEL), 0.02),
        "w_ff1": nrm(ks[21], (DEPTH, D_MODEL, D_FF), D_MODEL ** -0.5),
        "w_ff2": nrm(ks[22], (DEPTH, D_FF, D_MODEL), D_FF ** -0.5),
        "final_g": 1.0 + nrm(ks[23], (D_MODEL,), 0.02),
    }
    return inputs


def reference(x, c, w_ada, b_ada, norm1_g, w_in, sinks, ssm_lam_re, ssm_lam_im, ssm_log_step,
              ssm_b_re, ssm_b_im, ssm_c_re, ssm_c_im, ssm_d, w_glu, b_glu, attn_out_g, ssm_out_g,
              w_out, norm2_g, w_ff1, w_ff2, final_g):
    bsz, s, _ = x.shape
    c_act = jax.nn.silu(c.astype(jnp.float32))
    for l in range(DEPTH):
        mod = (c_act @ w_ada[l].astype(jnp.float32) + b_ada[l].astype(jnp.float32)).astype(x.dtype)
        shift1, scale1, gate1, shift2, scale2, gate2 = [m[:, None, :] for m in jnp.split(mod, N_MOD, axis=-1)]

        h = rmsnorm(x, norm1_g[l]) * (1.0 + scale1) + shift1
        proj = h @ w_in[l]
        q = proj[..., :D_ATTN].reshape(bsz, s, N_Q_HEADS, HEAD_DIM)
        k = proj[..., D_ATTN:D_ATTN + D_KV].reshape(bsz, s, N_KV_HEADS, HEAD_DIM)
        v = proj[..., D_ATTN + D_KV:D_ATTN + 2 * D_KV].reshape(bsz, s, N_KV_HEADS, HEAD_DIM)
        u = proj[..., D_ATTN + 2 * D_KV:]
        attn = sliding_window_attention(rope(q), rope(k), v, sinks[l])
        ssm = s5_mixer(u, ssm_lam_re[l], ssm_lam_im[l], ssm_log_step[l], ssm_b_re[l], ssm_b_im[l],
                       ssm_c_re[l], ssm_c_im[l], ssm_d[l], w_glu[l], b_glu[l])
        mixed = jnp.concatenate([rmsnorm(attn, attn_out_g[l]), rmsnorm(ssm, ssm_out_g[l])], axis=-1)
        x = x + gate1 * (mixed @ w_out[l])

        h2 = rmsnorm(x, norm2_g[l]) * (1.0 + scale2) + shift2
        ff = jnp.square(jax.nn.relu(h2 @ w_ff1[l])) @ w_ff2[l]
        x = x + gate2 * ff
    return rmsnorm(x, final_g)
```

```python
import contextlib
import math
import numpy as np
import concourse.bass as bass
import concourse.mybir as mybir
from concourse.bass_utils import run_bass_kernel_spmd

F32 = mybir.dt.float32
BF16 = mybir.dt.bfloat16
AF = mybir.ActivationFunctionType
ALU = mybir.AluOpType

D = 4096
KC = 32
T = 512
NTILE = 4
NPRE = 4
DFF = 16384
EPS = 1e-6
WINP = 5376
NW = 4
MAGIC = 12582912.0
TWO_PI = 2.0 * math.pi


class _Op:
    __slots__ = ("eng", "fn", "deps", "is_dma", "sem", "semval", "needs_inc", "incval")

    def __init__(self, eng, fn):
        self.eng = eng
        self.fn = fn
        self.deps = []
        self.is_dma = False
        self.sem = None
        self.semval = 0
        self.needs_inc = False
        self.incval = 0


class Sched:
    ENGS = ("pe", "act", "dve", "pool", "sp")

    def __init__(self, nc, es):
        self.nc = nc
        self.es = es
        self.ops = []
        self.W = {}
        self.R = {}
        self.dsems = {}
        self.dcount = {}

    def _deps(self, op, reads, writes):
        deps = op.deps
        for (k, lo, hi) in reads:
            for (l2, h2, o2) in self.W.get(k, ()):
                if l2 < hi and lo < h2:
                    deps.append(o2)
            self.R.setdefault(k, []).append((lo, hi, op))
        for (k, lo, hi) in writes:
            wl = self.W.get(k, [])
            rl = self.R.get(k, [])
            for (l2, h2, o2) in wl:
                if l2 < hi and lo < h2:
                    deps.append(o2)
            for (l2, h2, o2) in rl:
                if l2 < hi and lo < h2 and o2 is not op:
                    deps.append(o2)
            self.W[k] = [t for t in wl if not (lo <= t[0] and t[1] <= hi)] + [(lo, hi, op)]
            self.R[k] = [t for t in rl if not (lo <= t[0] and t[1] <= hi) or t[2] is op]

    def op(self, eng, fn, reads=(), writes=()):
        o = _Op(eng, fn)
        self._deps(o, reads, writes)
        self.ops.append(o)
        return o

    def dma(self, eng, out, in_, reads=(), writes=(), key=None):
        o = _Op(eng, None)
        o.is_dma = True
        if key not in self.dsems:
            self.dsems[key] = self.es.enter_context(self.nc.semaphore("d_" + key))
            self.dcount[key] = 0
        self.dcount[key] += 16
        o.sem = self.dsems[key]
        o.semval = self.dcount[key]
        o.fn = lambda e, out=out, in_=in_: e.dma_start(out=out, in_=in_)
        self._deps(o, reads, writes)
        self.ops.append(o)
        return o

    def emit(self):
        nc = self.nc
        esem = {e: self.es.enter_context(nc.semaphore("e_" + e)) for e in self.ENGS}
        fin = _Op("sp", None)
        last = {}
        for o in self.ops:
            if o.is_dma:
                last[("d", id(o.sem))] = o
            else:
                last[("e", o.eng)] = o
        fin.deps = list(last.values())
        self.ops.append(fin)
        idx = {id(o): i for i, o in enumerate(self.ops)}
        for o in self.ops:
            best = {}
            for d in o.deps:
                k = ("d", id(d.sem)) if d.is_dma else ("e", d.eng)
                if k not in best or idx[id(d)] > idx[id(best[k])]:
                    best[k] = d
            o.deps = list(best.values())
        for o in self.ops:
            for d in o.deps:
                if not d.is_dma:
                    if d.eng == "pe" and o.eng == "pe" and not o.is_dma:
                        continue
                    d.needs_inc = True
        cnt = {e: 0 for e in self.ENGS}
        for o in self.ops:
            if (not o.is_dma) and o.needs_inc:
                cnt[o.eng] += 1
                o.incval = cnt[o.eng]
        per = {e: [] for e in self.ENGS}
        for o in self.ops:
            per[o.eng].append(o)
        self.stats = {e: len(per[e]) for e in self.ENGS}
        self.stats["incs"] = dict(cnt)

        def run(engname, e):
            waited = {}
            for o in per[engname]:
                for d in o.deps:
                    if d.is_dma:
                        s, v = d.sem, d.semval
                    else:
                        if d.eng == "pe" and engname == "pe" and not o.is_dma:
                            continue
                        s, v = esem[d.eng], d.incval
                    kk = id(s)
                    if waited.get(kk, 0) >= v:
                        continue
                    waited[kk] = v
                    e.wait_ge(s, v)
                if o.fn is None:
                    continue
                ins = o.fn(e)
                if o.is_dma:
                    ins.then_inc(o.sem, 16)
                elif o.needs_inc:
                    ins.then_inc(esem[engname], 1)

        with nc.Block() as block:
            @block.tensor
            def _(e):
                run("pe", e)

            @block.scalar
            def _(e):
                run("act", e)

            @block.vector
            def _(e):
                run("dve", e)

            @block.gpsimd
            def _(e):
                run("pool", e)

            @block.sync
            def _(e):
                run("sp", e)


class Buf:
    def __init__(self, nc, es, name, shape, dtype, psum=False):
        self.name = name
        if psum:
            self.t = es.enter_context(nc.psum_tensor("p_" + name, shape, dtype))
        else:
            self.t = es.enter_context(nc.sbuf_tensor("s_" + name, shape, dtype))
        n = 1
        for s in shape[1:]:
            n *= s
        self.n = n

    def all(self):
        return (self.name, 0, self.n)

    def r(self, lo, hi):
        return (self.name, lo, hi)


SMALL_IN = [
    ("cT", [128, 32]), ("b_adaT", [128, 192]), ("n1g", [128, 32]), ("n2g", [128, 32]),
    ("fing", [128, 32]), ("sinks_bc", [128, 32]), ("flag", [128, 1]),
    ("lamre_p", [128, 64]), ("lamim_p", [128, 64]), ("lstep_p", [128, 64]),
    ("b_gluT", [128, 16]), ("ssm_gT", [128, 16]), ("attn_gT", [128, 16]), ("identf", [128, 128]),
]
CAST_IN = [
    ("cre_e", [128, 64, 32]), ("cim_e", [128, 64, 32]), ("d_e", [128, 16, 32]),
    ("mask_n", [128, 256]), ("mask_f", [128, 256]), ("identb", [128, 128]),
]
BIG_IN = [
    ("lamre_e", [128, 2048]), ("lamim_e", [128, 2048]), ("lstep_e", [128, 2048]),
    ("bre_e", [128, 2048]), ("bim_e", [128, 2048]),
]


def build(debug=None):
    nc = bass.Bass("TRN2", target_bir_lowering=False)
    dr = {}

    def din(name, shape):
        dr[name] = nc.dram_tensor(name, shape, F32, kind="ExternalInput").ap()

    din("xcat", [4096, D])
    for n, s in SMALL_IN + CAST_IN + BIG_IN:
        din(n, s)
    din("iota", [128, 512])
    din("ropec", [128, 4096])
    din("ropes", [128, 4096])
    din("w_ada", [D, 6 * D])
    din("w_in", [D, WINP])
    din("w_glu", [2048, 2048])
    din("w_out", [D, D])
    din("w_ff1", [D, DFF])
    din("w_ff2", [DFF, D])
    out_d = nc.dram_tensor("out", [2048, D], F32, kind="ExternalOutput").ap()
    tabs = nc.dram_tensor("tabs", [64, 2, 128, 512], F32, kind="Internal").ap()
    wtabs = nc.dram_tensor("wtabs", [64, 2, 128, 512], F32, kind="Internal").ap()

    with contextlib.ExitStack() as es:
        S = Sched(nc, es)
        B = lambda name, shape, dt=F32, psum=False: Buf(nc, es, name, shape, dt, psum)
        acc = B("acc", [128, KC, T])
        hb = B("hb", [128, KC, T], BF16)
        wr = B("wr", [128, NW, 4, 512], BF16)
        Kb = B("Kb", [128, 8, 640], BF16)
        Va = B("Va", [128, 5, 4, 65], BF16)
        U = B("U", [128, 13312])
        ps = [B("ps%d" % i, [128, 512], F32, psum=True) for i in range(8)]
        sm = {n: B(n, s) for n, s in SMALL_IN}
        cb = {n: B(n, s, BF16) for n, s in CAST_IN}
        Bexp = B("Bexp", [128, 16, 2, 128], BF16)
        mods = B("mods", [128, 192])
        gs1 = B("gs1", [128, 32]); gs2 = B("gs2", [128, 32])
        cact = B("cact", [128, 32], BF16)
        esink = B("esink", [128, 32])
        ones = B("ones", [128, 128])
        rr = B("rr", [128, 64]); cos1 = B("cos1", [128, 64]); sin1 = B("sin1", [128, 64]); phi = B("phi", [128, 64])
        car = B("car", [128, 2, 64]); wini = B("wini", [128, 2, 64]); ctmp = B("ctmp", [128, 4, 64])
        halfpi = B("halfpi", [128, 1])
        magic = B("magic", [128, 1]); magic2 = B("magic2", [128, 1])
        sml = B("sml", [128, 16])

        def uf(lo_b, n):
            return U.t[:, lo_b // 4: lo_b // 4 + n]

        def ubf(lo_b, n):
            return U.t[:, lo_b // 4: lo_b // 4 + n // 2].bitcast(BF16)

        def ur(lo_b, nbytes):
            return ("U", lo_b // 4, (lo_b + nbytes) // 4)

        K1 = 1024
        Qv = ubf(0, 16 * 512).rearrange("p (c t) -> p c t", c=16)
        gbv = Qv
        ubv = ubf(16 * K1, 16 * 512).rearrange("p (c t) -> p c t", c=16)

        def Qr(c):
            return ur(c * 1024, 1024)

        def ubr(c):
            return ur(16 * K1 + c * 1024, 1024)

        X0 = 32 * K1

        pctr = [0]

        def pnext():
            b = pctr[0] % 7
            pctr[0] += 1
            return b

        wctr = [0]

        def wtile(Wd, r0, c0, ncols=512):
            s = wctr[0] % NW
            wctr[0] += 1
            S.dma("pool", wr.t[:, s, :, 0:ncols],
                  Wd[r0:r0 + 512, c0:c0 + ncols].rearrange("(a p) n -> p a n", p=128),
                  writes=[wr.r(s * 2048, (s + 1) * 2048)], key="w%d" % s)
            return s

        def wreg(s):
            return wr.r(s * 2048, (s + 1) * 2048)

        def dve(fn, reads=(), writes=()):
            return S.op("dve", fn, reads, writes)

        def act(fn, reads=(), writes=()):
            return S.op("act", fn, reads, writes)

        def pe(fn, reads=(), writes=()):
            return S.op("pe", fn, reads, writes)

        def group_mm(Wd, nk4, c0, rhs, epi, nj=4):
            banks = [pnext() for _ in range(nj)]
            nk = nk4 * 4
            for k4 in range(nk4):
                s = wtile(Wd, k4 * 512, c0, nj * 128)
                for a in range(4):
                    kc = k4 * 4 + a
                    rap, rreg = rhs(kc)
                    for j in range(nj):
                        pe(lambda e, b=banks[j], s=s, a=a, j=j, rap=rap, kc=kc:
                           e.matmul(ps[b].t[:], wr.t[:, s, a, j * 128:(j + 1) * 128], rap,
                                    start=(kc == 0), stop=(kc == nk - 1)),
                           reads=[wreg(s), rreg], writes=[ps[banks[j]].all()])
            for j in range(nj):
                epi(j, banks[j])

        def dump(name, ap, reads):
            if debug and name in debug:
                dd = nc.dram_tensor("dbg_" + name, list(ap.shape), ap.dtype, kind="ExternalOutput").ap()
                S.dma("sp", dd, ap, reads=reads, key="dbg_" + name)

        for n, s in SMALL_IN:
            S.dma("sp", sm[n].t[:], dr[n], writes=[sm[n].all()], key="ld_" + n)
        for n, s in CAST_IN:
            S.dma("pool", cb[n].t[:], dr[n], writes=[cb[n].all()], key="ld_" + n)
        dve(lambda e: e.memset(ones.t[:], 1.0), writes=[ones.all()])
        dve(lambda e: e.memset(halfpi.t[:], math.pi / 2), writes=[halfpi.all()])
        dve(lambda e: e.memset(magic.t[:], MAGIC), writes=[magic.all()])
        dve(lambda e: e.memset(magic2.t[:], -MAGIC), writes=[magic2.all()])
        dve(lambda e: e.memset(Va.t[:], 0.0), writes=[Va.all()])
        dve(lambda e: e.memset(Va.t[:, :, :, 64:65], 1.0), writes=[Va.all()])
        dve(lambda e: e.memset(Kb.t[:], 0.0), writes=[Kb.all()])
        dve(lambda e: e.memset(car.t[:], 0.0), writes=[car.all()])
        act(lambda e: e.activation(out=cact.t[:], in_=sm["cT"].t[:], func=AF.Silu),
            reads=[sm["cT"].all()], writes=[cact.all()])
        act(lambda e: e.activation(out=esink.t[:], in_=sm["sinks_bc"].t[:], func=AF.Exp),
            reads=[sm["sinks_bc"].all()], writes=[esink.all()])

        def ada_all(slabs, banks):
            for slab in slabs:
                for k4 in range(8):
                    s = wtile(dr["w_ada"], k4 * 512, slab * 512)
                    for a in range(4):
                        kc = k4 * 4 + a
                        for j in range(4):
                            pe(lambda e, s=s, a=a, j=j, kc=kc, b=banks[j]:
                               e.matmul(ps[b].t[:, 0:1], wr.t[:, s, a, j * 128:(j + 1) * 128],
                                        cact.t[:, kc:kc + 1], start=(kc == 0), stop=(kc == 31)),
                               reads=[wreg(s), cact.all()], writes=[ps[banks[j]].all()])
                    if k4 == 7:
                        for j in range(4):
                            col = 4 * slab + j
                            dve(lambda e, b=banks[j], col=col: e.tensor_tensor(out=mods.t[:, col:col + 1], in0=ps[b].t[:, 0:1],
                                                                              in1=sm["b_adaT"].t[:, col:col + 1], op=ALU.add),
                                reads=[ps[banks[j]].all(), sm["b_adaT"].all()], writes=[mods.r(col, col + 1)])
                    yield

        sh1 = mods.t[:, 0:32]; sc1 = mods.t[:, 32:64]; gt1 = mods.t[:, 64:96]
        sh2 = mods.t[:, 96:128]; sc2 = mods.t[:, 128:160]; gt2 = mods.t[:, 160:192]

        def sincos(ang, n, tmpA, tmpB, sin_out, cos_out, rd, wrs):
            act(lambda e: e.activation(out=tmpA, in_=ang, func=AF.Identity, scale=1.0 / TWO_PI, bias=magic.t[:]), reads=rd + [magic.all()], writes=wrs)
            act(lambda e: e.activation(out=tmpB, in_=tmpA, func=AF.Identity, scale=1.0, bias=magic2.t[:]), reads=wrs + [magic2.all()], writes=wrs)
            dve(lambda e: e.scalar_tensor_tensor(out=ang, in0=tmpB, scalar=-TWO_PI, in1=ang, op0=ALU.mult, op1=ALU.add), reads=rd + wrs, writes=rd)
            dve(lambda e: e.tensor_scalar(ang, ang, 3.1415925, -3.1415925, op0=ALU.min, op1=ALU.max), reads=rd, writes=rd)
            act(lambda e: e.activation(out=sin_out, in_=ang, func=AF.Sin), reads=rd, writes=wrs)
            dve(lambda e: e.scalar_tensor_tensor(out=tmpA, in0=ang, scalar=-1.0, in1=ang, op0=ALU.mult, op1=ALU.max), reads=rd, writes=wrs)
            act(lambda e: e.activation(out=cos_out, in_=tmpA, func=AF.Sin, bias=halfpi.t[:], scale=-1.0),
                reads=wrs + [halfpi.all()], writes=wrs)

        def sincos_g(ang, tmpA, tmpB, sin_out, cos_out, rd, wrs):
            act(lambda e: e.activation(out=tmpA, in_=ang, func=AF.Identity, scale=1.0 / TWO_PI, bias=magic.t[:]), reads=rd + [magic.all()], writes=wrs)
            yield
            act(lambda e: e.activation(out=tmpB, in_=tmpA, func=AF.Identity, scale=1.0, bias=magic2.t[:]), reads=wrs + [magic2.all()], writes=wrs)
            yield
            dve(lambda e: e.scalar_tensor_tensor(out=ang, in0=tmpB, scalar=-TWO_PI, in1=ang, op0=ALU.mult, op1=ALU.add), reads=rd + wrs, writes=rd)
            yield
            dve(lambda e: e.tensor_scalar(ang, ang, 3.1415925, -3.1415925, op0=ALU.min, op1=ALU.max), reads=rd, writes=rd)
            yield
            act(lambda e: e.activation(out=sin_out, in_=ang, func=AF.Sin), reads=rd, writes=wrs)
            yield
            dve(lambda e: e.scalar_tensor_tensor(out=tmpA, in0=ang, scalar=-1.0, in1=ang, op0=ALU.mult, op1=ALU.max), reads=rd, writes=wrs)
            yield
            act(lambda e: e.activation(out=cos_out, in_=tmpA, func=AF.Sin, bias=halfpi.t[:], scale=-1.0),
                reads=wrs + [halfpi.all()], writes=wrs)
            yield

        av = acc.t[:].rearrange("p c t -> p (c t)")

        def a_(i):
            return av[:, i * 2048:(i + 1) * 2048]

        AR = [acc.all()]
        for i, n in enumerate(["lamre_e", "lamim_e", "lstep_e", "bre_e", "bim_e"]):
            S.dma("sp", a_(i), dr[n], writes=AR, key="ld_big")
        LRE, LIM, LST, BRE, BIM, T5, T6, T7 = [a_(i) for i in range(8)]
        act(lambda e: e.activation(out=LST, in_=LST, func=AF.Exp), reads=AR, writes=AR)
        dve(lambda e: e.tensor_tensor(out=T5, in0=LIM, in1=LST, op=ALU.mult), reads=AR, writes=AR)
        dve(lambda e: e.tensor_tensor(out=LST, in0=LRE, in1=LST, op=ALU.mult), reads=AR, writes=AR)
        act(lambda e: e.activation(out=LST, in_=LST, func=AF.Exp), reads=AR, writes=AR)
        hbf = hb.t[:].rearrange("p c t -> p (c t)").bitcast(F32)
        HR = [hb.all()]
        sincos(T5, 2048, hbf[:, 0:2048], hbf[:, 2048:4096], T6, T7, AR, HR + AR)
        dve(lambda e: e.tensor_tensor(out=T6, in0=T6, in1=LST, op=ALU.mult), reads=AR, writes=AR)
        dve(lambda e: e.tensor_tensor(out=T7, in0=T7, in1=LST, op=ALU.mult), reads=AR, writes=AR)
        dve(lambda e: e.tensor_scalar(T7, T7, -1.0, None, op0=ALU.add), reads=AR, writes=AR)
        h0, h1, h2, h3 = [hbf[:, i * 2048:(i + 1) * 2048] for i in range(4)]
        dve(lambda e: e.tensor_tensor(out=h0, in0=LRE, in1=LRE, op=ALU.mult), reads=AR, writes=HR)
        dve(lambda e: e.tensor_tensor(out=h1, in0=LIM, in1=LIM, op=ALU.mult), reads=AR, writes=HR)
        dve(lambda e: e.tensor_tensor(out=h0, in0=h0, in1=h1, op=ALU.add), reads=HR, writes=HR)
        dve(lambda e: e.reciprocal(h0, h0), reads=HR, writes=HR)
        dve(lambda e: e.tensor_tensor(out=h1, in0=T7, in1=LRE, op=ALU.mult), reads=AR, writes=HR)
        dve(lambda e: e.tensor_tensor(out=h3, in0=T6, in1=LIM, op=ALU.mult), reads=AR, writes=HR)
        dve(lambda e: e.tensor_tensor(out=h1, in0=h1, in1=h3, op=ALU.add), reads=HR, writes=HR)
        dve(lambda e: e.tensor_tensor(out=h1, in0=h1, in1=h0, op=ALU.mult), reads=HR, writes=HR)
        dve(lambda e: e.tensor_tensor(out=h2, in0=T6, in1=LRE, op=ALU.mult), reads=AR, writes=HR)
        dve(lambda e: e.tensor_tensor(out=h3, in0=T7, in1=LIM, op=ALU.mult), reads=AR, writes=HR)
        dve(lambda e: e.tensor_tensor(out=h2, in0=h2, in1=h3, op=ALU.subtract), reads=HR, writes=HR)
        dve(lambda e: e.tensor_tensor(out=h2, in0=h2, in1=h0, op=ALU.mult), reads=HR, writes=HR)
        dve(lambda e: e.tensor_tensor(out=T5, in0=h1, in1=BRE, op=ALU.mult), reads=AR + HR, writes=AR)
        dve(lambda e: e.tensor_tensor(out=h3, in0=h2, in1=BIM, op=ALU.mult), reads=AR + HR, writes=HR)
        dve(lambda e: e.tensor_tensor(out=Bexp.t[:, :, 0, :], in0=T5.rearrange("p (c n) -> p c n", c=16),
                                      in1=h3.rearrange("p (c n) -> p c n", c=16), op=ALU.subtract),
            reads=AR + HR, writes=[Bexp.all()])
        dve(lambda e: e.tensor_tensor(out=T5, in0=h1, in1=BIM, op=ALU.mult), reads=AR + HR, writes=AR)
        dve(lambda e: e.tensor_tensor(out=h3, in0=h2, in1=BRE, op=ALU.mult), reads=AR + HR, writes=HR)
        dve(lambda e: e.tensor_tensor(out=Bexp.t[:, :, 1, :], in0=T5.rearrange("p (c n) -> p c n", c=16),
                                      in1=h3.rearrange("p (c n) -> p c n", c=16), op=ALU.add),
            reads=AR + HR, writes=[Bexp.all()])

        lnr = B("lnr", [128, 64]); nphi = B("nphi", [128, 64]); phi511 = B("phi511", [128, 64])
        nlnr = B("nlnr", [128, 64]); lnr511 = B("lnr511", [128, 64]); Gre = B("Gre", [128, 64]); Gim = B("Gim", [128, 64])
        accS = B("accS", [128, 4, 64])
        PR = [rr.all(), cos1.all(), sin1.all(), phi.all(), ctmp.all(), lnr.all(), nphi.all(), phi511.all(), nlnr.all(), lnr511.all(), Gre.all(), Gim.all()]
        st_p = ctmp.t[:, 0, :]
        act(lambda e: e.activation(out=st_p, in_=sm["lstep_p"].t[:], func=AF.Exp), reads=[sm["lstep_p"].all()], writes=PR)
        dve(lambda e: e.tensor_tensor(out=lnr.t[:], in0=sm["lamre_p"].t[:], in1=st_p, op=ALU.mult), reads=PR + [sm["lamre_p"].all()], writes=PR)
        act(lambda e: e.activation(out=rr.t[:], in_=lnr.t[:], func=AF.Exp), reads=PR, writes=PR)
        dve(lambda e: e.tensor_tensor(out=phi.t[:], in0=sm["lamim_p"].t[:], in1=st_p, op=ALU.mult), reads=PR + [sm["lamim_p"].all()], writes=PR)
        sincos(phi.t[:], 64, ctmp.t[:, 1, :], ctmp.t[:, 2, :], sin1.t[:], cos1.t[:], PR, PR)

        dve(lambda e: e.tensor_scalar(nphi.t[:], phi.t[:], -1.0, None, op0=ALU.mult), reads=PR, writes=PR)
        dve(lambda e: e.tensor_scalar(phi511.t[:], phi.t[:], 511.0, None, op0=ALU.mult), reads=PR, writes=PR)
        dve(lambda e: e.tensor_scalar(nlnr.t[:], lnr.t[:], -1.0, None, op0=ALU.mult), reads=PR, writes=PR)
        dve(lambda e: e.tensor_scalar(lnr511.t[:], lnr.t[:], 511.0, None, op0=ALU.mult), reads=PR, writes=PR)
        dve(lambda e: e.tensor_scalar(ctmp.t[:, 0, :], phi.t[:], 512.0, None, op0=ALU.mult), reads=PR, writes=PR)
        sincos(ctmp.t[:, 0, :], 64, ctmp.t[:, 1, :], ctmp.t[:, 2, :], Gim.t[:], Gre.t[:], PR, PR)
        act(lambda e: e.activation(out=ctmp.t[:, 3, :], in_=lnr.t[:], func=AF.Exp, scale=512.0), reads=PR, writes=PR)
        dve(lambda e: e.tensor_tensor(out=Gre.t[:], in0=Gre.t[:], in1=ctmp.t[:, 3, :], op=ALU.mult), reads=PR, writes=PR)
        dve(lambda e: e.tensor_tensor(out=Gim.t[:], in0=Gim.t[:], in1=ctmp.t[:, 3, :], op=ALU.mult), reads=PR, writes=PR)

        c511 = B("c511", [128, 64]); s511 = B("s511", [128, 64]); wlast = B("wlast", [128, 2, 64])
        PR = PR + [c511.all(), s511.all()]
        dve(lambda e: e.tensor_scalar(ctmp.t[:, 0, :], phi.t[:], 511.0, None, op0=ALU.mult), reads=PR, writes=PR)
        sincos(ctmp.t[:, 0, :], 64, ctmp.t[:, 1, :], ctmp.t[:, 2, :], s511.t[:], c511.t[:], PR, PR)
        iotv = uf(X0 + 12288, 512); iot_r = ur(X0 + 12288, 2048)
        revv = uf(X0 + 14336, 512); rev_r = ur(X0 + 14336, 2048)
        S.dma("sp", iotv, dr["iota"], writes=[iot_r], key="ld_iota")
        dve(lambda e: e.tensor_scalar(revv, iotv, -1.0, 511.0, op0=ALU.mult, op1=ALU.add), reads=[iot_r], writes=[rev_r])
        NBT = 4
        tbb = [av[:, i * 2048:(i + 1) * 2048] for i in range(6)]
        v3 = lambda ap: ap.rearrange("p (q t) -> p q t", q=NBT)
        bc_t = lambda ap: ap.unsqueeze(1).to_broadcast([128, NBT, 512])
        hbs = [hbf[:, i * 2048:(i + 1) * 2048] for i in range(4)]

        def gen_E_batch(bt):
            p0 = bt * NBT
            bc_p = lambda buf: buf.t[:, p0:p0 + NBT].unsqueeze(2).to_broadcast([128, NBT, 512])
            dve(lambda e, a=bc_t(iotv), b=bc_p(nphi): e.tensor_tensor(out=v3(hbs[0]), in0=a, in1=b, op=ALU.mult), reads=[iot_r] + PR, writes=HR)
            yield
            yield from sincos_g(hbs[0], hbs[1], hbs[2], hbs[3], hbs[2], HR, HR)
            S.dma("sp", tabs[p0:p0 + NBT, 1].rearrange("q p t -> p q t"), v3(hbs[3]), reads=HR, writes=[("tabs", p0, p0 + NBT)], key="tabw")
            S.dma("sp", tabs[p0:p0 + NBT, 0].rearrange("q p t -> p q t"), v3(hbs[2]), reads=HR, writes=[("tabs", p0, p0 + NBT)], key="tabw")
            yield

        def gen_W_batch(bt):
            p0 = bt * NBT
            bc_p = lambda buf: buf.t[:, p0:p0 + NBT].unsqueeze(2).to_broadcast([128, NBT, 512])
            dve(lambda e, a=bc_t(revv), b=bc_p(phi): e.tensor_tensor(out=v3(tbb[0]), in0=a, in1=b, op=ALU.mult), reads=[rev_r] + PR, writes=AR)
            yield
            dve(lambda e, a=bc_t(revv), b=bc_p(lnr): e.tensor_tensor(out=v3(tbb[3]), in0=a, in1=b, op=ALU.mult), reads=[rev_r] + PR, writes=AR)
            yield
            act(lambda e: e.activation(out=tbb[3], in_=tbb[3], func=AF.Exp), reads=AR, writes=AR)
            yield
            yield from sincos_g(tbb[0], tbb[1], tbb[2], tbb[4], tbb[5], AR, AR)
            dve(lambda e: e.tensor_tensor(out=tbb[1], in0=tbb[5], in1=tbb[3], op=ALU.mult), reads=AR, writes=AR)
            yield
            dve(lambda e: e.tensor_tensor(out=tbb[2], in0=tbb[4], in1=tbb[3], op=ALU.mult), reads=AR, writes=AR)
            yield
            S.dma("sp", wtabs[p0:p0 + NBT, 0].rearrange("q p t -> p q t"), v3(tbb[1]), reads=AR, writes=[("wtabs", p0, p0 + NBT)], key="wtabw")
            S.dma("sp", wtabs[p0:p0 + NBT, 1].rearrange("q p t -> p q t"), v3(tbb[2]), reads=AR, writes=[("wtabs", p0, p0 + NBT)], key="wtabw")
            yield

        def interleave(*gens):
            gens = list(gens)
            while gens:
                for g_ in list(gens):
                    try:
                        next(g_)
                    except StopIteration:
                        gens.remove(g_)

        for i in range(16):
            for _ in ada_all(range(i, i + 1), [0, 1, 2, 3]):
                pass
            interleave(gen_W_batch(i), gen_E_batch(i))
        pctr[0] = 4
        dve(lambda e: e.scalar_tensor_tensor(out=gs1.t[:], in0=sc1, scalar=1.0, in1=sm["n1g"].t[:], op0=ALU.add, op1=ALU.mult),
            reads=[mods.all(), sm["n1g"].all()], writes=[gs1.all()])

        sqv = [uf(X0 + i * 2048, 512) for i in range(2)]
        sqr = [ur(X0 + i * 2048, 2048) for i in range(2)]
        rstd = uf(X0 + 4096, 512); rstd_r = ur(X0 + 4096, 2048)
        ntv = [uf(X0 + 6144 + i * 2048, 512) for i in range(2)]
        ntr = [ur(X0 + 6144 + i * 2048, 2048) for i in range(2)]
        s2v = uf(X0 + 10240, 512); s2_r = ur(X0 + 10240, 2048)
        sq4 = [uf(X0 + 12288 + i * 2048, 512) for i in range(4)]; sq4_r = [ur(X0 + 12288 + i * 2048, 2048) for i in range(4)]
        sqc = [0]

        class Stats:
            def __init__(self):
                self.i = 0

            def add(self, ap, reg):
                q = sqc[0] % 4
                sqc[0] += 1
                if self.i == 0:
                    act(lambda e: e.activation(out=s2v, in_=ap, func=AF.Square), reads=[reg], writes=[s2_r])
                else:
                    act(lambda e: e.activation(out=sq4[q], in_=ap, func=AF.Square), reads=[reg], writes=[sq4_r[q]])
                    dve(lambda e: e.tensor_tensor(out=s2v, in0=s2v, in1=sq4[q], op=ALU.add), reads=[s2_r, sq4_r[q]], writes=[s2_r])
                self.i += 1

            def finish(self, nfeat):
                pe(lambda e: e.matmul(ps[SSB].t[:], ones.t[:], s2v, start=True, stop=True), reads=[ones.all(), s2_r], writes=[ps[SSB].all()])
                finish_stats(nfeat)

        SSB = 7

        def accr(c):
            return acc.r(c * T, (c + 1) * T)

        def hbr(c):
            return hb.r(c * T, (c + 1) * T)

        def rms_stats(chunks, nfeat):
            n = len(chunks)
            for i, (ap, reg) in enumerate(chunks):
                q = i % 2
                act(lambda e, ap=ap, q=q: e.activation(out=sqv[q], in_=ap, func=AF.Square), reads=[reg], writes=[sqr[q]])
                pe(lambda e, q=q, i=i: e.matmul(ps[SSB].t[:], ones.t[:], sqv[q], start=(i == 0), stop=(i == n - 1)),
                   reads=[ones.all(), sqr[q]], writes=[ps[SSB].all()])
            finish_stats(nfeat)

        def finish_stats(nfeat):
            dve(lambda e: e.tensor_scalar(rstd, ps[SSB].t[:], 1.0 / nfeat, EPS, op0=ALU.mult, op1=ALU.add),
                reads=[ps[SSB].all()], writes=[rstd_r])
            act(lambda e: e.activation(out=rstd, in_=rstd, func=AF.Sqrt), reads=[rstd_r], writes=[rstd_r])
            dve(lambda e: e.reciprocal(rstd, rstd), reads=[rstd_r], writes=[rstd_r])

        def norm_mod(gs, sh, st=None):
            if st is None:
                rms_stats([(acc.t[:, c, :], accr(c)) for c in range(KC)], D)
            else:
                st.finish(D)
            for c in range(KC):
                q = c % 2
                dve(lambda e, c=c, q=q: e.tensor_tensor(out=ntv[q], in0=acc.t[:, c, :], in1=rstd, op=ALU.mult),
                    reads=[accr(c), rstd_r], writes=[ntr[q]])
                act(lambda e, c=c, q=q: e.activation(out=hb.t[:, c, :], in_=ntv[q], func=AF.Identity,
                                                    bias=sh[:, c:c + 1], scale=gs[:, c:c + 1]),
                    reads=[ntr[q], mods.all(), gs1.all(), gs2.all()], writes=[hbr(c)])

        xs = [uf(i * 8192, 2048) for i in range(2)]
        xsr = [ur(i * 8192, 8192) for i in range(2)]

        def load_x(row0):
            cnt = 0
            for blk in range(4):
                for half in range(2):
                    q = cnt % 2
                    cnt += 1
                    S.dma("sp", xs[q], dr["xcat"][row0 + blk * 128: row0 + (blk + 1) * 128, half * 2048:(half + 1) * 2048],
                          writes=[xsr[q]], key="xs%d" % q)
                    for c4 in range(4):
                        b = pnext()
                        for cc in range(4):
                            pe(lambda e, b=b, q=q, c4=c4, cc=cc:
                               e.transpose(ps[b].t[:, cc * 128:(cc + 1) * 128], xs[q][:, (c4 * 4 + cc) * 128:(c4 * 4 + cc + 1) * 128], sm["identf"].t[:]),
                               reads=[xsr[q], sm["identf"].all()], writes=[ps[b].all()])
                        c0 = half * 16 + c4 * 4
                        fn = lambda e, b=b, c0=c0, blk=blk: e.tensor_copy(
                            acc.t[:, c0:c0 + 4, blk * 128:(blk + 1) * 128], ps[b].t[:].rearrange("p (c t) -> p c t", c=4))
                        fn2 = lambda e, b=b, c0=c0, blk=blk: e.activation(
                            out=acc.t[:, c0:c0 + 4, blk * 128:(blk + 1) * 128], in_=ps[b].t[:].rearrange("p (c t) -> p c t", c=4), func=AF.Identity)
                        S.op("dve" if c4 % 2 == 0 else "act", fn if c4 % 2 == 0 else fn2,
                             reads=[ps[b].all()], writes=[acc.r(c0 * T, (c0 + 4) * T)])

        rcv = uf(X0, 512); rsv = uf(X0 + 2048, 512)
        rc_r = ur(X0, 4096)
        t4v = [uf(X0 + 4096 + i * 2048, 512) for i in range(4)]
        t4r = ur(X0 + 4096, 8192)

        def rope_epi(ba, bb, outA, outB, regA, regB):
            dve(lambda e: e.tensor_tensor(out=t4v[0], in0=ps[ba].t[:], in1=rcv, op=ALU.mult), reads=[ps[ba].all(), rc_r], writes=[t4r])
            dve(lambda e: e.tensor_tensor(out=t4v[1], in0=ps[bb].t[:], in1=rsv, op=ALU.mult), reads=[ps[bb].all(), rc_r], writes=[t4r])
            dve(lambda e: e.tensor_tensor(out=t4v[2], in0=ps[bb].t[:], in1=rcv, op=ALU.mult), reads=[ps[bb].all(), rc_r], writes=[t4r])
            dve(lambda e: e.tensor_tensor(out=t4v[3], in0=ps[ba].t[:], in1=rsv, op=ALU.mult), reads=[ps[ba].all(), rc_r], writes=[t4r])
            dve(lambda e: e.tensor_tensor(out=outA, in0=t4v[0], in1=t4v[1], op=ALU.subtract), reads=[t4r], writes=[regA])
            dve(lambda e: e.tensor_tensor(out=outB, in0=t4v[2], in1=t4v[3], op=ALU.add), reads=[t4r], writes=[regB])

        def in_proj(ti, pre, want_kv):
            col0 = ti * T
            S.dma("sp", rcv, dr["ropec"][:, col0:col0 + T], writes=[rc_r], key="rope")
            S.dma("sp", rsv, dr["ropes"][:, col0:col0 + T], writes=[rc_r], key="rope")
            rhs = lambda kc: (hb.t[:, kc, :], hbr(kc))
            if not pre:
                for slab in range(4):
                    banks = []
                    group_mm(dr["w_in"], 8, slab * 512, rhs, lambda j, b: banks.append(b))
                    for pr in range(2):
                        g = slab * 2 + pr
                        rope_epi(banks[2 * pr], banks[2 * pr + 1], Qv[:, 2 * g, :], Qv[:, 2 * g + 1, :], Qr(2 * g), Qr(2 * g + 1))
            if want_kv:
                for slab in range(2):
                    banks = []
                    group_mm(dr["w_in"], 8, 2048 + slab * 512, rhs, lambda j, b: banks.append(b))
                    for pr in range(2):
                        a = slab * 2 + pr
                        rope_epi(banks[2 * pr], banks[2 * pr + 1], Kb.t[:, 2 * a, 128:640], Kb.t[:, 2 * a + 1, 128:640],
                                 Kb.r((2 * a) * 640 + 128, (2 * a + 1) * 640), Kb.r((2 * a + 1) * 640 + 128, (2 * a + 2) * 640))
            for slab in range(4):
                def epi(j, b, slab=slab):
                    c = slab * 4 + j
                    act(lambda e: e.activation(out=ubv[:, c, :], in_=ps[b].t[:], func=AF.Identity), reads=[ps[b].all()], writes=[ubr(c)])
                group_mm(dr["w_in"], 8, 3072 + slab * 512, rhs, epi)
                if pre:
                    ada_full_slab()
            if want_kv:
                banks = [pnext() for _ in range(4)]
                for k4 in range(8):
                    s = wtile(dr["w_in"], k4 * 512, 5120, 256)
                    for a in range(4):
                        kc = k4 * 4 + a
                        for blk in range(4):
                            pe(lambda e, b=banks[blk], s=s, a=a, kc=kc, blk=blk:
                               e.matmul(ps[b].t[:, 0:256], hb.t[:, kc, blk * 128:(blk + 1) * 128], wr.t[:, s, a, 0:256],
                                        start=(kc == 0), stop=(kc == 31)),
                               reads=[wreg(s), hbr(kc)], writes=[ps[banks[blk]].all()])
                for blk in range(4):
                    b = banks[blk]
                    act(lambda e, b=b, blk=blk: e.activation(out=Va.t[:, 1 + blk, :, 0:64], in_=ps[b].t[:, 0:256].rearrange("p (h d) -> p h d", h=4), func=AF.Identity),
                        reads=[ps[b].all()], writes=[Va.r((1 + blk) * 260, (2 + blk) * 260)])

        def shift_halo():
            dve(lambda e: e.tensor_copy(Kb.t[:, :, 0:128], Kb.t[:, :, 512:640]), reads=[Kb.all()], writes=[Kb.all()])
            dve(lambda e: e.tensor_copy(Va.t[:, 0, :, :], Va.t[:, 4, :, :]), reads=[Va.all()], writes=[Va.all()])

        A0 = X0
        attok = uf(A0, 2048); attok_r = ur(A0, 8192)
        atn = ubf(A0 + 8192, 2048); atn_r = ur(A0 + 8192, 4096)
        NEB = 4
        ebv = [ubf(A0 + 12288 + i * 512, 256) for i in range(NEB)]
        ebr = [ur(A0 + 12288 + i * 512, 512) for i in range(NEB)]
        emv = [ubf(A0 + 14336 + i * 512, 256) for i in range(NEB)]
        emr = [ur(A0 + 14336 + i * 512, 512) for i in range(NEB)]
        atj = atn; atj_r = atn_r

        def attention(first_tile):
            for qb in range(4):
                msk = cb["mask_f"] if (first_tile and qb == 0) else cb["mask_n"]
                heads = [(g, j) for g in range(8) for j in range(4)]
                st = {}
                pob = {}

                def stageA(n, qb=qb):
                    g, j = heads[n]
                    a = g // 2
                    q = n % NEB
                    sb = pnext()
                    st[n] = (sb, q)
                    for kb in range(2):
                        kcol = (qb + kb) * 128
                        for ab in range(2):
                            pe(lambda e, sb=sb, kb=kb, ab=ab, kcol=kcol, g=g, j=j, a=a:
                               e.matmul(ps[sb].t[:, kb * 128:(kb + 1) * 128],
                                        Kb.t[32 * j:32 * j + 32, 2 * a + ab, kcol:kcol + 128],
                                        Qv[32 * j:32 * j + 32, 2 * g + ab, qb * 128:(qb + 1) * 128],
                                        start=(ab == 0), stop=(ab == 1), tile_position=(32 * j, 0)),
                               reads=[Kb.all(), Qr(2 * g + ab)], writes=[ps[sb].all()])
                    act(lambda e, sb=sb, q=q: e.activation(out=ebv[q], in_=ps[sb].t[:, 0:256], func=AF.Exp, scale=0.125),
                        reads=[ps[sb].all()], writes=[ebr[q]])

                def stageM(n, msk=msk):
                    q = st[n][1]
                    dve(lambda e, q=q: e.tensor_tensor(out=emv[q], in0=ebv[q], in1=msk.t[:], op=ALU.mult),
                        reads=[ebr[q], msk.all()], writes=[emr[q]])

                def stageP(n, qb=qb):
                    g, j = heads[n]
                    a = g // 2
                    q = st[n][1]
                    if j == 0:
                        pob[g] = pnext()
                    po = pob[g]
                    for kb in range(2):
                        pe(lambda e, po=po, q=q, kb=kb, j=j, a=a:
                           e.matmul(ps[po].t[:, j * 65:(j + 1) * 65], emv[q][:, kb * 128:(kb + 1) * 128],
                                    Va.t[:, qb + kb, a, :], start=(kb == 0), stop=(kb == 1)),
                           reads=[emr[q], Va.all()], writes=[ps[po].all()])
                    if j == 3:
                        pov = ps[po].t[:, 0:260].rearrange("p (h d) -> p h d", d=65)
                        SR = [sml.all()]
                        dve(lambda e, pov=pov, g=g: e.tensor_tensor(out=sml.t[:, 0:4], in0=pov[:, :, 64], in1=esink.t[:, 4 * g:4 * g + 4], op=ALU.add),
                            reads=[ps[po].all(), esink.all()], writes=SR)
                        dve(lambda e: e.reciprocal(sml.t[:, 4:8], sml.t[:, 0:4]), reads=SR, writes=SR)
                        dve(lambda e, pov=pov, g=g: e.tensor_tensor(
                            out=attok[:, g * 256:(g + 1) * 256].rearrange("p (h d) -> p h d", d=64), in0=pov[:, :, 0:64],
                            in1=sml.t[:, 4:8].unsqueeze(2).to_broadcast([128, 4, 64]), op=ALU.mult),
                            reads=[ps[po].all()] + SR, writes=[attok_r])

                for n in range(32 + 3):
                    if n < 32:
                        stageA(n)
                    if 0 <= n - 2 < 32:
                        stageM(n - 2)
                    if 0 <= n - 3 < 32:
                        stageP(n - 3)
                SR = [sml.all()]
                act(lambda e: e.activation(out=atj, in_=attok, func=AF.Square, accum_out=sml.t[:, 8:9]), reads=[attok_r], writes=[atj_r] + SR)
                dve(lambda e: e.tensor_scalar(sml.t[:, 9:10], sml.t[:, 8:9], 1.0 / 2048, EPS, op0=ALU.mult, op1=ALU.add), reads=SR, writes=SR)
                act(lambda e: e.activation(out=sml.t[:, 9:10], in_=sml.t[:, 9:10], func=AF.Sqrt), reads=SR, writes=SR)
                dve(lambda e: e.reciprocal(sml.t[:, 10:11], sml.t[:, 9:10]), reads=SR, writes=SR)
                dve(lambda e: e.tensor_scalar(atn, attok, sml.t[:, 10:11], None, op0=ALU.mult), reads=[attok_r] + SR, writes=[atn_r])
                for c4 in range(4):
                    b = pnext()
                    pb = ps[b].t[:].bitcast(BF16)
                    for cc in range(4):
                        c = c4 * 4 + cc
                        pe(lambda e, pb=pb, cc=cc, c=c: e.transpose(pb[:, cc * 128:(cc + 1) * 128], atn[:, c * 128:(c + 1) * 128], cb["identb"].t[:]),
                           reads=[atn_r, cb["identb"].all()], writes=[ps[b].all()])
                    for cc in range(4):
                        c = c4 * 4 + cc
                        act(lambda e, pb=pb, cc=cc, c=c, qb=qb: e.activation(out=hb.t[:, c, qb * 128:(qb + 1) * 128], in_=pb[:, cc * 128:(cc + 1) * 128],
                                                                   func=AF.Identity, scale=sm["attn_gT"].t[:, c:c + 1]),
                            reads=[ps[b].all(), sm["attn_gT"].all()], writes=[hb.r(c * T + qb * 128, c * T + (qb + 1) * 128)])

        S0 = X0
        tEs = [uf(S0 + i * 4096, 1024).rearrange("p (c t) -> p c t", c=2) for i in range(2)]
        tErs = [ur(S0 + i * 4096, 4096) for i in range(2)]
        stmp = [uf(S0 + 8192 + i * 2048, 512) for i in range(2)]; stmp_r = [ur(S0 + 8192 + i * 2048, 2048) for i in range(2)]
        mmv = [uf(S0 + 12288 + i * 2048, 512) for i in range(2)]; mm_r = [ur(S0 + 12288 + i * 2048, 2048) for i in range(2)]
        wwv = [uf(S0 + 16384 + i * 2048, 512) for i in range(2)]; ww_r = [ur(S0 + 16384 + i * 2048, 2048) for i in range(2)]
        zzv = [Kb.t[:, a_, 128:640] for a_ in range(4)]; zz_r = [Kb.r(a_ * 640 + 128, (a_ + 1) * 640) for a_ in range(4)]
        nwv = Kb.t[:, 4, 128:640]; nw_r = Kb.r(4 * 640 + 128, 5 * 640)

        def ssm():
            CR = [car.all(), wini.all(), ctmp.all(), cos1.all(), sin1.all()]
            cre, cim = car.t[:, 0, :], car.t[:, 1, :]
            dve(lambda e: e.tensor_tensor(out=ctmp.t[:, 0, :], in0=cos1.t[:], in1=cre, op=ALU.mult), reads=CR, writes=CR)
            dve(lambda e: e.tensor_tensor(out=ctmp.t[:, 1, :], in0=sin1.t[:], in1=cim, op=ALU.mult), reads=CR, writes=CR)
            dve(lambda e: e.tensor_tensor(out=wini.t[:, 0, :], in0=ctmp.t[:, 0, :], in1=ctmp.t[:, 1, :], op=ALU.subtract), reads=CR, writes=CR)
            dve(lambda e: e.tensor_tensor(out=ctmp.t[:, 2, :], in0=sin1.t[:], in1=cre, op=ALU.mult), reads=CR, writes=CR)
            dve(lambda e: e.tensor_tensor(out=ctmp.t[:, 3, :], in0=cos1.t[:], in1=cim, op=ALU.mult), reads=CR, writes=CR)
            dve(lambda e: e.tensor_tensor(out=wini.t[:, 1, :], in0=ctmp.t[:, 2, :], in1=ctmp.t[:, 3, :], op=ALU.add), reads=CR, writes=CR)
            bbank = [0, 1, 2, 3]
            ybank = [4, 5]

            def bk(n):
                return bbank[(2 * n) % 4], bbank[(2 * n + 1) % 4]

            def stL_dma(n):
                S.dma("sp", tEs[n % 2], tabs[n].rearrange("c p t -> p c t"), reads=[("tabs", n, n + 1)], writes=[tErs[n % 2]], key="tabr%d" % (n % 2))

            def stL_pe(n):
                c_, j_ = n // 4, n % 4
                for ri, b in zip((0, 1), bk(n)):
                    pe(lambda e, ri=ri, b=b, c_=c_, j_=j_: e.matmul(ps[b].t[:], Bexp.t[32 * j_:32 * j_ + 32, c_, ri, :],
                                                                  ubv[32 * j_:32 * j_ + 32, c_, :], start=True, stop=True,
                                                                  tile_position=(32 * j_, 0)),
                       reads=[Bexp.all(), ubr(c_)], writes=[ps[b].all()])

            def stM(n):
                Ec, Es = tEs[n % 2][:, 0, :], tEs[n % 2][:, 1, :]
                tr = tErs[n % 2]
                bre, bim = bk(n)
                dve(lambda e: e.tensor_tensor(out=mmv[0], in0=ps[bre].t[:], in1=Ec, op=ALU.mult), reads=[ps[bre].all(), tr], writes=[mm_r[0]])
                dve(lambda e: e.tensor_tensor(out=stmp[0], in0=ps[bim].t[:], in1=Es, op=ALU.mult), reads=[ps[bim].all(), tr], writes=[stmp_r[0]])
                dve(lambda e: e.tensor_tensor(out=mmv[1], in0=ps[bim].t[:], in1=Ec, op=ALU.mult), reads=[ps[bim].all(), tr], writes=[mm_r[1]])
                dve(lambda e: e.tensor_tensor(out=stmp[1], in0=ps[bre].t[:], in1=Es, op=ALU.mult), reads=[ps[bre].all(), tr], writes=[stmp_r[1]])

            def stA(n):
                S.op("pool", lambda e: e.tensor_tensor(out=mmv[0], in0=mmv[0], in1=stmp[0], op=ALU.subtract), [mm_r[0], stmp_r[0]], [mm_r[0]])
                S.op("pool", lambda e: e.tensor_tensor(out=mmv[1], in0=mmv[1], in1=stmp[1], op=ALU.add), [mm_r[1], stmp_r[1]], [mm_r[1]])

            def stS(n):
                for ri in range(2):
                    dve(lambda e, ri=ri, p=n: e.tensor_tensor_scan(out=wwv[ri], data0=rr.t[:, p:p + 1].to_broadcast([128, 512]), data1=mmv[ri],
                                                                  initial=wini.t[:, ri, p:p + 1], op0=ALU.mult, op1=ALU.add),
                        reads=[mm_r[ri], rr.all(), wini.all()], writes=[ww_r[ri]])
                for ri in range(2):
                    act(lambda e, ri=ri, p=n: e.activation(out=wlast.t[:, ri, p:p + 1], in_=wwv[ri][:, 511:512], func=AF.Identity),
                        reads=[ww_r[ri]], writes=[wlast.all()])
                act(lambda e: e.activation(out=nwv, in_=wwv[1], func=AF.Identity, scale=-1.0), reads=[ww_r[1]], writes=[nw_r])

            def stZ(n):
                c_, j_ = n // 4, n % 4
                Ec, Es = tEs[n % 2][:, 0, :], tEs[n % 2][:, 1, :]
                tr = tErs[n % 2]
                dve(lambda e: e.tensor_tensor(out=zzv[0], in0=wwv[0], in1=Ec, op=ALU.mult), reads=[ww_r[0], tr], writes=[zz_r[0]])
                dve(lambda e: e.tensor_tensor(out=zzv[1], in0=wwv[1], in1=Es, op=ALU.mult), reads=[ww_r[1], tr], writes=[zz_r[1]])
                dve(lambda e: e.tensor_tensor(out=zzv[2], in0=wwv[0], in1=Es, op=ALU.mult), reads=[ww_r[0], tr], writes=[zz_r[2]])
                dve(lambda e: e.tensor_tensor(out=zzv[3], in0=nwv, in1=Ec, op=ALU.mult), reads=[nw_r, tr], writes=[zz_r[3]])
                yb = ybank[n % 2]
                for zi, cm in ((0, "cre_e"), (1, "cre_e"), (2, "cim_e"), (3, "cim_e")):
                    pe(lambda e, zi=zi, cm=cm, yb=yb, p=n: e.matmul(ps[yb].t[0:32, :], cb[cm].t[:, p, :], zzv[zi], start=(zi == 0), stop=False),
                       reads=[cb[cm].all(), zz_r[zi]], writes=[ps[yb].all()])
                pe(lambda e, yb=yb, c_=c_, j_=j_: e.matmul(ps[yb].t[0:32, :], cb["d_e"].t[32 * j_:32 * j_ + 32, c_, :], ubv[32 * j_:32 * j_ + 32, c_, :],
                                                           start=False, stop=True, tile_position=(32 * j_, 0)),
                   reads=[cb["d_e"].all(), ubr(c_)], writes=[ps[yb].all()])
                act(lambda e, yb=yb, c_=c_, j_=j_: e.activation(out=gbv[32 * j_:32 * j_ + 32, c_, :], in_=ps[yb].t[0:32, :], func=AF.Gelu),
                    reads=[ps[yb].all()], writes=[Qr(c_)])

            stL_dma(0)
            stL_pe(0)
            for n in range(64):
                if n + 1 < 64:
                    stL_pe(n + 1)
                stM(n)
                stA(n)
                if n >= 1:
                    stZ(n - 1)
                if n + 1 < 64:
                    stL_dma(n + 1)
                stS(n)
            stZ(63)
            FR = [car.all(), ctmp.all(), wlast.all(), c511.all(), s511.all()]
            wl_re, wl_im = wlast.t[:, 0, :], wlast.t[:, 1, :]
            TT = lambda o, a, b, op: dve(lambda e: e.tensor_tensor(out=o, in0=a, in1=b, op=op), reads=FR, writes=FR)
            TT(ctmp.t[:, 0, :], c511.t[:], wl_re, ALU.mult)
            TT(ctmp.t[:, 1, :], s511.t[:], wl_im, ALU.mult)
            TT(cre, ctmp.t[:, 0, :], ctmp.t[:, 1, :], ALU.subtract)
            TT(ctmp.t[:, 2, :], s511.t[:], wl_re, ALU.mult)
            TT(ctmp.t[:, 3, :], c511.t[:], wl_im, ALU.mult)
            TT(cim, ctmp.t[:, 2, :], ctmp.t[:, 3, :], ALU.add)

        tWs = [uf(X0 + i * 4096, 1024).rearrange("p (c t) -> p c t", c=2) for i in range(2)]
        tWrs = [ur(X0 + i * 4096, 4096) for i in range(2)]
        jkv = [uf(i * 2048, 512) for i in range(4)]; jk_r = [ur(i * 2048, 2048) for i in range(4)]

        ada_slab = [16]

        def ada_full_slab():
            if ada_slab[0] < 48:
                sl = ada_slab[0]
                ada_slab[0] += 1
                for _ in ada_all([sl], [pnext() for _ in range(4)]):
                    pass

        def ssm_pre():
            sl0 = ada_slab[0]
            ada_slab[0] = min(48, sl0 + 4)
            ada_it = ada_all(range(sl0, ada_slab[0]), [3, 4, 5, 6])
            bl = [0, 1, 2, 7]
            for pi_ in range(64):
                c_, j_ = pi_ // 4, pi_ % 4
                tWv, tW_r = tWs[pi_ % 2], tWrs[pi_ % 2]
                S.dma("sp", tWv, wtabs[pi_].rearrange("c p t -> p c t"), reads=[("wtabs", pi_, pi_ + 1)], writes=[tW_r], key="tabr%d" % (pi_ % 2))
                bre, bim = bl[(2 * pi_) % 4], bl[(2 * pi_ + 1) % 4]
                for ri, b in ((0, bre), (1, bim)):
                    pe(lambda e, ri=ri, b=b, c_=c_, j_=j_: e.matmul(ps[b].t[:], Bexp.t[32 * j_:32 * j_ + 32, c_, ri, :],
                                                                  ubv[32 * j_:32 * j_ + 32, c_, :], start=True, stop=True,
                                                                  tile_position=(32 * j_, 0)),
                       reads=[Bexp.all(), ubr(c_)], writes=[ps[b].all()])
                for k, (bank, wi) in enumerate(((bre, 0), (bim, 1), (bim, 0), (bre, 1))):
                    dve(lambda e, k=k, bank=bank, wi=wi, tWv=tWv: e.tensor_tensor(out=jkv[k], in0=ps[bank].t[:], in1=tWv[:, wi, :], op=ALU.mult),
                        reads=[ps[bank].all(), tW_r], writes=[jk_r[k]])
                    act(lambda e, k=k, p=pi_: e.activation(out=jkv[k], in_=jkv[k], func=AF.Identity, accum_out=accS.t[:, k, p:p + 1]),
                        reads=[jk_r[k]], writes=[jk_r[k], accS.all()])
                if pi_ % 2 == 1:
                    next(ada_it, None)
            for _ in ada_it:
                pass
            CR = [car.all(), ctmp.all(), accS.all(), Gre.all(), Gim.all()]
            cre, cim = car.t[:, 0, :], car.t[:, 1, :]
            c0, c1, c2, c3 = [ctmp.t[:, i, :] for i in range(4)]
            TT = lambda o, a, b, op: dve(lambda e: e.tensor_tensor(out=o, in0=a, in1=b, op=op), reads=CR, writes=CR)
            TT(c0, accS.t[:, 0, :], accS.t[:, 1, :], ALU.subtract)
            TT(c1, accS.t[:, 2, :], accS.t[:, 3, :], ALU.add)
            TT(c2, Gre.t[:], cre, ALU.mult)
            TT(c0, c0, c2, ALU.add)
            TT(c2, Gim.t[:], cim, ALU.mult)
            TT(c0, c0, c2, ALU.subtract)
            TT(c3, Gre.t[:], cim, ALU.mult)
            TT(c1, c1, c3, ALU.add)
            TT(c3, Gim.t[:], cre, ALU.mult)
            TT(c1, c1, c3, ALU.add)
            dve(lambda e: e.tensor_copy(cre, c0), reads=CR, writes=CR)
            dve(lambda e: e.tensor_copy(cim, c1), reads=CR, writes=CR)

        G0 = X0
        gatev = [uf(G0 + 12288 + i * 2048, 512) for i in range(2)]; gate_r = [ur(G0 + 12288 + i * 2048, 2048) for i in range(2)]
        prodv = [uf(G0 + 16384 + i * 2048, 512) for i in range(2)]; prod_r = [ur(G0 + 16384 + i * 2048, 2048) for i in range(2)]

        def glu_and_norm():
            rhs = lambda kc: (gbv[:, kc, :], Qr(kc))
            cnt = [0]
            for slab in range(4):
                def epi(j, b, slab=slab):
                    ob = slab * 4 + j
                    q = cnt[0] % 2
                    i = cnt[0]
                    cnt[0] += 1
                    act(lambda e: e.activation(out=gatev[q], in_=ps[b].t[:], func=AF.Sigmoid, bias=sm["b_gluT"].t[:, ob:ob + 1]),
                        reads=[ps[b].all(), sm["b_gluT"].all()], writes=[gate_r[q]])
                    dve(lambda e: e.tensor_tensor(out=prodv[q], in0=gbv[:, ob, :], in1=gatev[q], op=ALU.mult),
                        reads=[Qr(ob), gate_r[q]], writes=[prod_r[q]])
                    act(lambda e: e.activation(out=sqv[q], in_=prodv[q], func=AF.Square), reads=[prod_r[q]], writes=[sqr[q]])
                    pe(lambda e: e.matmul(ps[SSB].t[:], ones.t[:], sqv[q], start=(i == 0), stop=(i == 15)),
                       reads=[ones.all(), sqr[q]], writes=[ps[SSB].all()])
                    dve(lambda e: e.tensor_copy(hb.t[:, 16 + ob, :], prodv[q]), reads=[prod_r[q]], writes=[hbr(16 + ob)])
                group_mm(dr["w_glu"], 4, slab * 512, rhs, epi)
            finish_stats(2048)
            for ob in range(16):
                dve(lambda e, ob=ob: e.scalar_tensor_tensor(out=hb.t[:, 16 + ob, :], in0=hb.t[:, 16 + ob, :], scalar=sm["ssm_gT"].t[:, ob:ob + 1],
                                                         in1=rstd, op0=ALU.mult, op1=ALU.mult),
                    reads=[hbr(16 + ob), rstd_r, sm["ssm_gT"].all()], writes=[hbr(16 + ob)])

        def resid_epi(gate, st):
            def epi_factory(slab):
                def epi(j, b):
                    ob = slab * 4 + j
                    dve(lambda e: e.scalar_tensor_tensor(out=acc.t[:, ob, :], in0=ps[b].t[:], scalar=gate[:, ob:ob + 1], in1=acc.t[:, ob, :],
                                                         op0=ALU.mult, op1=ALU.add),
                        reads=[ps[b].all(), mods.all(), accr(ob)], writes=[accr(ob)])
                    st.add(acc.t[:, ob, :], accr(ob))
                return epi
            return epi_factory

        def out_proj():
            rhs = lambda kc: (hb.t[:, kc, :], hbr(kc))
            st = Stats()
            ef = resid_epi(gt1, st)
            for slab in range(8):
                group_mm(dr["w_out"], 8, slab * 512, rhs, ef(slab))
            return st

        hidv = [ubf(i * 4096, 2048).rearrange("p (a t) -> p a t", a=4) for i in range(2)]
        hid_r = [ur(i * 4096, 4096) for i in range(2)]
        rlv = [uf(8192 + i * 2048, 512) for i in range(2)]; rl_r = [ur(8192 + i * 2048, 2048) for i in range(2)]

        def ffn():
            rhs = lambda kc: (hb.t[:, kc, :], hbr(kc))
            cnt = [0]
            st = Stats()
            for f in range(32):
                hq = f % 2

                def epi(j, b, hq=hq):
                    q = cnt[0] % 2
                    cnt[0] += 1
                    act(lambda e: e.activation(out=rlv[q], in_=ps[b].t[:], func=AF.Relu), reads=[ps[b].all()], writes=[rl_r[q]])
                    dve(lambda e: e.scalar_tensor_tensor(out=hidv[hq][:, j, :], in0=ps[b].t[:], scalar=0.0, in1=rlv[q], op0=ALU.max, op1=ALU.mult),
                        reads=[ps[b].all(), rl_r[q]], writes=[ur(hq * 4096 + j * 1024, 1024)])
                group_mm(dr["w_ff1"], 8, f * 512, rhs, epi)
                for slab in range(8):
                    s = wtile(dr["w_ff2"], f * 512, slab * 512)
                    for j in range(4):
                        ob = slab * 4 + j
                        b = pnext()
                        for a in range(4):
                            pe(lambda e, b=b, s=s, a=a, j=j, hq=hq: e.matmul(ps[b].t[:], wr.t[:, s, a, j * 128:(j + 1) * 128], hidv[hq][:, a, :],
                                                                          start=(a == 0), stop=(a == 3)),
                               reads=[wreg(s), hid_r[hq]], writes=[ps[b].all()])
                        dve(lambda e, b=b, ob=ob: e.scalar_tensor_tensor(out=acc.t[:, ob, :], in0=ps[b].t[:], scalar=gt2[:, ob:ob + 1], in1=acc.t[:, ob, :],
                                                                     op0=ALU.mult, op1=ALU.add),
                            reads=[ps[b].all(), mods.all(), accr(ob)], writes=[accr(ob)])
                        if f == 31:
                            st.add(acc.t[:, ob, :], accr(ob))
            return st

        def final_out(row0, st):
            st.finish(D)
            for c in range(KC):
                dve(lambda e, c=c: e.scalar_tensor_tensor(out=acc.t[:, c, :], in0=acc.t[:, c, :], scalar=sm["fing"].t[:, c:c + 1], in1=rstd,
                                                       op0=ALU.mult, op1=ALU.mult),
                    reads=[accr(c), rstd_r, sm["fing"].all()], writes=[accr(c)])
            cnt = 0
            for blk in range(4):
                for half in range(2):
                    q = cnt % 2
                    cnt += 1
                    for c4 in range(4):
                        b = pnext()
                        for cc in range(4):
                            c = half * 16 + c4 * 4 + cc
                            pe(lambda e, b=b, cc=cc, c=c, blk=blk: e.transpose(ps[b].t[:, cc * 128:(cc + 1) * 128], acc.t[:, c, blk * 128:(blk + 1) * 128], sm["identf"].t[:]),
                               reads=[accr(c), sm["identf"].all()], writes=[ps[b].all()])
                        if c4 % 2 == 0:
                            dve(lambda e, b=b, q=q, c4=c4: e.tensor_copy(xs[q][:, c4 * 512:(c4 + 1) * 512], ps[b].t[:]), reads=[ps[b].all()],
                                writes=[ur(q * 8192 + c4 * 2048, 2048)])
                        else:
                            act(lambda e, b=b, q=q, c4=c4: e.activation(out=xs[q][:, c4 * 512:(c4 + 1) * 512], in_=ps[b].t[:], func=AF.Identity), reads=[ps[b].all()],
                                writes=[ur(q * 8192 + c4 * 2048, 2048)])
                    S.dma("sp", out_d[row0 + blk * 128: row0 + (blk + 1) * 128, half * 2048:(half + 1) * 2048], xs[q], reads=[xsr[q]], key="os%d" % q)

        nt = NTILE if not debug else debug.get("_ntile", [NTILE])[0]
        npre = NPRE if not debug else debug.get("_npre", [NPRE])[0]
        e_cnt = [0]
        for ti in range(NPRE - npre, NPRE):
            load_x(ti * T)
            norm_mod(gs1.t, sh1)
            want_kv = (ti == NPRE - 1)
            in_proj(ti, True, want_kv)
            if want_kv:
                shift_halo()
            ssm_pre()
        while ada_slab[0] < 48:
            ada_full_slab()
        dve(lambda e: e.scalar_tensor_tensor(out=gs2.t[:], in0=sc2, scalar=1.0, in1=sm["n2g"].t[:], op0=ALU.add, op1=ALU.mult),
            reads=[mods.all(), sm["n2g"].all()], writes=[gs2.all()])
        dump("mods", mods.t[:], [mods.all()])
        dve(lambda e: e.tensor_scalar(car.t[:], car.t[:], sm["flag"].t[:, 0:1], None, op0=ALU.mult), reads=[car.all(), sm["flag"].all()], writes=[car.all()])
        for i in range(nt):
            ti = NPRE + i
            load_x(ti * T)
            norm_mod(gs1.t, sh1)
            if i == 0:
                dump("x0", acc.t[:, 0:2, :], [acc.all()])
                dump("h1", hb.t[:, 0:2, :], [hb.all()])
                dump("Bexp", Bexp.t[:, 0, :, :], [Bexp.all()])
                dump("rr", rr.t[:], [rr.all()])
                dump("phi", phi.t[:], [phi.all()])
            in_proj(ti, False, True)
            if i == 0:
                dump("q0", Qv[:, 0:2, :], [Qr(0), Qr(1)])
                dump("k0", Kb.t[:, 0:2, :], [Kb.all()])
                dump("ub0", ubv[:, 0:2, :], [ubr(0), ubr(1)])
                dump("va", Va.t[:, 1, :, :], [Va.all()])
            attention(i == 0)
            if i == 0:
                dump("att", hb.t[:, 0:2, :], [hb.all()])
            shift_halo()
            ssm()
            if i == 0:
                dump("gb0", gbv[:, 0:2, :], [Qr(0), Qr(1)])
                dump("car", car.t[:], [car.all()])
            glu_and_norm()
            if i == 0:
                dump("ssm", hb.t[:, 16:18, :], [hb.all()])
            st2 = out_proj()
            if i == 0:
                dump("x1", acc.t[:, 0:2, :], [acc.all()])
            norm_mod(gs2.t, sh2, st2)
            if i == 0:
                dump("h2", hb.t[:, 0:2, :], [hb.all()])
            st3 = ffn()
            if i == 0:
                dump("x2", acc.t[:, 0:2, :], [acc.all()])
            final_out(i * T, st3)
        S.emit()
        build.stats = S.stats
    return nc


def _prep_shared(inp):
    f = np.float32
    sh = {}
    idx = []
    for g in range(8):
        for half in range(2):
            for j in range(4):
                h = 4 * g + j
                idx.extend(range(h * 64 + half * 32, h * 64 + half * 32 + 32))
    for a in range(4):
        for half in range(2):
            for j in range(4):
                idx.extend(range(2048 + a * 64 + half * 32, 2048 + a * 64 + half * 32 + 32))
    idx.extend(range(2560, 4608))
    idx.extend(range(2304, 2560))
    idx = np.asarray(idx)
    assert idx.size == WINP
    sh["w_in"] = np.ascontiguousarray(inp["w_in"][0][:, idx])
    sh["w_ada"] = np.ascontiguousarray(inp["w_ada"][0])
    sh["w_glu"] = np.ascontiguousarray(inp["w_glu"][0])
    sh["w_out"] = np.ascontiguousarray(inp["w_out"][0])
    sh["w_ff1"] = np.ascontiguousarray(inp["w_ff1"][0])
    sh["w_ff2"] = np.ascontiguousarray(inp["w_ff2"][0])
    col = lambda v: np.ascontiguousarray(np.asarray(v, f).reshape(-1, 128).T)
    sh["b_adaT"] = col(inp["b_ada"][0])
    sh["n1g"] = col(inp["norm1_g"][0]); sh["n2g"] = col(inp["norm2_g"][0]); sh["fing"] = col(inp["final_g"])
    sh["sinks_bc"] = np.ascontiguousarray(np.broadcast_to(np.asarray(inp["sinks"][0], f)[None, :], (128, 32)))
    sh["b_gluT"] = col(inp["b_glu"][0]); sh["ssm_gT"] = col(inp["ssm_out_g"][0]); sh["attn_gT"] = col(inp["attn_out_g"][0])
    sh["identf"] = np.eye(128, dtype=f); sh["identb"] = np.eye(128, dtype=f)
    sh["iota"] = np.ascontiguousarray(np.broadcast_to(np.arange(512, dtype=f)[None, :], (128, 512)))
    lre, lim, lst = [np.asarray(inp[k][0], f) for k in ("ssm_lam_re", "ssm_lam_im", "ssm_log_step")]
    bre, bim = np.asarray(inp["ssm_b_re"][0], f), np.asarray(inp["ssm_b_im"][0], f)
    cre, cim = np.asarray(inp["ssm_c_re"][0], f), np.asarray(inp["ssm_c_im"][0], f)
    dsk = np.asarray(inp["ssm_d"][0], f)
    q = np.arange(128)
    pi_ = np.arange(64)
    Gp = 2 * pi_[None, :] + (q // 64)[:, None]
    Pp = np.broadcast_to((q % 64)[:, None], (128, 64))
    sh["lamre_p"] = np.ascontiguousarray(lre[Gp, Pp]); sh["lamim_p"] = np.ascontiguousarray(lim[Gp, Pp])
    sh["lstep_p"] = np.ascontiguousarray(lst[Gp])
    r = np.arange(128)
    cprime = np.arange(16)
    colx = np.arange(128)
    G = (8 * cprime[None, :] + 2 * (r // 32)[:, None] + ((r % 32) // 16)[:, None])[:, :, None]
    P = (colx % 64)[None, None, :]
    H = (r % 16)[:, None, None]
    same = ((colx // 64)[None, None, :] == ((r % 32) // 16)[:, None, None])
    Gb = np.broadcast_to(G, (128, 16, 128)); Pb = np.broadcast_to(P, (128, 16, 128)); Hb = np.broadcast_to(H, (128, 16, 128))
    sh["lamre_e"] = np.ascontiguousarray(lre[Gb, Pb]).reshape(128, 2048)
    sh["lamim_e"] = np.ascontiguousarray(lim[Gb, Pb]).reshape(128, 2048)
    sh["lstep_e"] = np.ascontiguousarray(lst[Gb]).reshape(128, 2048)
    sh["bre_e"] = np.where(same, bre[Gb, Pb, Hb], f(0)).astype(f).reshape(128, 2048)
    sh["bim_e"] = np.where(same, bim[Gb, Pb, Hb], f(0)).astype(f).reshape(128, 2048)
    c32 = np.arange(32)
    Gc = np.broadcast_to((2 * pi_[None, :, None] + (q // 64)[:, None, None]), (128, 64, 32))
    Hc = np.broadcast_to((c32 % 16)[None, None, :], (128, 64, 32))
    Pc = np.broadcast_to((q % 64)[:, None, None], (128, 64, 32))
    samec = ((c32 // 16)[None, None, :] == (q // 64)[:, None, None])
    sh["cre_e"] = np.where(samec, cre[Gc, Hc, Pc], f(0)).astype(f)
    sh["cim_e"] = np.where(samec, cim[Gc, Hc, Pc], f(0)).astype(f)
    ch = (8 * cprime[None, :] + 2 * (r // 32)[:, None]) * 16 + (r % 32)[:, None]
    dd = np.where((c32[None, None, :] == (r % 32)[:, None, None]), dsk[ch][:, :, None], f(0)).astype(f)
    sh["d_e"] = np.ascontiguousarray(dd)
    j = np.arange(128)[:, None]; i = np.arange(128)[None, :]
    mprev = (j > i).astype(f); mcur = (j <= i).astype(f)
    sh["mask_n"] = np.ascontiguousarray(np.concatenate([mprev, mcur], axis=1))
    return sh, (mprev, mcur)


def _rope_tables(base):
    f = np.float32
    half = 32
    inv_freq = (f(10000.0) ** (-(np.arange(half, dtype=f) / f(half)))).astype(f)
    pos = (np.arange(4096, dtype=np.int64) + base).astype(f)
    ang = (pos[None, :] * inv_freq[np.arange(128) % 32][:, None]).astype(f)
    return np.cos(ang).astype(f), np.sin(ang).astype(f)


_NC_CACHE = {}


def kernel(**inputs):
    inp = {k: np.asarray(v) for k, v in inputs.items()}
    f = np.float32
    sh, (mprev, mcur) = _prep_shared(inp)
    x = np.asarray(inp["x"], f)
    c = np.asarray(inp["c"], f)
    in_maps = []
    for core in range(8):
        b, half = core // 2, core % 2
        m = dict(sh)
        if half == 0:
            xcat = np.concatenate([np.zeros((2048, D), f), x[b, 0:2048]], axis=0)
        else:
            xcat = x[b]
        m["xcat"] = np.ascontiguousarray(xcat)
        m["cT"] = np.ascontiguousarray(c[b].reshape(32, 128).T)
        m["flag"] = np.full((128, 1), float(half), f)
        m["mask_f"] = np.ascontiguousarray(np.concatenate([mprev if half == 1 else np.zeros_like(mprev), mcur], axis=1))
        rc, rs = _rope_tables(half * 2048 - 2048)
        m["ropec"], m["ropes"] = rc, rs
        in_maps.append(m)
    if "nc" not in _NC_CACHE:
        _NC_CACHE["nc"] = build()
    nc = _NC_CACHE["nc"]
    res = run_bass_kernel_spmd(nc, in_maps, core_ids=list(range(8)))
    out = np.empty((4, 4096, D), f)
    for core in range(8):
        b, half = core // 2, core % 2
        out[b, half * 2048:(half + 1) * 2048] = res.results[core]["out"]
    return out
```

```python
import contextlib
import math
import numpy as np
import concourse.bass as bass
import concourse.mybir as mybir
from concourse.bass_utils import run_bass_kernel_spmd

F32 = mybir.dt.float32
BF16 = mybir.dt.bfloat16
AF = mybir.ActivationFunctionType
ALU = mybir.AluOpType

D = 4096
KC = 32
T = 512
NTILE = 4
NPRE = 4
DFF = 16384
EPS = 1e-6
WINP = 5376
NW = 4
MAGIC = 12582912.0
TWO_PI = 2.0 * math.pi


class _Op:
    __slots__ = ("eng", "fn", "deps", "is_dma", "sem", "semval", "needs_inc", "incval")

    def __init__(self, eng, fn):
        self.eng = eng
        self.fn = fn
        self.deps = []
        self.is_dma = False
        self.sem = None
        self.semval = 0
        self.needs_inc = False
        self.incval = 0


class Sched:
    ENGS = ("pe", "act", "dve", "pool", "sp")

    def __init__(self, nc, es):
        self.nc = nc
        self.es = es
        self.ops = []
        self.W = {}
        self.R = {}
        self.dsems = {}
        self.dcount = {}

    def _deps(self, op, reads, writes):
        deps = op.deps
        for (k, lo, hi) in reads:
            for (l2, h2, o2) in self.W.get(k, ()):
                if l2 < hi and lo < h2:
                    deps.append(o2)
            self.R.setdefault(k, []).append((lo, hi, op))
        for (k, lo, hi) in writes:
            wl = self.W.get(k, [])
            rl = self.R.get(k, [])
            for (l2, h2, o2) in wl:
                if l2 < hi and lo < h2:
                    deps.append(o2)
            for (l2, h2, o2) in rl:
                if l2 < hi and lo < h2 and o2 is not op:
                    deps.append(o2)
            self.W[k] = [t for t in wl if not (lo <= t[0] and t[1] <= hi)] + [(lo, hi, op)]
            self.R[k] = [t for t in rl if not (lo <= t[0] and t[1] <= hi) or t[2] is op]

    def op(self, eng, fn, reads=(), writes=()):
        o = _Op(eng, fn)
        self._deps(o, reads, writes)
        self.ops.append(o)
        return o

    def dma(self, eng, out, in_, reads=(), writes=(), key=None):
        o = _Op(eng, None)
        o.is_dma = True
        if key not in self.dsems:
            self.dsems[key] = self.es.enter_context(self.nc.semaphore("d_" + key))
            self.dcount[key] = 0
        self.dcount[key] += 16
        o.sem = self.dsems[key]
        o.semval = self.dcount[key]
        o.fn = lambda e, out=out, in_=in_: e.dma_start(out=out, in_=in_)
        self._deps(o, reads, writes)
        self.ops.append(o)
        return o

    def emit(self):
        nc = self.nc
        esem = {e: self.es.enter_context(nc.semaphore("e_" + e)) for e in self.ENGS}
        fin = _Op("sp", None)
        last = {}
        for o in self.ops:
            if o.is_dma:
                last[("d", id(o.sem))] = o
            else:
                last[("e", o.eng)] = o
        fin.deps = list(last.values())
        self.ops.append(fin)
        idx = {id(o): i for i, o in enumerate(self.ops)}
        for o in self.ops:
            best = {}
            for d in o.deps:
                k = ("d", id(d.sem)) if d.is_dma else ("e", d.eng)
                if k not in best or idx[id(d)] > idx[id(best[k])]:
                    best[k] = d
            o.deps = list(best.values())
        for o in self.ops:
            for d in o.deps:
                if not d.is_dma:
                    if d.eng == "pe" and o.eng == "pe" and not o.is_dma:
                        continue
                    d.needs_inc = True
        cnt = {e: 0 for e in self.ENGS}
        for o in self.ops:
            if (not o.is_dma) and o.needs_inc:
                cnt[o.eng] += 1
                o.incval = cnt[o.eng]
        per = {e: [] for e in self.ENGS}
        for o in self.ops:
            per[o.eng].append(o)
        self.stats = {e: len(per[e]) for e in self.ENGS}
        self.stats["incs"] = dict(cnt)

        def run(engname, e):
            waited = {}
            for o in per[engname]:
                for d in o.deps:
                    if d.is_dma:
                        s, v = d.sem, d.semval
                    else:
                        if d.eng == "pe" and engname == "pe" and not o.is_dma:
                            continue
                        s, v = esem[d.eng], d.incval
                    kk = id(s)
                    if waited.get(kk, 0) >= v:
                        continue
                    waited[kk] = v
                    e.wait_ge(s, v)
                if o.fn is None:
                    continue
                ins = o.fn(e)
                if o.is_dma:
                    ins.then_inc(o.sem, 16)
                elif o.needs_inc:
                    ins.then_inc(esem[engname], 1)

        with nc.Block() as block:
            @block.tensor
            def _(e):
                run("pe", e)

            @block.scalar
            def _(e):
                run("act", e)

            @block.vector
            def _(e):
                run("dve", e)

            @block.gpsimd
            def _(e):
                run("pool", e)

            @block.sync
            def _(e):
                run("sp", e)


class Buf:
    def __init__(self, nc, es, name, shape, dtype, psum=False):
        self.name = name
        if psum:
            self.t = es.enter_context(nc.psum_tensor("p_" + name, shape, dtype))
        else:
            self.t = es.enter_context(nc.sbuf_tensor("s_" + name, shape, dtype))
        n = 1
        for s in shape[1:]:
            n *= s
        self.n = n

    def all(self):
        return (self.name, 0, self.n)

    def r(self, lo, hi):
        return (self.name, lo, hi)


SMALL_IN = [
    ("cT", [128, 32]), ("b_adaT", [128, 192]), ("n1g", [128, 32]), ("n2g", [128, 32]),
    ("fing", [128, 32]), ("sinks_bc", [128, 32]), ("flag", [128, 1]),
    ("lamre_p", [128, 64]), ("lamim_p", [128, 64]), ("lstep_p", [128, 64]),
    ("b_gluT", [128, 16]), ("ssm_gT", [128, 16]), ("attn_gT", [128, 16]), ("identf", [128, 128]),
]
CAST_IN = [
    ("cre_e", [128, 64, 32]), ("cim_e", [128, 64, 32]), ("d_e", [128, 16, 32]),
    ("mask_n", [128, 256]), ("mask_f", [128, 256]), ("identb", [128, 128]),
]
BIG_IN = [
    ("lamre_e", [128, 2048]), ("lamim_e", [128, 2048]), ("lstep_e", [128, 2048]),
    ("bre_e", [128, 2048]), ("bim_e", [128, 2048]),
]


def build(debug=None):
    nc = bass.Bass("TRN2", target_bir_lowering=False)
    dr = {}

    def din(name, shape):
        dr[name] = nc.dram_tensor(name, shape, F32, kind="ExternalInput").ap()

    din("xcat", [4096, D])
    for n, s in SMALL_IN + CAST_IN + BIG_IN:
        din(n, s)
    din("iota", [128, 512])
    din("ropec", [128, 4096])
    din("ropes", [128, 4096])
    din("w_ada", [D, 6 * D])
    din("w_in", [D, WINP])
    din("w_glu", [2048, 2048])
    din("w_out", [D, D])
    din("w_ff1", [D, DFF])
    din("w_ff2", [DFF, D])
    out_d = nc.dram_tensor("out", [2048, D], F32, kind="ExternalOutput").ap()
    tabs = nc.dram_tensor("tabs", [64, 2, 128, 512], F32, kind="Internal").ap()
    wtabs = nc.dram_tensor("wtabs", [64, 2, 128, 512], F32, kind="Internal").ap()

    with contextlib.ExitStack() as es:
        S = Sched(nc, es)
        B = lambda name, shape, dt=F32, psum=False: Buf(nc, es, name, shape, dt, psum)
        acc = B("acc", [128, KC, T])
        hb = B("hb", [128, KC, T], BF16)
        wr = B("wr", [128, NW, 4, 512], BF16)
        Kb = B("Kb", [128, 8, 640], BF16)
        Va = B("Va", [128, 5, 4, 65], BF16)
        U = B("U", [128, 13312])
        ps = [B("ps%d" % i, [128, 512], F32, psum=True) for i in range(8)]
        sm = {n: B(n, s) for n, s in SMALL_IN}
        cb = {n: B(n, s, BF16) for n, s in CAST_IN}
        Bexp = B("Bexp", [128, 16, 2, 128], BF16)
        mods = B("mods", [128, 192])
        gs1 = B("gs1", [128, 32]); gs2 = B("gs2", [128, 32])
        cact = B("cact", [128, 32], BF16)
        esink = B("esink", [128, 32])
        ones = B("ones", [128, 128])
        rr = B("rr", [128, 64]); cos1 = B("cos1", [128, 64]); sin1 = B("sin1", [128, 64]); phi = B("phi", [128, 64])
        car = B("car", [128, 2, 64]); wini = B("wini", [128, 2, 64]); ctmp = B("ctmp", [128, 4, 64])
        halfpi = B("halfpi", [128, 1])
        magic = B("magic", [128, 1]); magic2 = B("magic2", [128, 1])
        sml = B("sml", [128, 16])

        def uf(lo_b, n):
            return U.t[:, lo_b // 4: lo_b // 4 + n]

        def ubf(lo_b, n):
            return U.t[:, lo_b // 4: lo_b // 4 + n // 2].bitcast(BF16)

        def ur(lo_b, nbytes):
            return ("U", lo_b // 4, (lo_b + nbytes) // 4)

        K1 = 1024
        Qv = ubf(0, 16 * 512).rearrange("p (c t) -> p c t", c=16)
        gbv = Qv
        ubv = ubf(16 * K1, 16 * 512).rearrange("p (c t) -> p c t", c=16)

        def Qr(c):
            return ur(c * 1024, 1024)

        def ubr(c):
            return ur(16 * K1 + c * 1024, 1024)

        X0 = 32 * K1

        pctr = [0]

        def pnext():
            b = pctr[0] % 7
            pctr[0] += 1
            return b

        wctr = [0]

        def wtile(Wd, r0, c0, ncols=512):
            s = wctr[0] % NW
            wctr[0] += 1
            S.dma("pool", wr.t[:, s, :, 0:ncols],
                  Wd[r0:r0 + 512, c0:c0 + ncols].rearrange("(a p) n -> p a n", p=128),
                  writes=[wr.r(s * 2048, (s + 1) * 2048)], key="w%d" % s)
            return s

        def wreg(s):
            return wr.r(s * 2048, (s + 1) * 2048)

        def dve(fn, reads=(), writes=()):
            return S.op("dve", fn, reads, writes)

        def act(fn, reads=(), writes=()):
            return S.op("act", fn, reads, writes)

        def pe(fn, reads=(), writes=()):
            return S.op("pe", fn, reads, writes)

        def group_mm(Wd, nk4, c0, rhs, epi, nj=4):
            banks = [pnext() for _ in range(nj)]
            nk = nk4 * 4
            for k4 in range(nk4):
                s = wtile(Wd, k4 * 512, c0, nj * 128)
                for a in range(4):
                    kc = k4 * 4 + a
                    rap, rreg = rhs(kc)
                    for j in range(nj):
                        pe(lambda e, b=banks[j], s=s, a=a, j=j, rap=rap, kc=kc:
                           e.matmul(ps[b].t[:], wr.t[:, s, a, j * 128:(j + 1) * 128], rap,
                                    start=(kc == 0), stop=(kc == nk - 1)),
                           reads=[wreg(s), rreg], writes=[ps[banks[j]].all()])
            for j in range(nj):
                epi(j, banks[j])

        def dump(name, ap, reads):
            if debug and name in debug:
                dd = nc.dram_tensor("dbg_" + name, list(ap.shape), ap.dtype, kind="ExternalOutput").ap()
                S.dma("sp", dd, ap, reads=reads, key="dbg_" + name)

        for n, s in SMALL_IN:
            S.dma("sp", sm[n].t[:], dr[n], writes=[sm[n].all()], key="ld_" + n)
        for n, s in CAST_IN:
            S.dma("pool", cb[n].t[:], dr[n], writes=[cb[n].all()], key="ld_" + n)
        dve(lambda e: e.memset(ones.t[:], 1.0), writes=[ones.all()])
        dve(lambda e: e.memset(halfpi.t[:], math.pi / 2), writes=[halfpi.all()])
        dve(lambda e: e.memset(magic.t[:], MAGIC), writes=[magic.all()])
        dve(lambda e: e.memset(magic2.t[:], -MAGIC), writes=[magic2.all()])
        dve(lambda e: e.memset(Va.t[:], 0.0), writes=[Va.all()])
        dve(lambda e: e.memset(Va.t[:, :, :, 64:65], 1.0), writes=[Va.all()])
        dve(lambda e: e.memset(Kb.t[:], 0.0), writes=[Kb.all()])
        dve(lambda e: e.memset(car.t[:], 0.0), writes=[car.all()])
        act(lambda e: e.activation(out=cact.t[:], in_=sm["cT"].t[:], func=AF.Silu),
            reads=[sm["cT"].all()], writes=[cact.all()])
        act(lambda e: e.activation(out=esink.t[:], in_=sm["sinks_bc"].t[:], func=AF.Exp),
            reads=[sm["sinks_bc"].all()], writes=[esink.all()])

        def ada_all(slabs, banks):
            for slab in slabs:
                for k4 in range(8):
                    s = wtile(dr["w_ada"], k4 * 512, slab * 512)
                    for a in range(4):
                        kc = k4 * 4 + a
                        for j in range(4):
                            pe(lambda e, s=s, a=a, j=j, kc=kc, b=banks[j]:
                               e.matmul(ps[b].t[:, 0:1], wr.t[:, s, a, j * 128:(j + 1) * 128],
                                        cact.t[:, kc:kc + 1], start=(kc == 0), stop=(kc == 31)),
                               reads=[wreg(s), cact.all()], writes=[ps[banks[j]].all()])
                    if k4 == 7:
                        for j in range(4):
                            col = 4 * slab + j
                            dve(lambda e, b=banks[j], col=col: e.tensor_tensor(out=mods.t[:, col:col + 1], in0=ps[b].t[:, 0:1],
                                                                              in1=sm["b_adaT"].t[:, col:col + 1], op=ALU.add),
                                reads=[ps[banks[j]].all(), sm["b_adaT"].all()], writes=[mods.r(col, col + 1)])
                    yield

        sh1 = mods.t[:, 0:32]; sc1 = mods.t[:, 32:64]; gt1 = mods.t[:, 64:96]
        sh2 = mods.t[:, 96:128]; sc2 = mods.t[:, 128:160]; gt2 = mods.t[:, 160:192]

        def sincos(ang, n, tmpA, tmpB, sin_out, cos_out, rd, wrs):
            act(lambda e: e.activation(out=tmpA, in_=ang, func=AF.Identity, scale=1.0 / TWO_PI, bias=magic.t[:]), reads=rd + [magic.all()], writes=wrs)
            act(lambda e: e.activation(out=tmpB, in_=tmpA, func=AF.Identity, scale=1.0, bias=magic2.t[:]), reads=wrs + [magic2.all()], writes=wrs)
            dve(lambda e: e.scalar_tensor_tensor(out=ang, in0=tmpB, scalar=-TWO_PI, in1=ang, op0=ALU.mult, op1=ALU.add), reads=rd + wrs, writes=rd)
            dve(lambda e: e.tensor_scalar(ang, ang, 3.1415925, -3.1415925, op0=ALU.min, op1=ALU.max), reads=rd, writes=rd)
            act(lambda e: e.activation(out=sin_out, in_=ang, func=AF.Sin), reads=rd, writes=wrs)
            dve(lambda e: e.scalar_tensor_tensor(out=tmpA, in0=ang, scalar=-1.0, in1=ang, op0=ALU.mult, op1=ALU.max), reads=rd, writes=wrs)
            act(lambda e: e.activation(out=cos_out, in_=tmpA, func=AF.Sin, bias=halfpi.t[:], scale=-1.0),
                reads=wrs + [halfpi.all()], writes=wrs)

        def sincos_g(ang, tmpA, tmpB, sin_out, cos_out, rd, wrs):
            act(lambda e: e.activation(out=tmpA, in_=ang, func=AF.Identity, scale=1.0 / TWO_PI, bias=magic.t[:]), reads=rd + [magic.all()], writes=wrs)
            yield
            act(lambda e: e.activation(out=tmpB, in_=tmpA, func=AF.Identity, scale=1.0, bias=magic2.t[:]), reads=wrs + [magic2.all()], writes=wrs)
            yield
            dve(lambda e: e.scalar_tensor_tensor(out=ang, in0=tmpB, scalar=-TWO_PI, in1=ang, op0=ALU.mult, op1=ALU.add), reads=rd + wrs, writes=rd)
            yield
            dve(lambda e: e.tensor_scalar(ang, ang, 3.1415925, -3.1415925, op0=ALU.min, op1=ALU.max), reads=rd, writes=rd)
            yield
            act(lambda e: e.activation(out=sin_out, in_=ang, func=AF.Sin), reads=rd, writes=wrs)
            yield
            dve(lambda e: e.scalar_tensor_tensor(out=tmpA, in0=ang, scalar=-1.0, in1=ang, op0=ALU.mult, op1=ALU.max), reads=rd, writes=wrs)
            yield
            act(lambda e: e.activation(out=cos_out, in_=tmpA, func=AF.Sin, bias=halfpi.t[:], scale=-1.0),
                reads=wrs + [halfpi.all()], writes=wrs)
            yield

        av = acc.t[:].rearrange("p c t -> p (c t)")

        def a_(i):
            return av[:, i * 2048:(i + 1) * 2048]

        AR = [acc.all()]
        for i, n in enumerate(["lamre_e", "lamim_e", "lstep_e", "bre_e", "bim_e"]):
            S.dma("sp", a_(i), dr[n], writes=AR, key="ld_big")
        LRE, LIM, LST, BRE, BIM, T5, T6, T7 = [a_(i) for i in range(8)]
        act(lambda e: e.activation(out=LST, in_=LST, func=AF.Exp), reads=AR, writes=AR)
        dve(lambda e: e.tensor_tensor(out=T5, in0=LIM, in1=LST, op=ALU.mult), reads=AR, writes=AR)
        dve(lambda e: e.tensor_tensor(out=LST, in0=LRE, in1=LST, op=ALU.mult), reads=AR, writes=AR)
        act(lambda e: e.activation(out=LST, in_=LST, func=AF.Exp), reads=AR, writes=AR)
        hbf = hb.t[:].rearrange("p c t -> p (c t)").bitcast(F32)
        HR = [hb.all()]
        sincos(T5, 2048, hbf[:, 0:2048], hbf[:, 2048:4096], T6, T7, AR, HR + AR)
        dve(lambda e: e.tensor_tensor(out=T6, in0=T6, in1=LST, op=ALU.mult), reads=AR, writes=AR)
        dve(lambda e: e.tensor_tensor(out=T7, in0=T7, in1=LST, op=ALU.mult), reads=AR, writes=AR)
        dve(lambda e: e.tensor_scalar(T7, T7, -1.0, None, op0=ALU.add), reads=AR, writes=AR)
        h0, h1, h2, h3 = [hbf[:, i * 2048:(i + 1) * 2048] for i in range(4)]
        dve(lambda e: e.tensor_tensor(out=h0, in0=LRE, in1=LRE, op=ALU.mult), reads=AR, writes=HR)
        dve(lambda e: e.tensor_tensor(out=h1, in0=LIM, in1=LIM, op=ALU.mult), reads=AR, writes=HR)
        dve(lambda e: e.tensor_tensor(out=h0, in0=h0, in1=h1, op=ALU.add), reads=HR, writes=HR)
        dve(lambda e: e.reciprocal(h0, h0), reads=HR, writes=HR)
        dve(lambda e: e.tensor_tensor(out=h1, in0=T7, in1=LRE, op=ALU.mult), reads=AR, writes=HR)
        dve(lambda e: e.tensor_tensor(out=h3, in0=T6, in1=LIM, op=ALU.mult), reads=AR, writes=HR)
        dve(lambda e: e.tensor_tensor(out=h1, in0=h1, in1=h3, op=ALU.add), reads=HR, writes=HR)
        dve(lambda e: e.tensor_tensor(out=h1, in0=h1, in1=h0, op=ALU.mult), reads=HR, writes=HR)
        dve(lambda e: e.tensor_tensor(out=h2, in0=T6, in1=LRE, op=ALU.mult), reads=AR, writes=HR)
        dve(lambda e: e.tensor_tensor(out=h3, in0=T7, in1=LIM, op=ALU.mult), reads=AR, writes=HR)
        dve(lambda e: e.tensor_tensor(out=h2, in0=h2, in1=h3, op=ALU.subtract), reads=HR, writes=HR)
        dve(lambda e: e.tensor_tensor(out=h2, in0=h2, in1=h0, op=ALU.mult), reads=HR, writes=HR)
        dve(lambda e: e.tensor_tensor(out=T5, in0=h1, in1=BRE, op=ALU.mult), reads=AR + HR, writes=AR)
        dve(lambda e: e.tensor_tensor(out=h3, in0=h2, in1=BIM, op=ALU.mult), reads=AR + HR, writes=HR)
        dve(lambda e: e.tensor_tensor(out=Bexp.t[:, :, 0, :], in0=T5.rearrange("p (c n) -> p c n", c=16),
                                      in1=h3.rearrange("p (c n) -> p c n", c=16), op=ALU.subtract),
            reads=AR + HR, writes=[Bexp.all()])
        dve(lambda e: e.tensor_tensor(out=T5, in0=h1, in1=BIM, op=ALU.mult), reads=AR + HR, writes=AR)
        dve(lambda e: e.tensor_tensor(out=h3, in0=h2, in1=BRE, op=ALU.mult), reads=AR + HR, writes=HR)
        dve(lambda e: e.tensor_tensor(out=Bexp.t[:, :, 1, :], in0=T5.rearrange("p (c n) -> p c n", c=16),
                                      in1=h3.rearrange("p (c n) -> p c n", c=16), op=ALU.add),
            reads=AR + HR, writes=[Bexp.all()])

        lnr = B("lnr", [128, 64]); nphi = B("nphi", [128, 64]); phi511 = B("phi511", [128, 64])
        nlnr = B("nlnr", [128, 64]); lnr511 = B("lnr511", [128, 64]); Gre = B("Gre", [128, 64]); Gim = B("Gim", [128, 64])
        accS = B("accS", [128, 4, 64])
        PR = [rr.all(), cos1.all(), sin1.all(), phi.all(), ctmp.all(), lnr.all(), nphi.all(), phi511.all(), nlnr.all(), lnr511.all(), Gre.all(), Gim.all()]
        st_p = ctmp.t[:, 0, :]
        act(lambda e: e.activation(out=st_p, in_=sm["lstep_p"].t[:], func=AF.Exp), reads=[sm["lstep_p"].all()], writes=PR)
        dve(lambda e: e.tensor_tensor(out=lnr.t[:], in0=sm["lamre_p"].t[:], in1=st_p, op=ALU.mult), reads=PR + [sm["lamre_p"].all()], writes=PR)
        act(lambda e: e.activation(out=rr.t[:], in_=lnr.t[:], func=AF.Exp), reads=PR, writes=PR)
        dve(lambda e: e.tensor_tensor(out=phi.t[:], in0=sm["lamim_p"].t[:], in1=st_p, op=ALU.mult), reads=PR + [sm["lamim_p"].all()], writes=PR)
        sincos(phi.t[:], 64, ctmp.t[:, 1, :], ctmp.t[:, 2, :], sin1.t[:], cos1.t[:], PR, PR)

        dve(lambda e: e.tensor_scalar(nphi.t[:], phi.t[:], -1.0, None, op0=ALU.mult), reads=PR, writes=PR)
        dve(lambda e: e.tensor_scalar(phi511.t[:], phi.t[:], 511.0, None, op0=ALU.mult), reads=PR, writes=PR)
        dve(lambda e: e.tensor_scalar(nlnr.t[:], lnr.t[:], -1.0, None, op0=ALU.mult), reads=PR, writes=PR)
        dve(lambda e: e.tensor_scalar(lnr511.t[:], lnr.t[:], 511.0, None, op0=ALU.mult), reads=PR, writes=PR)
        dve(lambda e: e.tensor_scalar(ctmp.t[:, 0, :], phi.t[:], 512.0, None, op0=ALU.mult), reads=PR, writes=PR)
        sincos(ctmp.t[:, 0, :], 64, ctmp.t[:, 1, :], ctmp.t[:, 2, :], Gim.t[:], Gre.t[:], PR, PR)
        act(lambda e: e.activation(out=ctmp.t[:, 3, :], in_=lnr.t[:], func=AF.Exp, scale=512.0), reads=PR, writes=PR)
        dve(lambda e: e.tensor_tensor(out=Gre.t[:], in0=Gre.t[:], in1=ctmp.t[:, 3, :], op=ALU.mult), reads=PR, writes=PR)
        dve(lambda e: e.tensor_tensor(out=Gim.t[:], in0=Gim.t[:], in1=ctmp.t[:, 3, :], op=ALU.mult), reads=PR, writes=PR)

        c511 = B("c511", [128, 64]); s511 = B("s511", [128, 64]); wlast = B("wlast", [128, 2, 64])
        PR = PR + [c511.all(), s511.all()]
        dve(lambda e: e.tensor_scalar(ctmp.t[:, 0, :], phi.t[:], 511.0, None, op0=ALU.mult), reads=PR, writes=PR)
        sincos(ctmp.t[:, 0, :], 64, ctmp.t[:, 1, :], ctmp.t[:, 2, :], s511.t[:], c511.t[:], PR, PR)
        iotv = uf(X0 + 12288, 512); iot_r = ur(X0 + 12288, 2048)
        revv = uf(X0 + 14336, 512); rev_r = ur(X0 + 14336, 2048)
        S.dma("sp", iotv, dr["iota"], writes=[iot_r], key="ld_iota")
        dve(lambda e: e.tensor_scalar(revv, iotv, -1.0, 511.0, op0=ALU.mult, op1=ALU.add), reads=[iot_r], writes=[rev_r])
        NBT = 4
        tbb = [av[:, i * 2048:(i + 1) * 2048] for i in range(6)]
        v3 = lambda ap: ap.rearrange("p (q t) -> p q t", q=NBT)
        bc_t = lambda ap: ap.unsqueeze(1).to_broadcast([128, NBT, 512])
        hbs = [hbf[:, i * 2048:(i + 1) * 2048] for i in range(4)]

        def gen_E_batch(bt):
            p0 = bt * NBT
            bc_p = lambda buf: buf.t[:, p0:p0 + NBT].unsqueeze(2).to_broadcast([128, NBT, 512])
            dve(lambda e, a=bc_t(iotv), b=bc_p(nphi): e.tensor_tensor(out=v3(hbs[0]), in0=a, in1=b, op=ALU.mult), reads=[iot_r] + PR, writes=HR)
            yield
            yield from sincos_g(hbs[0], hbs[1], hbs[2], hbs[3], hbs[2], HR, HR)
            S.dma("sp", tabs[p0:p0 + NBT, 1].rearrange("q p t -> p q t"), v3(hbs[3]), reads=HR, writes=[("tabs", p0, p0 + NBT)], key="tabw")
            S.dma("sp", tabs[p0:p0 + NBT, 0].rearrange("q p t -> p q t"), v3(hbs[2]), reads=HR, writes=[("tabs", p0, p0 + NBT)], key="tabw")
            yield

        def gen_W_batch(bt):
            p0 = bt * NBT
            bc_p = lambda buf: buf.t[:, p0:p0 + NBT].unsqueeze(2).to_broadcast([128, NBT, 512])
            dve(lambda e, a=bc_t(revv), b=bc_p(phi): e.tensor_tensor(out=v3(tbb[0]), in0=a, in1=b, op=ALU.mult), reads=[rev_r] + PR, writes=AR)
            yield
            dve(lambda e, a=bc_t(revv), b=bc_p(lnr): e.tensor_tensor(out=v3(tbb[3]), in0=a, in1=b, op=ALU.mult), reads=[rev_r] + PR, writes=AR)
            yield
            act(lambda e: e.activation(out=tbb[3], in_=tbb[3], func=AF.Exp), reads=AR, writes=AR)
            yield
            yield from sincos_g(tbb[0], tbb[1], tbb[2], tbb[4], tbb[5], AR, AR)
            dve(lambda e: e.tensor_tensor(out=tbb[1], in0=tbb[5], in1=tbb[3], op=ALU.mult), reads=AR, writes=AR)
            yield
            dve(lambda e: e.tensor_tensor(out=tbb[2], in0=tbb[4], in1=tbb[3], op=ALU.mult), reads=AR, writes=AR)
            yield
            S.dma("sp", wtabs[p0:p0 + NBT, 0].rearrange("q p t -> p q t"), v3(tbb[1]), reads=AR, writes=[("wtabs", p0, p0 + NBT)], key="wtabw")
            S.dma("sp", wtabs[p0:p0 + NBT, 1].rearrange("q p t -> p q t"), v3(tbb[2]), reads=AR, writes=[("wtabs", p0, p0 + NBT)], key="wtabw")
            yield

        def interleave(*gens):
            gens = list(gens)
            while gens:
                for g_ in list(gens):
                    try:
                        next(g_)
                    except StopIteration:
                        gens.remove(g_)

        for i in range(16):
            for _ in ada_all(range(i, i + 1), [0, 1, 2, 3]):
                pass
            interleave(gen_W_batch(i), gen_E_batch(i))
        pctr[0] = 4
        dve(lambda e: e.scalar_tensor_tensor(out=gs1.t[:], in0=sc1, scalar=1.0, in1=sm["n1g"].t[:], op0=ALU.add, op1=ALU.mult),
            reads=[mods.all(), sm["n1g"].all()], writes=[gs1.all()])

        sqv = [uf(X0 + i * 2048, 512) for i in range(2)]
        sqr = [ur(X0 + i * 2048, 2048) for i in range(2)]
        rstd = uf(X0 + 4096, 512); rstd_r = ur(X0 + 4096, 2048)
        ntv = [uf(X0 + 6144 + i * 2048, 512) for i in range(2)]
        ntr = [ur(X0 + 6144 + i * 2048, 2048) for i in range(2)]
        s2v = uf(X0 + 10240, 512); s2_r = ur(X0 + 10240, 2048)
        sq4 = [uf(X0 + 12288 + i * 2048, 512) for i in range(4)]; sq4_r = [ur(X0 + 12288 + i * 2048, 2048) for i in range(4)]
        sqc = [0]

        class Stats:
            def __init__(self):
                self.i = 0

            def add(self, ap, reg):
                q = sqc[0] % 4
                sqc[0] += 1
                if self.i == 0:
                    act(lambda e: e.activation(out=s2v, in_=ap, func=AF.Square), reads=[reg], writes=[s2_r])
                else:
                    act(lambda e: e.activation(out=sq4[q], in_=ap, func=AF.Square), reads=[reg], writes=[sq4_r[q]])
                    dve(lambda e: e.tensor_tensor(out=s2v, in0=s2v, in1=sq4[q], op=ALU.add), reads=[s2_r, sq4_r[q]], writes=[s2_r])
                self.i += 1

            def finish(self, nfeat):
                pe(lambda e: e.matmul(ps[SSB].t[:], ones.t[:], s2v, start=True, stop=True), reads=[ones.all(), s2_r], writes=[ps[SSB].all()])
                finish_stats(nfeat)

        SSB = 7

        def accr(c):
            return acc.r(c * T, (c + 1) * T)

        def hbr(c):
            return hb.r(c * T, (c + 1) * T)

        def rms_stats(chunks, nfeat):
            n = len(chunks)
            for i, (ap, reg) in enumerate(chunks):
                q = i % 2
                act(lambda e, ap=ap, q=q: e.activation(out=sqv[q], in_=ap, func=AF.Square), reads=[reg], writes=[sqr[q]])
                pe(lambda e, q=q, i=i: e.matmul(ps[SSB].t[:], ones.t[:], sqv[q], start=(i == 0), stop=(i == n - 1)),
                   reads=[ones.all(), sqr[q]], writes=[ps[SSB].all()])
            finish_stats(nfeat)

        def finish_stats(nfeat):
            dve(lambda e: e.tensor_scalar(rstd, ps[SSB].t[:], 1.0 / nfeat, EPS, op0=ALU.mult, op1=ALU.add),
                reads=[ps[SSB].all()], writes=[rstd_r])
            act(lambda e: e.activation(out=rstd, in_=rstd, func=AF.Sqrt), reads=[rstd_r], writes=[rstd_r])
            dve(lambda e: e.reciprocal(rstd, rstd), reads=[rstd_r], writes=[rstd_r])

        def norm_mod(gs, sh, st=None):
            if st is None:
                rms_stats([(acc.t[:, c, :], accr(c)) for c in range(KC)], D)
            else:
                st.finish(D)
            for c in range(KC):
                q = c % 2
                dve(lambda e, c=c, q=q: e.tensor_tensor(out=ntv[q], in0=acc.t[:, c, :], in1=rstd, op=ALU.mult),
                    reads=[accr(c), rstd_r], writes=[ntr[q]])
                act(lambda e, c=c, q=q: e.activation(out=hb.t[:, c, :], in_=ntv[q], func=AF.Identity,
                                                    bias=sh[:, c:c + 1], scale=gs[:, c:c + 1]),
                    reads=[ntr[q], mods.all(), gs1.all(), gs2.all()], writes=[hbr(c)])

        xs = [uf(i * 8192, 2048) for i in range(2)]
        xsr = [ur(i * 8192, 8192) for i in range(2)]

        def load_x(row0):
            cnt = 0
            st = Stats()
            for blk in range(4):
                for half in range(2):
                    q = cnt % 2
                    cnt += 1
                    S.dma("sp", xs[q], dr["xcat"][row0 + blk * 128: row0 + (blk + 1) * 128, half * 2048:(half + 1) * 2048],
                          writes=[xsr[q]], key="xs%d" % q)
                    for c4 in range(4):
                        b = pnext()
                        for cc in range(4):
                            pe(lambda e, b=b, q=q, c4=c4, cc=cc:
                               e.transpose(ps[b].t[:, cc * 128:(cc + 1) * 128], xs[q][:, (c4 * 4 + cc) * 128:(c4 * 4 + cc + 1) * 128], sm["identf"].t[:]),
                               reads=[xsr[q], sm["identf"].all()], writes=[ps[b].all()])
                        c0 = half * 16 + c4 * 4
                        dve(lambda e, b=b, c0=c0, blk=blk: e.tensor_copy(
                            acc.t[:, c0:c0 + 4, blk * 128:(blk + 1) * 128], ps[b].t[:].rearrange("p (c t) -> p c t", c=4)),
                            reads=[ps[b].all()], writes=[acc.r(c0 * T, (c0 + 4) * T)])
                        sq = sqc[0] % 4
                        sqc[0] += 1
                        act(lambda e, c0=c0, blk=blk, sq=sq: e.activation(out=sq4[sq].rearrange("p (c t) -> p c t", c=4),
                                                                        in_=acc.t[:, c0:c0 + 4, blk * 128:(blk + 1) * 128], func=AF.Square),
                            reads=[acc.r(c0 * T, (c0 + 4) * T)], writes=[sq4_r[sq]])
                        sview = sq4[sq].rearrange("p (c t) -> p t c", c=4)
                        s2blk = s2v[:, blk * 128:(blk + 1) * 128]
                        if half == 0 and c4 == 0:
                            dve(lambda e, sview=sview, s2blk=s2blk: e.tensor_reduce(out=s2blk, in_=sview, axis=mybir.AxisListType.X, op=ALU.add),
                                reads=[sq4_r[sq]], writes=[s2_r])
                        else:
                            tmpb = ntv[0][:, 0:128]
                            dve(lambda e, sview=sview, tmpb=tmpb: e.tensor_reduce(out=tmpb, in_=sview, axis=mybir.AxisListType.X, op=ALU.add),
                                reads=[sq4_r[sq]], writes=[ntr[0]])
                            dve(lambda e, s2blk=s2blk, tmpb=tmpb: e.tensor_tensor(out=s2blk, in0=s2blk, in1=tmpb, op=ALU.add),
                                reads=[s2_r, ntr[0]], writes=[s2_r])
            st.i = 1
            return st

        rcv = uf(X0, 512); rsv = uf(X0 + 2048, 512)
        rc_r = ur(X0, 4096)
        t4v = [uf(X0 + 4096 + i * 2048, 512) for i in range(4)]
        t4r = ur(X0 + 4096, 8192)

        def rope_epi(ba, bb, outA, outB, regA, regB):
            dve(lambda e: e.tensor_tensor(out=t4v[0], in0=ps[ba].t[:], in1=rcv, op=ALU.mult), reads=[ps[ba].all(), rc_r], writes=[t4r])
            dve(lambda e: e.tensor_tensor(out=t4v[1], in0=ps[bb].t[:], in1=rsv, op=ALU.mult), reads=[ps[bb].all(), rc_r], writes=[t4r])
            dve(lambda e: e.tensor_tensor(out=t4v[2], in0=ps[bb].t[:], in1=rcv, op=ALU.mult), reads=[ps[bb].all(), rc_r], writes=[t4r])
            dve(lambda e: e.tensor_tensor(out=t4v[3], in0=ps[ba].t[:], in1=rsv, op=ALU.mult), reads=[ps[ba].all(), rc_r], writes=[t4r])
            dve(lambda e: e.tensor_tensor(out=outA, in0=t4v[0], in1=t4v[1], op=ALU.subtract), reads=[t4r], writes=[regA])
            dve(lambda e: e.tensor_tensor(out=outB, in0=t4v[2], in1=t4v[3], op=ALU.add), reads=[t4r], writes=[regB])

        def in_proj(ti, pre, want_kv):
            col0 = ti * T
            S.dma("sp", rcv, dr["ropec"][:, col0:col0 + T], writes=[rc_r], key="rope")
            S.dma("sp", rsv, dr["ropes"][:, col0:col0 + T], writes=[rc_r], key="rope")
            rhs = lambda kc: (hb.t[:, kc, :], hbr(kc))
            if not pre:
                for slab in range(4):
                    banks = []
                    group_mm(dr["w_in"], 8, slab * 512, rhs, lambda j, b: banks.append(b))
                    for pr in range(2):
                        g = slab * 2 + pr
                        rope_epi(banks[2 * pr], banks[2 * pr + 1], Qv[:, 2 * g, :], Qv[:, 2 * g + 1, :], Qr(2 * g), Qr(2 * g + 1))
            if want_kv:
                for slab in range(2):
                    banks = []
                    group_mm(dr["w_in"], 8, 2048 + slab * 512, rhs, lambda j, b: banks.append(b))
                    for pr in range(2):
                        a = slab * 2 + pr
                        rope_epi(banks[2 * pr], banks[2 * pr + 1], Kb.t[:, 2 * a, 128:640], Kb.t[:, 2 * a + 1, 128:640],
                                 Kb.r((2 * a) * 640 + 128, (2 * a + 1) * 640), Kb.r((2 * a + 1) * 640 + 128, (2 * a + 2) * 640))
            for slab in range(4):
                def epi(j, b, slab=slab):
                    c = slab * 4 + j
                    act(lambda e: e.activation(out=ubv[:, c, :], in_=ps[b].t[:], func=AF.Identity), reads=[ps[b].all()], writes=[ubr(c)])
                group_mm(dr["w_in"], 8, 3072 + slab * 512, rhs, epi)
                if pre:
                    ada_full_slab()
            if want_kv:
                banks = [pnext() for _ in range(4)]
                for k4 in range(8):
                    s = wtile(dr["w_in"], k4 * 512, 5120, 256)
                    for a in range(4):
                        kc = k4 * 4 + a
                        for blk in range(4):
                            pe(lambda e, b=banks[blk], s=s, a=a, kc=kc, blk=blk:
                               e.matmul(ps[b].t[:, 0:256], hb.t[:, kc, blk * 128:(blk + 1) * 128], wr.t[:, s, a, 0:256],
                                        start=(kc == 0), stop=(kc == 31)),
                               reads=[wreg(s), hbr(kc)], writes=[ps[banks[blk]].all()])
                for blk in range(4):
                    b = banks[blk]
                    act(lambda e, b=b, blk=blk: e.activation(out=Va.t[:, 1 + blk, :, 0:64], in_=ps[b].t[:, 0:256].rearrange("p (h d) -> p h d", h=4), func=AF.Identity),
                        reads=[ps[b].all()], writes=[Va.r((1 + blk) * 260, (2 + blk) * 260)])

        def shift_halo():
            dve(lambda e: e.tensor_copy(Kb.t[:, :, 0:128], Kb.t[:, :, 512:640]), reads=[Kb.all()], writes=[Kb.all()])
            dve(lambda e: e.tensor_copy(Va.t[:, 0, :, :], Va.t[:, 4, :, :]), reads=[Va.all()], writes=[Va.all()])

        A0 = X0
        attok = uf(A0, 2048); attok_r = ur(A0, 8192)
        atn = ubf(A0 + 8192, 2048); atn_r = ur(A0 + 8192, 4096)
        NEB = 4
        ebv = [ubf(A0 + 12288 + i * 512, 256) for i in range(NEB)]
        ebr = [ur(A0 + 12288 + i * 512, 512) for i in range(NEB)]
        emv = [ubf(A0 + 14336 + i * 512, 256) for i in range(NEB)]
        emr = [ur(A0 + 14336 + i * 512, 512) for i in range(NEB)]
        atj = atn; atj_r = atn_r

        def attention(first_tile):
            for qb in range(4):
                msk = cb["mask_f"] if (first_tile and qb == 0) else cb["mask_n"]
                heads = [(g, j) for g in range(8) for j in range(4)]
                st = {}
                pob = {}

                def stageA(n, qb=qb):
                    g, j = heads[n]
                    a = g // 2
                    q = n % NEB
                    sb = pnext()
                    st[n] = (sb, q)
                    for kb in range(2):
                        kcol = (qb + kb) * 128
                        for ab in range(2):
                            pe(lambda e, sb=sb, kb=kb, ab=ab, kcol=kcol, g=g, j=j, a=a:
                               e.matmul(ps[sb].t[:, kb * 128:(kb + 1) * 128],
                                        Kb.t[32 * j:32 * j + 32, 2 * a + ab, kcol:kcol + 128],
                                        Qv[32 * j:32 * j + 32, 2 * g + ab, qb * 128:(qb + 1) * 128],
                                        start=(ab == 0), stop=(ab == 1), tile_position=(32 * j, 0)),
                               reads=[Kb.all(), Qr(2 * g + ab)], writes=[ps[sb].all()])
                    act(lambda e, sb=sb, q=q: e.activation(out=ebv[q], in_=ps[sb].t[:, 0:256], func=AF.Exp, scale=0.125),
                        reads=[ps[sb].all()], writes=[ebr[q]])

                def stageM(n, msk=msk):
                    q = st[n][1]
                    dve(lambda e, q=q: e.tensor_tensor(out=emv[q], in0=ebv[q], in1=msk.t[:], op=ALU.mult),
                        reads=[ebr[q], msk.all()], writes=[emr[q]])

                def stageP(n, qb=qb):
                    g, j = heads[n]
                    a = g // 2
                    q = st[n][1]
                    if j == 0:
                        pob[g] = pnext()
                    po = pob[g]
                    for kb in range(2):
                        pe(lambda e, po=po, q=q, kb=kb, j=j, a=a:
                           e.matmul(ps[po].t[:, j * 65:(j + 1) * 65], emv[q][:, kb * 128:(kb + 1) * 128],
                                    Va.t[:, qb + kb, a, :], start=(kb == 0), stop=(kb == 1)),
                           reads=[emr[q], Va.all()], writes=[ps[po].all()])
                    if j == 3:
                        pov = ps[po].t[:, 0:260].rearrange("p (h d) -> p h d", d=65)
                        SR = [sml.all()]
                        dve(lambda e, pov=pov, g=g: e.tensor_tensor(out=sml.t[:, 0:4], in0=pov[:, :, 64], in1=esink.t[:, 4 * g:4 * g + 4], op=ALU.add),
                            reads=[ps[po].all(), esink.all()], writes=SR)
                        dve(lambda e: e.reciprocal(sml.t[:, 4:8], sml.t[:, 0:4]), reads=SR, writes=SR)
                        dve(lambda e, pov=pov, g=g: e.tensor_tensor(
                            out=attok[:, g * 256:(g + 1) * 256].rearrange("p (h d) -> p h d", d=64), in0=pov[:, :, 0:64],
                            in1=sml.t[:, 4:8].unsqueeze(2).to_broadcast([128, 4, 64]), op=ALU.mult),
                            reads=[ps[po].all()] + SR, writes=[attok_r])

                for n in range(32 + 3):
                    if n < 32:
                        stageA(n)
                    if 0 <= n - 2 < 32:
                        stageM(n - 2)
                    if 0 <= n - 3 < 32:
                        stageP(n - 3)
                SR = [sml.all()]
                act(lambda e: e.activation(out=atj, in_=attok, func=AF.Square, accum_out=sml.t[:, 8:9]), reads=[attok_r], writes=[atj_r] + SR)
                dve(lambda e: e.tensor_scalar(sml.t[:, 9:10], sml.t[:, 8:9], 1.0 / 2048, EPS, op0=ALU.mult, op1=ALU.add), reads=SR, writes=SR)
                act(lambda e: e.activation(out=sml.t[:, 9:10], in_=sml.t[:, 9:10], func=AF.Sqrt), reads=SR, writes=SR)
                dve(lambda e: e.reciprocal(sml.t[:, 10:11], sml.t[:, 9:10]), reads=SR, writes=SR)
                dve(lambda e: e.tensor_scalar(atn, attok, sml.t[:, 10:11], None, op0=ALU.mult), reads=[attok_r] + SR, writes=[atn_r])
                for c4 in range(4):
                    b = pnext()
                    pb = ps[b].t[:].bitcast(BF16)
                    for cc in range(4):
                        c = c4 * 4 + cc
                        pe(lambda e, pb=pb, cc=cc, c=c: e.transpose(pb[:, cc * 128:(cc + 1) * 128], atn[:, c * 128:(c + 1) * 128], cb["identb"].t[:]),
                           reads=[atn_r, cb["identb"].all()], writes=[ps[b].all()])
                    for cc in range(4):
                        c = c4 * 4 + cc
                        act(lambda e, pb=pb, cc=cc, c=c, qb=qb: e.activation(out=hb.t[:, c, qb * 128:(qb + 1) * 128], in_=pb[:, cc * 128:(cc + 1) * 128],
                                                                   func=AF.Identity, scale=sm["attn_gT"].t[:, c:c + 1]),
                            reads=[ps[b].all(), sm["attn_gT"].all()], writes=[hb.r(c * T + qb * 128, c * T + (qb + 1) * 128)])

        S0 = X0
        tEs = [uf(S0 + i * 4096, 1024).rearrange("p (c t) -> p c t", c=2) for i in range(2)]
        tErs = [ur(S0 + i * 4096, 4096) for i in range(2)]
        stmp = [uf(S0 + 8192 + i * 2048, 512) for i in range(2)]; stmp_r = [ur(S0 + 8192 + i * 2048, 2048) for i in range(2)]
        mmv = [uf(S0 + 12288 + i * 2048, 512) for i in range(2)]; mm_r = [ur(S0 + 12288 + i * 2048, 2048) for i in range(2)]
        wwv = [uf(S0 + 16384 + i * 2048, 512) for i in range(2)]; ww_r = [ur(S0 + 16384 + i * 2048, 2048) for i in range(2)]
        zzv = [Kb.t[:, a_, 128:640] for a_ in range(4)]; zz_r = [Kb.r(a_ * 640 + 128, (a_ + 1) * 640) for a_ in range(4)]
        nwv = Kb.t[:, 4, 128:640]; nw_r = Kb.r(4 * 640 + 128, 5 * 640)

        def ssm():
            CR = [car.all(), wini.all(), ctmp.all(), cos1.all(), sin1.all()]
            cre, cim = car.t[:, 0, :], car.t[:, 1, :]
            dve(lambda e: e.tensor_tensor(out=ctmp.t[:, 0, :], in0=cos1.t[:], in1=cre, op=ALU.mult), reads=CR, writes=CR)
            dve(lambda e: e.tensor_tensor(out=ctmp.t[:, 1, :], in0=sin1.t[:], in1=cim, op=ALU.mult), reads=CR, writes=CR)
            dve(lambda e: e.tensor_tensor(out=wini.t[:, 0, :], in0=ctmp.t[:, 0, :], in1=ctmp.t[:, 1, :], op=ALU.subtract), reads=CR, writes=CR)
            dve(lambda e: e.tensor_tensor(out=ctmp.t[:, 2, :], in0=sin1.t[:], in1=cre, op=ALU.mult), reads=CR, writes=CR)
            dve(lambda e: e.tensor_tensor(out=ctmp.t[:, 3, :], in0=cos1.t[:], in1=cim, op=ALU.mult), reads=CR, writes=CR)
            dve(lambda e: e.tensor_tensor(out=wini.t[:, 1, :], in0=ctmp.t[:, 2, :], in1=ctmp.t[:, 3, :], op=ALU.add), reads=CR, writes=CR)
            bbank = [0, 1, 2, 3]
            ybank = [4, 5]

            def bk(n):
                return bbank[(2 * n) % 4], bbank[(2 * n + 1) % 4]

            def stL_dma(n):
                S.dma("sp", tEs[n % 2], tabs[n].rearrange("c p t -> p c t"), reads=[("tabs", n, n + 1)], writes=[tErs[n % 2]], key="tabr%d" % (n % 2))

            def stL_pe(n):
                c_, j_ = n // 4, n % 4
                for ri, b in zip((0, 1), bk(n)):
                    pe(lambda e, ri=ri, b=b, c_=c_, j_=j_: e.matmul(ps[b].t[:], Bexp.t[32 * j_:32 * j_ + 32, c_, ri, :],
                                                                  ubv[32 * j_:32 * j_ + 32, c_, :], start=True, stop=True,
                                                                  tile_position=(32 * j_, 0)),
                       reads=[Bexp.all(), ubr(c_)], writes=[ps[b].all()])

            def stM(n):
                Ec, Es = tEs[n % 2][:, 0, :], tEs[n % 2][:, 1, :]
                tr = tErs[n % 2]
                bre, bim = bk(n)
                dve(lambda e: e.tensor_tensor(out=mmv[0], in0=ps[bre].t[:], in1=Ec, op=ALU.mult), reads=[ps[bre].all(), tr], writes=[mm_r[0]])
                dve(lambda e: e.tensor_tensor(out=stmp[0], in0=ps[bim].t[:], in1=Es, op=ALU.mult), reads=[ps[bim].all(), tr], writes=[stmp_r[0]])
                dve(lambda e: e.tensor_tensor(out=mmv[1], in0=ps[bim].t[:], in1=Ec, op=ALU.mult), reads=[ps[bim].all(), tr], writes=[mm_r[1]])
                dve(lambda e: e.tensor_tensor(out=stmp[1], in0=ps[bre].t[:], in1=Es, op=ALU.mult), reads=[ps[bre].all(), tr], writes=[stmp_r[1]])

            def stA(n):
                S.op("pool", lambda e: e.tensor_tensor(out=mmv[0], in0=mmv[0], in1=stmp[0], op=ALU.subtract), [mm_r[0], stmp_r[0]], [mm_r[0]])
                S.op("pool", lambda e: e.tensor_tensor(out=mmv[1], in0=mmv[1], in1=stmp[1], op=ALU.add), [mm_r[1], stmp_r[1]], [mm_r[1]])

            def stS(n):
                for ri in range(2):
                    dve(lambda e, ri=ri, p=n: e.tensor_tensor_scan(out=wwv[ri], data0=rr.t[:, p:p + 1].to_broadcast([128, 512]), data1=mmv[ri],
                                                                  initial=wini.t[:, ri, p:p + 1], op0=ALU.mult, op1=ALU.add),
                        reads=[mm_r[ri], rr.all(), wini.all()], writes=[ww_r[ri]])
                for ri in range(2):
                    act(lambda e, ri=ri, p=n: e.activation(out=wlast.t[:, ri, p:p + 1], in_=wwv[ri][:, 511:512], func=AF.Identity),
                        reads=[ww_r[ri]], writes=[wlast.all()])
                act(lambda e: e.activation(out=nwv, in_=wwv[1], func=AF.Identity, scale=-1.0), reads=[ww_r[1]], writes=[nw_r])

            def stZ(n):
                c_, j_ = n // 4, n % 4
                Ec, Es = tEs[n % 2][:, 0, :], tEs[n % 2][:, 1, :]
                tr = tErs[n % 2]
                dve(lambda e: e.tensor_tensor(out=zzv[0], in0=wwv[0], in1=Ec, op=ALU.mult), reads=[ww_r[0], tr], writes=[zz_r[0]])
                dve(lambda e: e.tensor_tensor(out=zzv[1], in0=wwv[1], in1=Es, op=ALU.mult), reads=[ww_r[1], tr], writes=[zz_r[1]])
                dve(lambda e: e.tensor_tensor(out=zzv[2], in0=wwv[0], in1=Es, op=ALU.mult), reads=[ww_r[0], tr], writes=[zz_r[2]])
                dve(lambda e: e.tensor_tensor(out=zzv[3], in0=nwv, in1=Ec, op=ALU.mult), reads=[nw_r, tr], writes=[zz_r[3]])
                yb = ybank[n % 2]
                for zi, cm in ((0, "cre_e"), (1, "cre_e"), (2, "cim_e"), (3, "cim_e")):
                    pe(lambda e, zi=zi, cm=cm, yb=yb, p=n: e.matmul(ps[yb].t[0:32, :], cb[cm].t[:, p, :], zzv[zi], start=(zi == 0), stop=False),
                       reads=[cb[cm].all(), zz_r[zi]], writes=[ps[yb].all()])
                pe(lambda e, yb=yb, c_=c_, j_=j_: e.matmul(ps[yb].t[0:32, :], cb["d_e"].t[32 * j_:32 * j_ + 32, c_, :], ubv[32 * j_:32 * j_ + 32, c_, :],
                                                           start=False, stop=True, tile_position=(32 * j_, 0)),
                   reads=[cb["d_e"].all(), ubr(c_)], writes=[ps[yb].all()])
                act(lambda e, yb=yb, c_=c_, j_=j_: e.activation(out=gbv[32 * j_:32 * j_ + 32, c_, :], in_=ps[yb].t[0:32, :], func=AF.Gelu),
                    reads=[ps[yb].all()], writes=[Qr(c_)])

            stL_dma(0)
            stL_pe(0)
            for n in range(64):
                if n + 1 < 64:
                    stL_pe(n + 1)
                stM(n)
                stA(n)
                if n >= 1:
                    stZ(n - 1)
                if n + 1 < 64:
                    stL_dma(n + 1)
                stS(n)
            stZ(63)
            FR = [car.all(), ctmp.all(), wlast.all(), c511.all(), s511.all()]
            wl_re, wl_im = wlast.t[:, 0, :], wlast.t[:, 1, :]
            TT = lambda o, a, b, op: dve(lambda e: e.tensor_tensor(out=o, in0=a, in1=b, op=op), reads=FR, writes=FR)
            TT(ctmp.t[:, 0, :], c511.t[:], wl_re, ALU.mult)
            TT(ctmp.t[:, 1, :], s511.t[:], wl_im, ALU.mult)
            TT(cre, ctmp.t[:, 0, :], ctmp.t[:, 1, :], ALU.subtract)
            TT(ctmp.t[:, 2, :], s511.t[:], wl_re, ALU.mult)
            TT(ctmp.t[:, 3, :], c511.t[:], wl_im, ALU.mult)
            TT(cim, ctmp.t[:, 2, :], ctmp.t[:, 3, :], ALU.add)

        tWs = [uf(X0 + i * 4096, 1024).rearrange("p (c t) -> p c t", c=2) for i in range(2)]
        tWrs = [ur(X0 + i * 4096, 4096) for i in range(2)]
        jkv = [uf(i * 2048, 512) for i in range(4)]; jk_r = [ur(i * 2048, 2048) for i in range(4)]

        ada_slab = [16]

        def ada_full_slab():
            if ada_slab[0] < 48:
                sl = ada_slab[0]
                ada_slab[0] += 1
                for _ in ada_all([sl], [pnext() for _ in range(4)]):
                    pass

        def ssm_pre():
            sl0 = ada_slab[0]
            ada_slab[0] = min(48, sl0 + 4)
            ada_it = ada_all(range(sl0, ada_slab[0]), [3, 4, 5, 6])
            bl = [0, 1, 2, 7]
            for pi_ in range(64):
                c_, j_ = pi_ // 4, pi_ % 4
                tWv, tW_r = tWs[pi_ % 2], tWrs[pi_ % 2]
                S.dma("sp", tWv, wtabs[pi_].rearrange("c p t -> p c t"), reads=[("wtabs", pi_, pi_ + 1)], writes=[tW_r], key="tabr%d" % (pi_ % 2))
                bre, bim = bl[(2 * pi_) % 4], bl[(2 * pi_ + 1) % 4]
                for ri, b in ((0, bre), (1, bim)):
                    pe(lambda e, ri=ri, b=b, c_=c_, j_=j_: e.matmul(ps[b].t[:], Bexp.t[32 * j_:32 * j_ + 32, c_, ri, :],
                                                                  ubv[32 * j_:32 * j_ + 32, c_, :], start=True, stop=True,
                                                                  tile_position=(32 * j_, 0)),
                       reads=[Bexp.all(), ubr(c_)], writes=[ps[b].all()])
                for k, (bank, wi) in enumerate(((bre, 0), (bim, 1), (bim, 0), (bre, 1))):
                    dve(lambda e, k=k, bank=bank, wi=wi, tWv=tWv: e.tensor_tensor(out=jkv[k], in0=ps[bank].t[:], in1=tWv[:, wi, :], op=ALU.mult),
                        reads=[ps[bank].all(), tW_r], writes=[jk_r[k]])
                    act(lambda e, k=k, p=pi_: e.activation(out=jkv[k], in_=jkv[k], func=AF.Identity, accum_out=accS.t[:, k, p:p + 1]),
                        reads=[jk_r[k]], writes=[jk_r[k], accS.all()])
                if pi_ % 2 == 1:
                    next(ada_it, None)
            for _ in ada_it:
                pass
            CR = [car.all(), ctmp.all(), accS.all(), Gre.all(), Gim.all()]
            cre, cim = car.t[:, 0, :], car.t[:, 1, :]
            c0, c1, c2, c3 = [ctmp.t[:, i, :] for i in range(4)]
            TT = lambda o, a, b, op: dve(lambda e: e.tensor_tensor(out=o, in0=a, in1=b, op=op), reads=CR, writes=CR)
            TT(c0, accS.t[:, 0, :], accS.t[:, 1, :], ALU.subtract)
            TT(c1, accS.t[:, 2, :], accS.t[:, 3, :], ALU.add)
            TT(c2, Gre.t[:], cre, ALU.mult)
            TT(c0, c0, c2, ALU.add)
            TT(c2, Gim.t[:], cim, ALU.mult)
            TT(c0, c0, c2, ALU.subtract)
            TT(c3, Gre.t[:], cim, ALU.mult)
            TT(c1, c1, c3, ALU.add)
            TT(c3, Gim.t[:], cre, ALU.mult)
            TT(c1, c1, c3, ALU.add)
            dve(lambda e: e.tensor_copy(cre, c0), reads=CR, writes=CR)
            dve(lambda e: e.tensor_copy(cim, c1), reads=CR, writes=CR)

        G0 = X0
        gatev = [uf(G0 + 12288 + i * 2048, 512) for i in range(2)]; gate_r = [ur(G0 + 12288 + i * 2048, 2048) for i in range(2)]
        prodv = [uf(G0 + 16384 + i * 2048, 512) for i in range(2)]; prod_r = [ur(G0 + 16384 + i * 2048, 2048) for i in range(2)]

        def glu_and_norm():
            rhs = lambda kc: (gbv[:, kc, :], Qr(kc))
            cnt = [0]
            for slab in range(4):
                def epi(j, b, slab=slab):
                    ob = slab * 4 + j
                    q = cnt[0] % 2
                    i = cnt[0]
                    cnt[0] += 1
                    act(lambda e: e.activation(out=gatev[q], in_=ps[b].t[:], func=AF.Sigmoid, bias=sm["b_gluT"].t[:, ob:ob + 1]),
                        reads=[ps[b].all(), sm["b_gluT"].all()], writes=[gate_r[q]])
                    dve(lambda e: e.tensor_tensor(out=prodv[q], in0=gbv[:, ob, :], in1=gatev[q], op=ALU.mult),
                        reads=[Qr(ob), gate_r[q]], writes=[prod_r[q]])
                    act(lambda e: e.activation(out=sqv[q], in_=prodv[q], func=AF.Square), reads=[prod_r[q]], writes=[sqr[q]])
                    pe(lambda e: e.matmul(ps[SSB].t[:], ones.t[:], sqv[q], start=(i == 0), stop=(i == 15)),
                       reads=[ones.all(), sqr[q]], writes=[ps[SSB].all()])
                    dve(lambda e: e.tensor_copy(hb.t[:, 16 + ob, :], prodv[q]), reads=[prod_r[q]], writes=[hbr(16 + ob)])
                group_mm(dr["w_glu"], 4, slab * 512, rhs, epi)
            finish_stats(2048)
            for ob in range(16):
                dve(lambda e, ob=ob: e.scalar_tensor_tensor(out=hb.t[:, 16 + ob, :], in0=hb.t[:, 16 + ob, :], scalar=sm["ssm_gT"].t[:, ob:ob + 1],
                                                         in1=rstd, op0=ALU.mult, op1=ALU.mult),
                    reads=[hbr(16 + ob), rstd_r, sm["ssm_gT"].all()], writes=[hbr(16 + ob)])

        def resid_epi(gate, st):
            def epi_factory(slab):
                def epi(j, b):
                    ob = slab * 4 + j
                    dve(lambda e: e.scalar_tensor_tensor(out=acc.t[:, ob, :], in0=ps[b].t[:], scalar=gate[:, ob:ob + 1], in1=acc.t[:, ob, :],
                                                         op0=ALU.mult, op1=ALU.add),
                        reads=[ps[b].all(), mods.all(), accr(ob)], writes=[accr(ob)])
                    st.add(acc.t[:, ob, :], accr(ob))
                return epi
            return epi_factory

        def out_proj():
            rhs = lambda kc: (hb.t[:, kc, :], hbr(kc))
            st = Stats()
            ef = resid_epi(gt1, st)
            for slab in range(8):
                group_mm(dr["w_out"], 8, slab * 512, rhs, ef(slab))
            return st

        hidv = [ubf(i * 4096, 2048).rearrange("p (a t) -> p a t", a=4) for i in range(2)]
        hid_r = [ur(i * 4096, 4096) for i in range(2)]
        rlv = [uf(8192 + i * 2048, 512) for i in range(2)]; rl_r = [ur(8192 + i * 2048, 2048) for i in range(2)]

        def ffn():
            rhs = lambda kc: (hb.t[:, kc, :], hbr(kc))
            cnt = [0]
            st = Stats()
            for f in range(32):
                hq = f % 2

                def epi(j, b, hq=hq):
                    q = cnt[0] % 2
                    cnt[0] += 1
                    act(lambda e: e.activation(out=rlv[q], in_=ps[b].t[:], func=AF.Relu), reads=[ps[b].all()], writes=[rl_r[q]])
                    dve(lambda e: e.scalar_tensor_tensor(out=hidv[hq][:, j, :], in0=ps[b].t[:], scalar=0.0, in1=rlv[q], op0=ALU.max, op1=ALU.mult),
                        reads=[ps[b].all(), rl_r[q]], writes=[ur(hq * 4096 + j * 1024, 1024)])
                group_mm(dr["w_ff1"], 8, f * 512, rhs, epi)
                for slab in range(8):
                    s = wtile(dr["w_ff2"], f * 512, slab * 512)
                    for j in range(4):
                        ob = slab * 4 + j
                        b = pnext()
                        for a in range(4):
                            pe(lambda e, b=b, s=s, a=a, j=j, hq=hq: e.matmul(ps[b].t[:], wr.t[:, s, a, j * 128:(j + 1) * 128], hidv[hq][:, a, :],
                                                                          start=(a == 0), stop=(a == 3)),
                               reads=[wreg(s), hid_r[hq]], writes=[ps[b].all()])
                        dve(lambda e, b=b, ob=ob: e.scalar_tensor_tensor(out=acc.t[:, ob, :], in0=ps[b].t[:], scalar=gt2[:, ob:ob + 1], in1=acc.t[:, ob, :],
                                                                     op0=ALU.mult, op1=ALU.add),
                            reads=[ps[b].all(), mods.all(), accr(ob)], writes=[accr(ob)])
                        if f == 31:
                            st.add(acc.t[:, ob, :], accr(ob))
            return st

        def final_out(row0, st):
            st.finish(D)
            for c in range(KC):
                dve(lambda e, c=c: e.scalar_tensor_tensor(out=acc.t[:, c, :], in0=acc.t[:, c, :], scalar=sm["fing"].t[:, c:c + 1], in1=rstd,
                                                       op0=ALU.mult, op1=ALU.mult),
                    reads=[accr(c), rstd_r, sm["fing"].all()], writes=[accr(c)])
            cnt = 0
            for blk in range(4):
                for half in range(2):
                    q = cnt % 2
                    cnt += 1
                    for c4 in range(4):
                        b = pnext()
                        for cc in range(4):
                            c = half * 16 + c4 * 4 + cc
                            pe(lambda e, b=b, cc=cc, c=c, blk=blk: e.transpose(ps[b].t[:, cc * 128:(cc + 1) * 128], acc.t[:, c, blk * 128:(blk + 1) * 128], sm["identf"].t[:]),
                               reads=[accr(c), sm["identf"].all()], writes=[ps[b].all()])
                        if c4 % 2 == 0:
                            dve(lambda e, b=b, q=q, c4=c4: e.tensor_copy(xs[q][:, c4 * 512:(c4 + 1) * 512], ps[b].t[:]), reads=[ps[b].all()],
                                writes=[ur(q * 8192 + c4 * 2048, 2048)])
                        else:
                            act(lambda e, b=b, q=q, c4=c4: e.activation(out=xs[q][:, c4 * 512:(c4 + 1) * 512], in_=ps[b].t[:], func=AF.Identity), reads=[ps[b].all()],
                                writes=[ur(q * 8192 + c4 * 2048, 2048)])
                    S.dma("sp", out_d[row0 + blk * 128: row0 + (blk + 1) * 128, half * 2048:(half + 1) * 2048], xs[q], reads=[xsr[q]], key="os%d" % q)

        nt = NTILE if not debug else debug.get("_ntile", [NTILE])[0]
        npre = NPRE if not debug else debug.get("_npre", [NPRE])[0]
        e_cnt = [0]
        for ti in range(NPRE - npre, NPRE):
            st1 = load_x(ti * T)
            norm_mod(gs1.t, sh1, st1)
            want_kv = (ti == NPRE - 1)
            in_proj(ti, True, want_kv)
            if want_kv:
                shift_halo()
            ssm_pre()
        while ada_slab[0] < 48:
            ada_full_slab()
        dve(lambda e: e.scalar_tensor_tensor(out=gs2.t[:], in0=sc2, scalar=1.0, in1=sm["n2g"].t[:], op0=ALU.add, op1=ALU.mult),
            reads=[mods.all(), sm["n2g"].all()], writes=[gs2.all()])
        dump("mods", mods.t[:], [mods.all()])
        dve(lambda e: e.tensor_scalar(car.t[:], car.t[:], sm["flag"].t[:, 0:1], None, op0=ALU.mult), reads=[car.all(), sm["flag"].all()], writes=[car.all()])
        for i in range(nt):
            ti = NPRE + i
            st1 = load_x(ti * T)
            norm_mod(gs1.t, sh1, st1)
            if i == 0:
                dump("x0", acc.t[:, 0:2, :], [acc.all()])
                dump("h1", hb.t[:, 0:2, :], [hb.all()])
                dump("Bexp", Bexp.t[:, 0, :, :], [Bexp.all()])
                dump("rr", rr.t[:], [rr.all()])
                dump("phi", phi.t[:], [phi.all()])
            in_proj(ti, False, True)
            if i == 0:
                dump("q0", Qv[:, 0:2, :], [Qr(0), Qr(1)])
                dump("k0", Kb.t[:, 0:2, :], [Kb.all()])
                dump("ub0", ubv[:, 0:2, :], [ubr(0), ubr(1)])
                dump("va", Va.t[:, 1, :, :], [Va.all()])
            attention(i == 0)
            if i == 0:
                dump("att", hb.t[:, 0:2, :], [hb.all()])
            shift_halo()
            ssm()
            if i == 0:
                dump("gb0", gbv[:, 0:2, :], [Qr(0), Qr(1)])
                dump("car", car.t[:], [car.all()])
            glu_and_norm()
            if i == 0:
                dump("ssm", hb.t[:, 16:18, :], [hb.all()])
            st2 = out_proj()
            if i == 0:
                dump("x1", acc.t[:, 0:2, :], [acc.all()])
            norm_mod(gs2.t, sh2, st2)
            if i == 0:
                dump("h2", hb.t[:, 0:2, :], [hb.all()])
            st3 = ffn()
            if i == 0:
                dump("x2", acc.t[:, 0:2, :], [acc.all()])
            final_out(i * T, st3)
        S.emit()
        build.stats = S.stats
    return nc


def _prep_shared(inp):
    f = np.float32
    sh = {}
    idx = []
    for g in range(8):
        for half in range(2):
            for j in range(4):
                h = 4 * g + j
                idx.extend(range(h * 64 + half * 32, h * 64 + half * 32 + 32))
    for a in range(4):
        for half in range(2):
            for j in range(4):
                idx.extend(range(2048 + a * 64 + half * 32, 2048 + a * 64 + half * 32 + 32))
    idx.extend(range(2560, 4608))
    idx.extend(range(2304, 2560))
    idx = np.asarray(idx)
    assert idx.size == WINP
    sh["w_in"] = np.ascontiguousarray(inp["w_in"][0][:, idx])
    sh["w_ada"] = np.ascontiguousarray(inp["w_ada"][0])
    sh["w_glu"] = np.ascontiguousarray(inp["w_glu"][0])
    sh["w_out"] = np.ascontiguousarray(inp["w_out"][0])
    sh["w_ff1"] = np.ascontiguousarray(inp["w_ff1"][0])
    sh["w_ff2"] = np.ascontiguousarray(inp["w_ff2"][0])
    col = lambda v: np.ascontiguousarray(np.asarray(v, f).reshape(-1, 128).T)
    sh["b_adaT"] = col(inp["b_ada"][0])
    sh["n1g"] = col(inp["norm1_g"][0]); sh["n2g"] = col(inp["norm2_g"][0]); sh["fing"] = col(inp["final_g"])
    sh["sinks_bc"] = np.ascontiguousarray(np.broadcast_to(np.asarray(inp["sinks"][0], f)[None, :], (128, 32)))
    sh["b_gluT"] = col(inp["b_glu"][0]); sh["ssm_gT"] = col(inp["ssm_out_g"][0]); sh["attn_gT"] = col(inp["attn_out_g"][0])
    sh["identf"] = np.eye(128, dtype=f); sh["identb"] = np.eye(128, dtype=f)
    sh["iota"] = np.ascontiguousarray(np.broadcast_to(np.arange(512, dtype=f)[None, :], (128, 512)))
    lre, lim, lst = [np.asarray(inp[k][0], f) for k in ("ssm_lam_re", "ssm_lam_im", "ssm_log_step")]
    bre, bim = np.asarray(inp["ssm_b_re"][0], f), np.asarray(inp["ssm_b_im"][0], f)
    cre, cim = np.asarray(inp["ssm_c_re"][0], f), np.asarray(inp["ssm_c_im"][0], f)
    dsk = np.asarray(inp["ssm_d"][0], f)
    q = np.arange(128)
    pi_ = np.arange(64)
    Gp = 2 * pi_[None, :] + (q // 64)[:, None]
    Pp = np.broadcast_to((q % 64)[:, None], (128, 64))
    sh["lamre_p"] = np.ascontiguousarray(lre[Gp, Pp]); sh["lamim_p"] = np.ascontiguousarray(lim[Gp, Pp])
    sh["lstep_p"] = np.ascontiguousarray(lst[Gp])
    r = np.arange(128)
    cprime = np.arange(16)
    colx = np.arange(128)
    G = (8 * cprime[None, :] + 2 * (r // 32)[:, None] + ((r % 32) // 16)[:, None])[:, :, None]
    P = (colx % 64)[None, None, :]
    H = (r % 16)[:, None, None]
    same = ((colx // 64)[None, None, :] == ((r % 32) // 16)[:, None, None])
    Gb = np.broadcast_to(G, (128, 16, 128)); Pb = np.broadcast_to(P, (128, 16, 128)); Hb = np.broadcast_to(H, (128, 16, 128))
    sh["lamre_e"] = np.ascontiguousarray(lre[Gb, Pb]).reshape(128, 2048)
    sh["lamim_e"] = np.ascontiguousarray(lim[Gb, Pb]).reshape(128, 2048)
    sh["lstep_e"] = np.ascontiguousarray(lst[Gb]).reshape(128, 2048)
    sh["bre_e"] = np.where(same, bre[Gb, Pb, Hb], f(0)).astype(f).reshape(128, 2048)
    sh["bim_e"] = np.where(same, bim[Gb, Pb, Hb], f(0)).astype(f).reshape(128, 2048)
    c32 = np.arange(32)
    Gc = np.broadcast_to((2 * pi_[None, :, None] + (q // 64)[:, None, None]), (128, 64, 32))
    Hc = np.broadcast_to((c32 % 16)[None, None, :], (128, 64, 32))
    Pc = np.broadcast_to((q % 64)[:, None, None], (128, 64, 32))
    samec = ((c32 // 16)[None, None, :] == (q // 64)[:, None, None])
    sh["cre_e"] = np.where(samec, cre[Gc, Hc, Pc], f(0)).astype(f)
    sh["cim_e"] = np.where(samec, cim[Gc, Hc, Pc], f(0)).astype(f)
    ch = (8 * cprime[None, :] + 2 * (r // 32)[:, None]) * 16 + (r % 32)[:, None]
    dd = np.where((c32[None, None, :] == (r % 32)[:, None, None]), dsk[ch][:, :, None], f(0)).astype(f)
    sh["d_e"] = np.ascontiguousarray(dd)
    j = np.arange(128)[:, None]; i = np.arange(128)[None, :]
    mprev = (j > i).astype(f); mcur = (j <= i).astype(f)
    sh["mask_n"] = np.ascontiguousarray(np.concatenate([mprev, mcur], axis=1))
    return sh, (mprev, mcur)


def _rope_tables(base):
    f = np.float32
    half = 32
    inv_freq = (f(10000.0) ** (-(np.arange(half, dtype=f) / f(half)))).astype(f)
    pos = (np.arange(4096, dtype=np.int64) + base).astype(f)
    ang = (pos[None, :] * inv_freq[np.arange(128) % 32][:, None]).astype(f)
    return np.cos(ang).astype(f), np.sin(ang).astype(f)


_NC_CACHE = {}


def kernel(**inputs):
    inp = {k: np.asarray(v) for k, v in inputs.items()}
    f = np.float32
    sh, (mprev, mcur) = _prep_shared(inp)
    x = np.asarray(inp["x"], f)
    c = np.asarray(inp["c"], f)
    in_maps = []
    for core in range(8):
        b, half = core // 2, core % 2
        m = dict(sh)
        if half == 0:
            xcat = np.concatenate([np.zeros((2048, D), f), x[b, 0:2048]], axis=0)
        else:
            xcat = x[b]
        m["xcat"] = np.ascontiguousarray(xcat)
        m["cT"] = np.ascontiguousarray(c[b].reshape(32, 128).T)
        m["flag"] = np.full((128, 1), float(half), f)
        m["mask_f"] = np.ascontiguousarray(np.concatenate([mprev if half == 1 else np.zeros_like(mprev), mcur], axis=1))
        rc, rs = _rope_tables(half * 2048 - 2048)
        m["ropec"], m["ropes"] = rc, rs
        in_maps.append(m)
    if "nc" not in _NC_CACHE:
        _NC_CACHE["nc"] = build()
    nc = _NC_CACHE["nc"]
    res = run_bass_kernel_spmd(nc, in_maps, core_ids=list(range(8)))
    out = np.empty((4, 4096, D), f)
    for core in range(8):
        b, half = core // 2, core % 2
        out[b, half * 2048:(half + 1) * 2048] = res.results[core]["out"]
    return out
```

```python
import contextlib
import math
import numpy as np
import concourse.bass as bass
import concourse.mybir as mybir
from concourse.bass_utils import run_bass_kernel_spmd

F32 = mybir.dt.float32
BF16 = mybir.dt.bfloat16
AF = mybir.ActivationFunctionType
ALU = mybir.AluOpType

D = 4096
KC = 32
T = 512
NTILE = 4
NPRE = 4
DFF = 16384
EPS = 1e-6
WINP = 5376
NW = 4
MAGIC = 12582912.0
TWO_PI = 2.0 * math.pi


class _Op:
    __slots__ = ("eng", "fn", "deps", "is_dma", "sem", "semval", "needs_inc", "incval")

    def __init__(self, eng, fn):
        self.eng = eng
        self.fn = fn
        self.deps = []
        self.is_dma = False
        self.sem = None
        self.semval = 0
        self.needs_inc = False
        self.incval = 0


class Sched:
    ENGS = ("pe", "act", "dve", "pool", "sp")

    def __init__(self, nc, es):
        self.nc = nc
        self.es = es
        self.ops = []
        self.W = {}
        self.R = {}
        self.dsems = {}
        self.dcount = {}

    def _deps(self, op, reads, writes):
        deps = op.deps
        for (k, lo, hi) in reads:
            for (l2, h2, o2) in self.W.get(k, ()):
                if l2 < hi and lo < h2:
                    deps.append(o2)
            self.R.setdefault(k, []).append((lo, hi, op))
        for (k, lo, hi) in writes:
            wl = self.W.get(k, [])
            rl = self.R.get(k, [])
            for (l2, h2, o2) in wl:
                if l2 < hi and lo < h2:
                    deps.append(o2)
            for (l2, h2, o2) in rl:
                if l2 < hi and lo < h2 and o2 is not op:
                    deps.append(o2)
            self.W[k] = [t for t in wl if not (lo <= t[0] and t[1] <= hi)] + [(lo, hi, op)]
            self.R[k] = [t for t in rl if not (lo <= t[0] and t[1] <= hi) or t[2] is op]

    def op(self, eng, fn, reads=(), writes=()):
        o = _Op(eng, fn)
        self._deps(o, reads, writes)
        self.ops.append(o)
        return o

    def dma(self, eng, out, in_, reads=(), writes=(), key=None):
        o = _Op(eng, None)
        o.is_dma = True
        if key not in self.dsems:
            self.dsems[key] = self.es.enter_context(self.nc.semaphore("d_" + key))
            self.dcount[key] = 0
        self.dcount[key] += 16
        o.sem = self.dsems[key]
        o.semval = self.dcount[key]
        o.fn = lambda e, out=out, in_=in_: e.dma_start(out=out, in_=in_)
        self._deps(o, reads, writes)
        self.ops.append(o)
        return o

    def emit(self):
        nc = self.nc
        esem = {e: self.es.enter_context(nc.semaphore("e_" + e)) for e in self.ENGS}
        fin = _Op("sp", None)
        last = {}
        for o in self.ops:
            if o.is_dma:
                last[("d", id(o.sem))] = o
            else:
                last[("e", o.eng)] = o
        fin.deps = list(last.values())
        self.ops.append(fin)
        idx = {id(o): i for i, o in enumerate(self.ops)}
        for o in self.ops:
            best = {}
            for d in o.deps:
                k = ("d", id(d.sem)) if d.is_dma else ("e", d.eng)
                if k not in best or idx[id(d)] > idx[id(best[k])]:
                    best[k] = d
            o.deps = list(best.values())
        for o in self.ops:
            for d in o.deps:
                if not d.is_dma:
                    if d.eng == "pe" and o.eng == "pe" and not o.is_dma:
                        continue
                    d.needs_inc = True
        cnt = {e: 0 for e in self.ENGS}
        for o in self.ops:
            if (not o.is_dma) and o.needs_inc:
                cnt[o.eng] += 1
                o.incval = cnt[o.eng]
        per = {e: [] for e in self.ENGS}
        for o in self.ops:
            per[o.eng].append(o)
        self.stats = {e: len(per[e]) for e in self.ENGS}
        self.stats["incs"] = dict(cnt)

        def run(engname, e):
            waited = {}
            for o in per[engname]:
                for d in o.deps:
                    if d.is_dma:
                        s, v = d.sem, d.semval
                    else:
                        if d.eng == "pe" and engname == "pe" and not o.is_dma:
                            continue
                        s, v = esem[d.eng], d.incval
                    kk = id(s)
                    if waited.get(kk, 0) >= v:
                        continue
                    waited[kk] = v
                    e.wait_ge(s, v)
                if o.fn is None:
                    continue
                ins = o.fn(e)
                if o.is_dma:
                    ins.then_inc(o.sem, 16)
                elif o.needs_inc:
                    ins.then_inc(esem[engname], 1)

        with nc.Block() as block:
            @block.tensor
            def _(e):
                run("pe", e)

            @block.scalar
            def _(e):
                run("act", e)

            @block.vector
            def _(e):
                run("dve", e)

            @block.gpsimd
            def _(e):
                run("pool", e)

            @block.sync
            def _(e):
                run("sp", e)


class Buf:
    def __init__(self, nc, es, name, shape, dtype, psum=False):
        self.name = name
        if psum:
            self.t = es.enter_context(nc.psum_tensor("p_" + name, shape, dtype))
        else:
            self.t = es.enter_context(nc.sbuf_tensor("s_" + name, shape, dtype))
        n = 1
        for s in shape[1:]:
            n *= s
        self.n = n

    def all(self):
        return (self.name, 0, self.n)

    def r(self, lo, hi):
        return (self.name, lo, hi)


SMALL_IN = [
    ("cT", [128, 32]), ("b_adaT", [128, 192]), ("n1g", [128, 32]), ("n2g", [128, 32]),
    ("fing", [128, 32]), ("sinks_bc", [128, 32]), ("flag", [128, 1]),
    ("lamre_p", [128, 64]), ("lamim_p", [128, 64]), ("lstep_p", [128, 64]),
    ("b_gluT", [128, 16]), ("ssm_gT", [128, 16]), ("attn_gT", [128, 16]), ("identf", [128, 128]),
]
CAST_IN = [
    ("cre_e", [128, 64, 32]), ("cim_e", [128, 64, 32]), ("d_e", [128, 16, 32]),
    ("mask_n", [128, 256]), ("mask_f", [128, 256]), ("identb", [128, 128]),
]
BIG_IN = [
    ("lamre_e", [128, 2048]), ("lamim_e", [128, 2048]), ("lstep_e", [128, 2048]),
    ("bre_e", [128, 2048]), ("bim_e", [128, 2048]),
]


def build(debug=None):
    nc = bass.Bass("TRN2", target_bir_lowering=False)
    dr = {}

    def din(name, shape):
        dr[name] = nc.dram_tensor(name, shape, F32, kind="ExternalInput").ap()

    din("xcat", [4096, D])
    for n, s in SMALL_IN + CAST_IN + BIG_IN:
        din(n, s)
    din("iota", [128, 512])
    din("ropec", [128, 4096])
    din("ropes", [128, 4096])
    din("w_ada", [D, 6 * D])
    din("w_in", [D, WINP])
    din("w_glu", [2048, 2048])
    din("w_out", [D, D])
    din("w_ff1", [D, DFF])
    din("w_ff2", [DFF, D])
    out_d = nc.dram_tensor("out", [2048, D], F32, kind="ExternalOutput").ap()
    tabs = nc.dram_tensor("tabs", [64, 2, 128, 512], F32, kind="Internal").ap()
    wtabs = nc.dram_tensor("wtabs", [64, 2, 128, 512], F32, kind="Internal").ap()

    with contextlib.ExitStack() as es:
        S = Sched(nc, es)
        B = lambda name, shape, dt=F32, psum=False: Buf(nc, es, name, shape, dt, psum)
        acc = B("acc", [128, KC, T])
        hb = B("hb", [128, KC, T], BF16)
        wr = B("wr", [128, NW, 4, 512], BF16)
        Kb = B("Kb", [128, 8, 640], BF16)
        Va = B("Va", [128, 5, 4, 65], BF16)
        U = B("U", [128, 13312])
        ps = [B("ps%d" % i, [128, 512], F32, psum=True) for i in range(8)]
        sm = {n: B(n, s) for n, s in SMALL_IN}
        cb = {n: B(n, s, BF16) for n, s in CAST_IN}
        Bexp = B("Bexp", [128, 16, 2, 128], BF16)
        mods = B("mods", [128, 192])
        gs1 = B("gs1", [128, 32]); gs2 = B("gs2", [128, 32])
        cact = B("cact", [128, 32], BF16)
        esink = B("esink", [128, 32])
        ones = B("ones", [128, 128])
        rr = B("rr", [128, 64]); cos1 = B("cos1", [128, 64]); sin1 = B("sin1", [128, 64]); phi = B("phi", [128, 64])
        car = B("car", [128, 2, 64]); wini = B("wini", [128, 2, 64]); ctmp = B("ctmp", [128, 4, 64])
        halfpi = B("halfpi", [128, 1])
        magic = B("magic", [128, 1]); magic2 = B("magic2", [128, 1])
        sml = B("sml", [128, 16])

        def uf(lo_b, n):
            return U.t[:, lo_b // 4: lo_b // 4 + n]

        def ubf(lo_b, n):
            return U.t[:, lo_b // 4: lo_b // 4 + n // 2].bitcast(BF16)

        def ur(lo_b, nbytes):
            return ("U", lo_b // 4, (lo_b + nbytes) // 4)

        K1 = 1024
        Qv = ubf(0, 16 * 512).rearrange("p (c t) -> p c t", c=16)
        gbv = Qv
        ubv = ubf(16 * K1, 16 * 512).rearrange("p (c t) -> p c t", c=16)

        def Qr(c):
            return ur(c * 1024, 1024)

        def ubr(c):
            return ur(16 * K1 + c * 1024, 1024)

        X0 = 32 * K1

        pctr = [0]

        def pnext():
            b = pctr[0] % 7
            pctr[0] += 1
            return b

        wctr = [0]

        def wtile(Wd, r0, c0, ncols=512):
            s = wctr[0] % NW
            wctr[0] += 1
            S.dma("pool", wr.t[:, s, :, 0:ncols],
                  Wd[r0:r0 + 512, c0:c0 + ncols].rearrange("(a p) n -> p a n", p=128),
                  writes=[wr.r(s * 2048, (s + 1) * 2048)], key="w%d" % s)
            return s

        def wreg(s):
            return wr.r(s * 2048, (s + 1) * 2048)

        def dve(fn, reads=(), writes=()):
            return S.op("dve", fn, reads, writes)

        def act(fn, reads=(), writes=()):
            return S.op("act", fn, reads, writes)

        def pe(fn, reads=(), writes=()):
            return S.op("pe", fn, reads, writes)

        def group_mm(Wd, nk4, c0, rhs, epi, nj=4):
            banks = [pnext() for _ in range(nj)]
            nk = nk4 * 4
            for k4 in range(nk4):
                s = wtile(Wd, k4 * 512, c0, nj * 128)
                for a in range(4):
                    kc = k4 * 4 + a
                    rap, rreg = rhs(kc)
                    for j in range(nj):
                        pe(lambda e, b=banks[j], s=s, a=a, j=j, rap=rap, kc=kc:
                           e.matmul(ps[b].t[:], wr.t[:, s, a, j * 128:(j + 1) * 128], rap,
                                    start=(kc == 0), stop=(kc == nk - 1)),
                           reads=[wreg(s), rreg], writes=[ps[banks[j]].all()])
            for j in range(nj):
                epi(j, banks[j])

        def dump(name, ap, reads):
            if debug and name in debug:
                dd = nc.dram_tensor("dbg_" + name, list(ap.shape), ap.dtype, kind="ExternalOutput").ap()
                S.dma("sp", dd, ap, reads=reads, key="dbg_" + name)

        for n, s in SMALL_IN:
            S.dma("sp", sm[n].t[:], dr[n], writes=[sm[n].all()], key="ld_" + n)
        for n, s in CAST_IN:
            S.dma("pool", cb[n].t[:], dr[n], writes=[cb[n].all()], key="ld_" + n)
        dve(lambda e: e.memset(ones.t[:], 1.0), writes=[ones.all()])
        dve(lambda e: e.memset(halfpi.t[:], math.pi / 2), writes=[halfpi.all()])
        dve(lambda e: e.memset(magic.t[:], MAGIC), writes=[magic.all()])
        dve(lambda e: e.memset(magic2.t[:], -MAGIC), writes=[magic2.all()])
        dve(lambda e: e.memset(Va.t[:], 0.0), writes=[Va.all()])
        dve(lambda e: e.memset(Va.t[:, :, :, 64:65], 1.0), writes=[Va.all()])
        dve(lambda e: e.memset(Kb.t[:], 0.0), writes=[Kb.all()])
        dve(lambda e: e.memset(car.t[:], 0.0), writes=[car.all()])
        act(lambda e: e.activation(out=cact.t[:], in_=sm["cT"].t[:], func=AF.Silu),
            reads=[sm["cT"].all()], writes=[cact.all()])
        act(lambda e: e.activation(out=esink.t[:], in_=sm["sinks_bc"].t[:], func=AF.Exp),
            reads=[sm["sinks_bc"].all()], writes=[esink.all()])

        def ada_all(slabs, banks):
            for slab in slabs:
                for k4 in range(8):
                    s = wtile(dr["w_ada"], k4 * 512, slab * 512)
                    for a in range(4):
                        kc = k4 * 4 + a
                        for j in range(4):
                            pe(lambda e, s=s, a=a, j=j, kc=kc, b=banks[j]:
                               e.matmul(ps[b].t[:, 0:1], wr.t[:, s, a, j * 128:(j + 1) * 128],
                                        cact.t[:, kc:kc + 1], start=(kc == 0), stop=(kc == 31)),
                               reads=[wreg(s), cact.all()], writes=[ps[banks[j]].all()])
                    if k4 == 7:
                        for j in range(4):
                            col = 4 * slab + j
                            dve(lambda e, b=banks[j], col=col: e.tensor_tensor(out=mods.t[:, col:col + 1], in0=ps[b].t[:, 0:1],
                                                                              in1=sm["b_adaT"].t[:, col:col + 1], op=ALU.add),
                                reads=[ps[banks[j]].all(), sm["b_adaT"].all()], writes=[mods.r(col, col + 1)])
                    yield

        sh1 = mods.t[:, 0:32]; sc1 = mods.t[:, 32:64]; gt1 = mods.t[:, 64:96]
        sh2 = mods.t[:, 96:128]; sc2 = mods.t[:, 128:160]; gt2 = mods.t[:, 160:192]

        def sincos(ang, n, tmpA, tmpB, sin_out, cos_out, rd, wrs):
            act(lambda e: e.activation(out=tmpA, in_=ang, func=AF.Identity, scale=1.0 / TWO_PI, bias=magic.t[:]), reads=rd + [magic.all()], writes=wrs)
            act(lambda e: e.activation(out=tmpB, in_=tmpA, func=AF.Identity, scale=1.0, bias=magic2.t[:]), reads=wrs + [magic2.all()], writes=wrs)
            dve(lambda e: e.scalar_tensor_tensor(out=ang, in0=tmpB, scalar=-TWO_PI, in1=ang, op0=ALU.mult, op1=ALU.add), reads=rd + wrs, writes=rd)
            dve(lambda e: e.tensor_scalar(ang, ang, 3.1415925, -3.1415925, op0=ALU.min, op1=ALU.max), reads=rd, writes=rd)
            act(lambda e: e.activation(out=sin_out, in_=ang, func=AF.Sin), reads=rd, writes=wrs)
            dve(lambda e: e.scalar_tensor_tensor(out=tmpA, in0=ang, scalar=-1.0, in1=ang, op0=ALU.mult, op1=ALU.max), reads=rd, writes=wrs)
            act(lambda e: e.activation(out=cos_out, in_=tmpA, func=AF.Sin, bias=halfpi.t[:], scale=-1.0),
                reads=wrs + [halfpi.all()], writes=wrs)

        def sincos_g(ang, tmpA, tmpB, sin_out, cos_out, rd, wrs):
            act(lambda e: e.activation(out=tmpA, in_=ang, func=AF.Identity, scale=1.0 / TWO_PI, bias=magic.t[:]), reads=rd + [magic.all()], writes=wrs)
            yield
            act(lambda e: e.activation(out=tmpB, in_=tmpA, func=AF.Identity, scale=1.0, bias=magic2.t[:]), reads=wrs + [magic2.all()], writes=wrs)
            yield
            dve(lambda e: e.scalar_tensor_tensor(out=ang, in0=tmpB, scalar=-TWO_PI, in1=ang, op0=ALU.mult, op1=ALU.add), reads=rd + wrs, writes=rd)
            yield
            dve(lambda e: e.tensor_scalar(ang, ang, 3.1415925, -3.1415925, op0=ALU.min, op1=ALU.max), reads=rd, writes=rd)
            yield
            act(lambda e: e.activation(out=sin_out, in_=ang, func=AF.Sin), reads=rd, writes=wrs)
            yield
            dve(lambda e: e.scalar_tensor_tensor(out=tmpA, in0=ang, scalar=-1.0, in1=ang, op0=ALU.mult, op1=ALU.max), reads=rd, writes=wrs)
            yield
            act(lambda e: e.activation(out=cos_out, in_=tmpA, func=AF.Sin, bias=halfpi.t[:], scale=-1.0),
                reads=wrs + [halfpi.all()], writes=wrs)
            yield

        av = acc.t[:].rearrange("p c t -> p (c t)")

        def a_(i):
            return av[:, i * 2048:(i + 1) * 2048]

        AR = [acc.all()]
        for i, n in enumerate(["lamre_e", "lamim_e", "lstep_e", "bre_e", "bim_e"]):
            S.dma("sp", a_(i), dr[n], writes=AR, key="ld_big")
        LRE, LIM, LST, BRE, BIM, T5, T6, T7 = [a_(i) for i in range(8)]
        act(lambda e: e.activation(out=LST, in_=LST, func=AF.Exp), reads=AR, writes=AR)
        dve(lambda e: e.tensor_tensor(out=T5, in0=LIM, in1=LST, op=ALU.mult), reads=AR, writes=AR)
        dve(lambda e: e.tensor_tensor(out=LST, in0=LRE, in1=LST, op=ALU.mult), reads=AR, writes=AR)
        act(lambda e: e.activation(out=LST, in_=LST, func=AF.Exp), reads=AR, writes=AR)
        hbf = hb.t[:].rearrange("p c t -> p (c t)").bitcast(F32)
        HR = [hb.all()]
        sincos(T5, 2048, hbf[:, 0:2048], hbf[:, 2048:4096], T6, T7, AR, HR + AR)
        dve(lambda e: e.tensor_tensor(out=T6, in0=T6, in1=LST, op=ALU.mult), reads=AR, writes=AR)
        dve(lambda e: e.tensor_tensor(out=T7, in0=T7, in1=LST, op=ALU.mult), reads=AR, writes=AR)
        dve(lambda e: e.tensor_scalar(T7, T7, -1.0, None, op0=ALU.add), reads=AR, writes=AR)
        h0, h1, h2, h3 = [hbf[:, i * 2048:(i + 1) * 2048] for i in range(4)]
        dve(lambda e: e.tensor_tensor(out=h0, in0=LRE, in1=LRE, op=ALU.mult), reads=AR, writes=HR)
        dve(lambda e: e.tensor_tensor(out=h1, in0=LIM, in1=LIM, op=ALU.mult), reads=AR, writes=HR)
        dve(lambda e: e.tensor_tensor(out=h0, in0=h0, in1=h1, op=ALU.add), reads=HR, writes=HR)
        dve(lambda e: e.reciprocal(h0, h0), reads=HR, writes=HR)
        dve(lambda e: e.tensor_tensor(out=h1, in0=T7, in1=LRE, op=ALU.mult), reads=AR, writes=HR)
        dve(lambda e: e.tensor_tensor(out=h3, in0=T6, in1=LIM, op=ALU.mult), reads=AR, writes=HR)
        dve(lambda e: e.tensor_tensor(out=h1, in0=h1, in1=h3, op=ALU.add), reads=HR, writes=HR)
        dve(lambda e: e.tensor_tensor(out=h1, in0=h1, in1=h0, op=ALU.mult), reads=HR, writes=HR)
        dve(lambda e: e.tensor_tensor(out=h2, in0=T6, in1=LRE, op=ALU.mult), reads=AR, writes=HR)
        dve(lambda e: e.tensor_tensor(out=h3, in0=T7, in1=LIM, op=ALU.mult), reads=AR, writes=HR)
        dve(lambda e: e.tensor_tensor(out=h2, in0=h2, in1=h3, op=ALU.subtract), reads=HR, writes=HR)
        dve(lambda e: e.tensor_tensor(out=h2, in0=h2, in1=h0, op=ALU.mult), reads=HR, writes=HR)
        dve(lambda e: e.tensor_tensor(out=T5, in0=h1, in1=BRE, op=ALU.mult), reads=AR + HR, writes=AR)
        dve(lambda e: e.tensor_tensor(out=h3, in0=h2, in1=BIM, op=ALU.mult), reads=AR + HR, writes=HR)
        dve(lambda e: e.tensor_tensor(out=Bexp.t[:, :, 0, :], in0=T5.rearrange("p (c n) -> p c n", c=16),
                                      in1=h3.rearrange("p (c n) -> p c n", c=16), op=ALU.subtract),
            reads=AR + HR, writes=[Bexp.all()])
        dve(lambda e: e.tensor_tensor(out=T5, in0=h1, in1=BIM, op=ALU.mult), reads=AR + HR, writes=AR)
        dve(lambda e: e.tensor_tensor(out=h3, in0=h2, in1=BRE, op=ALU.mult), reads=AR + HR, writes=HR)
        dve(lambda e: e.tensor_tensor(out=Bexp.t[:, :, 1, :], in0=T5.rearrange("p (c n) -> p c n", c=16),
                                      in1=h3.rearrange("p (c n) -> p c n", c=16), op=ALU.add),
            reads=AR + HR, writes=[Bexp.all()])

        lnr = B("lnr", [128, 64]); nphi = B("nphi", [128, 64]); phi511 = B("phi511", [128, 64])
        nlnr = B("nlnr", [128, 64]); lnr511 = B("lnr511", [128, 64]); Gre = B("Gre", [128, 64]); Gim = B("Gim", [128, 64])
        accS = B("accS", [128, 4, 64])
        PR = [rr.all(), cos1.all(), sin1.all(), phi.all(), ctmp.all(), lnr.all(), nphi.all(), phi511.all(), nlnr.all(), lnr511.all(), Gre.all(), Gim.all()]
        st_p = ctmp.t[:, 0, :]
        act(lambda e: e.activation(out=st_p, in_=sm["lstep_p"].t[:], func=AF.Exp), reads=[sm["lstep_p"].all()], writes=PR)
        dve(lambda e: e.tensor_tensor(out=lnr.t[:], in0=sm["lamre_p"].t[:], in1=st_p, op=ALU.mult), reads=PR + [sm["lamre_p"].all()], writes=PR)
        act(lambda e: e.activation(out=rr.t[:], in_=lnr.t[:], func=AF.Exp), reads=PR, writes=PR)
        dve(lambda e: e.tensor_tensor(out=phi.t[:], in0=sm["lamim_p"].t[:], in1=st_p, op=ALU.mult), reads=PR + [sm["lamim_p"].all()], writes=PR)
        sincos(phi.t[:], 64, ctmp.t[:, 1, :], ctmp.t[:, 2, :], sin1.t[:], cos1.t[:], PR, PR)

        dve(lambda e: e.tensor_scalar(nphi.t[:], phi.t[:], -1.0, None, op0=ALU.mult), reads=PR, writes=PR)
        dve(lambda e: e.tensor_scalar(phi511.t[:], phi.t[:], 511.0, None, op0=ALU.mult), reads=PR, writes=PR)
        dve(lambda e: e.tensor_scalar(nlnr.t[:], lnr.t[:], -1.0, None, op0=ALU.mult), reads=PR, writes=PR)
        dve(lambda e: e.tensor_scalar(lnr511.t[:], lnr.t[:], 511.0, None, op0=ALU.mult), reads=PR, writes=PR)
        dve(lambda e: e.tensor_scalar(ctmp.t[:, 0, :], phi.t[:], 512.0, None, op0=ALU.mult), reads=PR, writes=PR)
        sincos(ctmp.t[:, 0, :], 64, ctmp.t[:, 1, :], ctmp.t[:, 2, :], Gim.t[:], Gre.t[:], PR, PR)
        act(lambda e: e.activation(out=ctmp.t[:, 3, :], in_=lnr.t[:], func=AF.Exp, scale=512.0), reads=PR, writes=PR)
        dve(lambda e: e.tensor_tensor(out=Gre.t[:], in0=Gre.t[:], in1=ctmp.t[:, 3, :], op=ALU.mult), reads=PR, writes=PR)
        dve(lambda e: e.tensor_tensor(out=Gim.t[:], in0=Gim.t[:], in1=ctmp.t[:, 3, :], op=ALU.mult), reads=PR, writes=PR)

        c511 = B("c511", [128, 64]); s511 = B("s511", [128, 64]); wlast = B("wlast", [128, 2, 64])
        PR = PR + [c511.all(), s511.all()]
        dve(lambda e: e.tensor_scalar(ctmp.t[:, 0, :], phi.t[:], 511.0, None, op0=ALU.mult), reads=PR, writes=PR)
        sincos(ctmp.t[:, 0, :], 64, ctmp.t[:, 1, :], ctmp.t[:, 2, :], s511.t[:], c511.t[:], PR, PR)
        iotv = uf(X0 + 12288, 512); iot_r = ur(X0 + 12288, 2048)
        revv = uf(X0 + 14336, 512); rev_r = ur(X0 + 14336, 2048)
        S.dma("sp", iotv, dr["iota"], writes=[iot_r], key="ld_iota")
        dve(lambda e: e.tensor_scalar(revv, iotv, -1.0, 511.0, op0=ALU.mult, op1=ALU.add), reads=[iot_r], writes=[rev_r])
        NBT = 4
        tbb = [av[:, i * 2048:(i + 1) * 2048] for i in range(6)]
        v3 = lambda ap: ap.rearrange("p (q t) -> p q t", q=NBT)
        bc_t = lambda ap: ap.unsqueeze(1).to_broadcast([128, NBT, 512])
        hbs = [hbf[:, i * 2048:(i + 1) * 2048] for i in range(4)]

        def gen_E_batch(bt):
            p0 = bt * NBT
            bc_p = lambda buf: buf.t[:, p0:p0 + NBT].unsqueeze(2).to_broadcast([128, NBT, 512])
            dve(lambda e, a=bc_t(iotv), b=bc_p(nphi): e.tensor_tensor(out=v3(hbs[0]), in0=a, in1=b, op=ALU.mult), reads=[iot_r] + PR, writes=HR)
            yield
            yield from sincos_g(hbs[0], hbs[1], hbs[2], hbs[3], hbs[2], HR, HR)
            S.dma("sp", tabs[p0:p0 + NBT, 1].rearrange("q p t -> p q t"), v3(hbs[3]), reads=HR, writes=[("tabs", p0, p0 + NBT)], key="tabw")
            S.dma("sp", tabs[p0:p0 + NBT, 0].rearrange("q p t -> p q t"), v3(hbs[2]), reads=HR, writes=[("tabs", p0, p0 + NBT)], key="tabw")
            yield

        def gen_W_batch(bt):
            p0 = bt * NBT
            bc_p = lambda buf: buf.t[:, p0:p0 + NBT].unsqueeze(2).to_broadcast([128, NBT, 512])
            dve(lambda e, a=bc_t(revv), b=bc_p(phi): e.tensor_tensor(out=v3(tbb[0]), in0=a, in1=b, op=ALU.mult), reads=[rev_r] + PR, writes=AR)
            yield
            dve(lambda e, a=bc_t(revv), b=bc_p(lnr): e.tensor_tensor(out=v3(tbb[3]), in0=a, in1=b, op=ALU.mult), reads=[rev_r] + PR, writes=AR)
            yield
            act(lambda e: e.activation(out=tbb[3], in_=tbb[3], func=AF.Exp), reads=AR, writes=AR)
            yield
            yield from sincos_g(tbb[0], tbb[1], tbb[2], tbb[4], tbb[5], AR, AR)
            dve(lambda e: e.tensor_tensor(out=tbb[1], in0=tbb[5], in1=tbb[3], op=ALU.mult), reads=AR, writes=AR)
            yield
            dve(lambda e: e.tensor_tensor(out=tbb[2], in0=tbb[4], in1=tbb[3], op=ALU.mult), reads=AR, writes=AR)
            yield
            S.dma("sp", wtabs[p0:p0 + NBT, 0].rearrange("q p t -> p q t"), v3(tbb[1]), reads=AR, writes=[("wtabs", p0, p0 + NBT)], key="wtabw")
            S.dma("sp", wtabs[p0:p0 + NBT, 1].rearrange("q p t -> p q t"), v3(tbb[2]), reads=AR, writes=[("wtabs", p0, p0 + NBT)], key="wtabw")
            yield

        def interleave(*gens):
            gens = list(gens)
            while gens:
                for g_ in list(gens):
                    try:
                        next(g_)
                    except StopIteration:
                        gens.remove(g_)

        for i in range(16):
            for _ in ada_all(range(i, i + 1), [0, 1, 2, 3]):
                pass
            interleave(gen_W_batch(i), gen_E_batch(i))
        pctr[0] = 4
        dve(lambda e: e.scalar_tensor_tensor(out=gs1.t[:], in0=sc1, scalar=1.0, in1=sm["n1g"].t[:], op0=ALU.add, op1=ALU.mult),
            reads=[mods.all(), sm["n1g"].all()], writes=[gs1.all()])

        sqv = [uf(X0 + i * 2048, 512) for i in range(2)]
        sqr = [ur(X0 + i * 2048, 2048) for i in range(2)]
        rstd = uf(X0 + 4096, 512); rstd_r = ur(X0 + 4096, 2048)
        ntv = [uf(X0 + 6144 + i * 2048, 512) for i in range(2)]
        ntr = [ur(X0 + 6144 + i * 2048, 2048) for i in range(2)]
        s2v = uf(X0 + 10240, 512); s2_r = ur(X0 + 10240, 2048)
        sq4 = [uf(X0 + 12288 + i * 2048, 512) for i in range(4)]; sq4_r = [ur(X0 + 12288 + i * 2048, 2048) for i in range(4)]
        sqc = [0]

        class Stats:
            def __init__(self):
                self.i = 0

            def add(self, ap, reg):
                q = sqc[0] % 4
                sqc[0] += 1
                if self.i == 0:
                    act(lambda e: e.activation(out=s2v, in_=ap, func=AF.Square), reads=[reg], writes=[s2_r])
                else:
                    act(lambda e: e.activation(out=sq4[q], in_=ap, func=AF.Square), reads=[reg], writes=[sq4_r[q]])
                    dve(lambda e: e.tensor_tensor(out=s2v, in0=s2v, in1=sq4[q], op=ALU.add), reads=[s2_r, sq4_r[q]], writes=[s2_r])
                self.i += 1

            def finish(self, nfeat):
                pe(lambda e: e.matmul(ps[SSB].t[:], ones.t[:], s2v, start=True, stop=True), reads=[ones.all(), s2_r], writes=[ps[SSB].all()])
                finish_stats(nfeat)

        SSB = 7

        def accr(c):
            return acc.r(c * T, (c + 1) * T)

        def hbr(c):
            return hb.r(c * T, (c + 1) * T)

        def rms_stats(chunks, nfeat):
            n = len(chunks)
            for i, (ap, reg) in enumerate(chunks):
                q = i % 2
                act(lambda e, ap=ap, q=q: e.activation(out=sqv[q], in_=ap, func=AF.Square), reads=[reg], writes=[sqr[q]])
                pe(lambda e, q=q, i=i: e.matmul(ps[SSB].t[:], ones.t[:], sqv[q], start=(i == 0), stop=(i == n - 1)),
                   reads=[ones.all(), sqr[q]], writes=[ps[SSB].all()])
            finish_stats(nfeat)

        def finish_stats(nfeat):
            dve(lambda e: e.tensor_scalar(rstd, ps[SSB].t[:], 1.0 / nfeat, EPS, op0=ALU.mult, op1=ALU.add),
                reads=[ps[SSB].all()], writes=[rstd_r])
            act(lambda e: e.activation(out=rstd, in_=rstd, func=AF.Sqrt), reads=[rstd_r], writes=[rstd_r])
            dve(lambda e: e.reciprocal(rstd, rstd), reads=[rstd_r], writes=[rstd_r])

        def norm_mod(gs, sh, st=None):
            if st is None:
                rms_stats([(acc.t[:, c, :], accr(c)) for c in range(KC)], D)
            else:
                st.finish(D)
            for c in range(KC):
                q = c % 2
                dve(lambda e, c=c, q=q: e.tensor_tensor(out=ntv[q], in0=acc.t[:, c, :], in1=rstd, op=ALU.mult),
                    reads=[accr(c), rstd_r], writes=[ntr[q]])
                act(lambda e, c=c, q=q: e.activation(out=hb.t[:, c, :], in_=ntv[q], func=AF.Identity,
                                                    bias=sh[:, c:c + 1], scale=gs[:, c:c + 1]),
                    reads=[ntr[q], mods.all(), gs1.all(), gs2.all()], writes=[hbr(c)])

        xs = [uf(i * 8192, 2048) for i in range(2)]
        xsr = [ur(i * 8192, 8192) for i in range(2)]

        def load_x(row0):
            cnt = 0
            for blk in range(4):
                for half in range(2):
                    q = cnt % 2
                    cnt += 1
                    S.dma("sp", xs[q], dr["xcat"][row0 + blk * 128: row0 + (blk + 1) * 128, half * 2048:(half + 1) * 2048],
                          writes=[xsr[q]], key="xs%d" % q)
                    for c4 in range(4):
                        b = pnext()
                        for cc in range(4):
                            pe(lambda e, b=b, q=q, c4=c4, cc=cc:
                               e.transpose(ps[b].t[:, cc * 128:(cc + 1) * 128], xs[q][:, (c4 * 4 + cc) * 128:(c4 * 4 + cc + 1) * 128], sm["identf"].t[:]),
                               reads=[xsr[q], sm["identf"].all()], writes=[ps[b].all()])
                        c0 = half * 16 + c4 * 4
                        fn = lambda e, b=b, c0=c0, blk=blk: e.tensor_copy(
                            acc.t[:, c0:c0 + 4, blk * 128:(blk + 1) * 128], ps[b].t[:].rearrange("p (c t) -> p c t", c=4))
                        fn2 = lambda e, b=b, c0=c0, blk=blk: e.activation(
                            out=acc.t[:, c0:c0 + 4, blk * 128:(blk + 1) * 128], in_=ps[b].t[:].rearrange("p (c t) -> p c t", c=4), func=AF.Identity)
                        S.op("dve" if c4 % 2 == 0 else "act", fn if c4 % 2 == 0 else fn2,
                             reads=[ps[b].all()], writes=[acc.r(c0 * T, (c0 + 4) * T)])

        rcv = uf(X0, 512); rsv = uf(X0 + 2048, 512)
        rc_r = ur(X0, 4096)
        t4v = [uf(X0 + 4096 + i * 2048, 512) for i in range(4)]
        t4r = ur(X0 + 4096, 8192)

        def rope_epi(ba, bb, outA, outB, regA, regB):
            dve(lambda e: e.tensor_tensor(out=t4v[0], in0=ps[ba].t[:], in1=rcv, op=ALU.mult), reads=[ps[ba].all(), rc_r], writes=[t4r])
            dve(lambda e: e.tensor_tensor(out=t4v[1], in0=ps[bb].t[:], in1=rsv, op=ALU.mult), reads=[ps[bb].all(), rc_r], writes=[t4r])
            dve(lambda e: e.tensor_tensor(out=t4v[2], in0=ps[bb].t[:], in1=rcv, op=ALU.mult), reads=[ps[bb].all(), rc_r], writes=[t4r])
            dve(lambda e: e.tensor_tensor(out=t4v[3], in0=ps[ba].t[:], in1=rsv, op=ALU.mult), reads=[ps[ba].all(), rc_r], writes=[t4r])
            dve(lambda e: e.tensor_tensor(out=outA, in0=t4v[0], in1=t4v[1], op=ALU.subtract), reads=[t4r], writes=[regA])
            dve(lambda e: e.tensor_tensor(out=outB, in0=t4v[2], in1=t4v[3], op=ALU.add), reads=[t4r], writes=[regB])

        def in_proj(ti, pre, want_kv):
            col0 = ti * T
            S.dma("sp", rcv, dr["ropec"][:, col0:col0 + T], writes=[rc_r], key="rope")
            S.dma("sp", rsv, dr["ropes"][:, col0:col0 + T], writes=[rc_r], key="rope")
            rhs = lambda kc: (hb.t[:, kc, :], hbr(kc))
            if not pre:
                for slab in range(4):
                    banks = []
                    group_mm(dr["w_in"], 8, slab * 512, rhs, lambda j, b: banks.append(b))
                    for pr in range(2):
                        g = slab * 2 + pr
                        rope_epi(banks[2 * pr], banks[2 * pr + 1], Qv[:, 2 * g, :], Qv[:, 2 * g + 1, :], Qr(2 * g), Qr(2 * g + 1))
            if want_kv:
                for slab in range(2):
                    banks = []
                    group_mm(dr["w_in"], 8, 2048 + slab * 512, rhs, lambda j, b: banks.append(b))
                    for pr in range(2):
                        a = slab * 2 + pr
                        rope_epi(banks[2 * pr], banks[2 * pr + 1], Kb.t[:, 2 * a, 128:640], Kb.t[:, 2 * a + 1, 128:640],
                                 Kb.r((2 * a) * 640 + 128, (2 * a + 1) * 640), Kb.r((2 * a + 1) * 640 + 128, (2 * a + 2) * 640))
            for slab in range(4):
                def epi(j, b, slab=slab):
                    c = slab * 4 + j
                    act(lambda e: e.activation(out=ubv[:, c, :], in_=ps[b].t[:], func=AF.Identity), reads=[ps[b].all()], writes=[ubr(c)])
                group_mm(dr["w_in"], 8, 3072 + slab * 512, rhs, epi)
                if pre:
                    ada_full_slab()
            if want_kv:
                banks = [pnext() for _ in range(4)]
                for k4 in range(8):
                    s = wtile(dr["w_in"], k4 * 512, 5120, 256)
                    for a in range(4):
                        kc = k4 * 4 + a
                        for blk in range(4):
                            pe(lambda e, b=banks[blk], s=s, a=a, kc=kc, blk=blk:
                               e.matmul(ps[b].t[:, 0:256], hb.t[:, kc, blk * 128:(blk + 1) * 128], wr.t[:, s, a, 0:256],
                                        start=(kc == 0), stop=(kc == 31)),
                               reads=[wreg(s), hbr(kc)], writes=[ps[banks[blk]].all()])
                for blk in range(4):
                    b = banks[blk]
                    act(lambda e, b=b, blk=blk: e.activation(out=Va.t[:, 1 + blk, :, 0:64], in_=ps[b].t[:, 0:256].rearrange("p (h d) -> p h d", h=4), func=AF.Identity),
                        reads=[ps[b].all()], writes=[Va.r((1 + blk) * 260, (2 + blk) * 260)])

        def shift_halo():
            dve(lambda e: e.tensor_copy(Kb.t[:, :, 0:128], Kb.t[:, :, 512:640]), reads=[Kb.all()], writes=[Kb.all()])
            dve(lambda e: e.tensor_copy(Va.t[:, 0, :, :], Va.t[:, 4, :, :]), reads=[Va.all()], writes=[Va.all()])

        A0 = X0
        attok = uf(A0, 2048); attok_r = ur(A0, 8192)
        atn = ubf(A0 + 8192, 2048); atn_r = ur(A0 + 8192, 4096)
        NEB = 4
        ebv = [ubf(A0 + 12288 + i * 512, 256) for i in range(NEB)]
        ebr = [ur(A0 + 12288 + i * 512, 512) for i in range(NEB)]
        emv = [ubf(A0 + 14336 + i * 512, 256) for i in range(NEB)]
        emr = [ur(A0 + 14336 + i * 512, 512) for i in range(NEB)]
        atj = atn; atj_r = atn_r

        def attention(first_tile):
            for qb in range(4):
                msk = cb["mask_f"] if (first_tile and qb == 0) else cb["mask_n"]
                heads = [(g, j) for g in range(8) for j in range(4)]
                st = {}
                pob = {}

                def stageA(n, qb=qb):
                    g, j = heads[n]
                    a = g // 2
                    q = n % NEB
                    sb = pnext()
                    st[n] = (sb, q)
                    for kb in range(2):
                        kcol = (qb + kb) * 128
                        for ab in range(2):
                            pe(lambda e, sb=sb, kb=kb, ab=ab, kcol=kcol, g=g, j=j, a=a:
                               e.matmul(ps[sb].t[:, kb * 128:(kb + 1) * 128],
                                        Kb.t[32 * j:32 * j + 32, 2 * a + ab, kcol:kcol + 128],
                                        Qv[32 * j:32 * j + 32, 2 * g + ab, qb * 128:(qb + 1) * 128],
                                        start=(ab == 0), stop=(ab == 1), tile_position=(32 * j, 0)),
                               reads=[Kb.all(), Qr(2 * g + ab)], writes=[ps[sb].all()])
                    act(lambda e, sb=sb, q=q: e.activation(out=ebv[q], in_=ps[sb].t[:, 0:256], func=AF.Exp, scale=0.125),
                        reads=[ps[sb].all()], writes=[ebr[q]])

                def stageM(n, msk=msk):
                    q = st[n][1]
                    dve(lambda e, q=q: e.tensor_tensor(out=emv[q], in0=ebv[q], in1=msk.t[:], op=ALU.mult),
                        reads=[ebr[q], msk.all()], writes=[emr[q]])

                def stageP(n, qb=qb):
                    g, j = heads[n]
                    a = g // 2
                    q = st[n][1]
                    if j == 0:
                        pob[g] = pnext()
                    po = pob[g]
                    for kb in range(2):
                        pe(lambda e, po=po, q=q, kb=kb, j=j, a=a:
                           e.matmul(ps[po].t[:, j * 65:(j + 1) * 65], emv[q][:, kb * 128:(kb + 1) * 128],
                                    Va.t[:, qb + kb, a, :], start=(kb == 0), stop=(kb == 1)),
                           reads=[emr[q], Va.all()], writes=[ps[po].all()])
                    if j == 3:
                        pov = ps[po].t[:, 0:260].rearrange("p (h d) -> p h d", d=65)
                        SR = [sml.all()]
                        dve(lambda e, pov=pov, g=g: e.tensor_tensor(out=sml.t[:, 0:4], in0=pov[:, :, 64], in1=esink.t[:, 4 * g:4 * g + 4], op=ALU.add),
                            reads=[ps[po].all(), esink.all()], writes=SR)
                        dve(lambda e: e.reciprocal(sml.t[:, 4:8], sml.t[:, 0:4]), reads=SR, writes=SR)
                        dve(lambda e, pov=pov, g=g: e.tensor_tensor(
                            out=attok[:, g * 256:(g + 1) * 256].rearrange("p (h d) -> p h d", d=64), in0=pov[:, :, 0:64],
                            in1=sml.t[:, 4:8].unsqueeze(2).to_broadcast([128, 4, 64]), op=ALU.mult),
                            reads=[ps[po].all()] + SR, writes=[attok_r])

                for n in range(32 + 3):
                    if n < 32:
                        stageA(n)
                    if 0 <= n - 2 < 32:
                        stageM(n - 2)
                    if 0 <= n - 3 < 32:
                        stageP(n - 3)
                SR = [sml.all()]
                act(lambda e: e.activation(out=atj, in_=attok, func=AF.Square, accum_out=sml.t[:, 8:9]), reads=[attok_r], writes=[atj_r] + SR)
                dve(lambda e: e.tensor_scalar(sml.t[:, 9:10], sml.t[:, 8:9], 1.0 / 2048, EPS, op0=ALU.mult, op1=ALU.add), reads=SR, writes=SR)
                act(lambda e: e.activation(out=sml.t[:, 9:10], in_=sml.t[:, 9:10], func=AF.Sqrt), reads=SR, writes=SR)
                dve(lambda e: e.reciprocal(sml.t[:, 10:11], sml.t[:, 9:10]), reads=SR, writes=SR)
                dve(lambda e: e.tensor_scalar(atn, attok, sml.t[:, 10:11], None, op0=ALU.mult), reads=[attok_r] + SR, writes=[atn_r])
                for c4 in range(4):
                    b = pnext()
                    pb = ps[b].t[:].bitcast(BF16)
                    for cc in range(4):
                        c = c4 * 4 + cc
                        pe(lambda e, pb=pb, cc=cc, c=c: e.transpose(pb[:, cc * 128:(cc + 1) * 128], atn[:, c * 128:(c + 1) * 128], cb["identb"].t[:]),
                           reads=[atn_r, cb["identb"].all()], writes=[ps[b].all()])
                    for cc in range(4):
                        c = c4 * 4 + cc
                        act(lambda e, pb=pb, cc=cc, c=c, qb=qb: e.activation(out=hb.t[:, c, qb * 128:(qb + 1) * 128], in_=pb[:, cc * 128:(cc + 1) * 128],
                                                                   func=AF.Identity, scale=sm["attn_gT"].t[:, c:c + 1]),
                            reads=[ps[b].all(), sm["attn_gT"].all()], writes=[hb.r(c * T + qb * 128, c * T + (qb + 1) * 128)])

        S0 = X0
        tEs = [uf(S0 + i * 4096, 1024).rearrange("p (c t) -> p c t", c=2) for i in range(2)]
        tErs = [ur(S0 + i * 4096, 4096) for i in range(2)]
        stmp = [uf(S0 + 8192 + i * 2048, 512) for i in range(2)]; stmp_r = [ur(S0 + 8192 + i * 2048, 2048) for i in range(2)]
        mmv = [uf(S0 + 12288 + i * 2048, 512) for i in range(2)]; mm_r = [ur(S0 + 12288 + i * 2048, 2048) for i in range(2)]
        wwv = [uf(S0 + 16384 + i * 2048, 512) for i in range(2)]; ww_r = [ur(S0 + 16384 + i * 2048, 2048) for i in range(2)]
        zzv = [Kb.t[:, a_, 128:640] for a_ in range(4)]; zz_r = [Kb.r(a_ * 640 + 128, (a_ + 1) * 640) for a_ in range(4)]
        nwv = Kb.t[:, 4, 128:640]; nw_r = Kb.r(4 * 640 + 128, 5 * 640)
        wrbv = Kb.t[:, 5, 128:640]; wrb_r = Kb.r(5 * 640 + 128, 6 * 640)
        ecbv = Kb.t[:, 6, 128:640]; ecb_r = Kb.r(6 * 640 + 128, 7 * 640)
        snbv = Kb.t[:, 7, 128:640]; snb_r = Kb.r(7 * 640 + 128, 8 * 640)

        def ssm():
            CR = [car.all(), wini.all(), ctmp.all(), cos1.all(), sin1.all()]
            cre, cim = car.t[:, 0, :], car.t[:, 1, :]
            dve(lambda e: e.tensor_tensor(out=ctmp.t[:, 0, :], in0=cos1.t[:], in1=cre, op=ALU.mult), reads=CR, writes=CR)
            dve(lambda e: e.tensor_tensor(out=ctmp.t[:, 1, :], in0=sin1.t[:], in1=cim, op=ALU.mult), reads=CR, writes=CR)
            dve(lambda e: e.tensor_tensor(out=wini.t[:, 0, :], in0=ctmp.t[:, 0, :], in1=ctmp.t[:, 1, :], op=ALU.subtract), reads=CR, writes=CR)
            dve(lambda e: e.tensor_tensor(out=ctmp.t[:, 2, :], in0=sin1.t[:], in1=cre, op=ALU.mult), reads=CR, writes=CR)
            dve(lambda e: e.tensor_tensor(out=ctmp.t[:, 3, :], in0=cos1.t[:], in1=cim, op=ALU.mult), reads=CR, writes=CR)
            dve(lambda e: e.tensor_tensor(out=wini.t[:, 1, :], in0=ctmp.t[:, 2, :], in1=ctmp.t[:, 3, :], op=ALU.add), reads=CR, writes=CR)
            bbank = [0, 1, 2, 3]
            ybank = [4, 5]

            def bk(n):
                return bbank[(2 * n) % 4], bbank[(2 * n + 1) % 4]

            def stL_dma(n):
                S.dma("sp", tEs[n % 2], tabs[n].rearrange("c p t -> p c t"), reads=[("tabs", n, n + 1)], writes=[tErs[n % 2]], key="tabr%d" % (n % 2))

            def stL_pe(n):
                c_, j_ = n // 4, n % 4
                for ri, b in zip((0, 1), bk(n)):
                    pe(lambda e, ri=ri, b=b, c_=c_, j_=j_: e.matmul(ps[b].t[:], Bexp.t[32 * j_:32 * j_ + 32, c_, ri, :],
                                                                  ubv[32 * j_:32 * j_ + 32, c_, :], start=True, stop=True,
                                                                  tile_position=(32 * j_, 0)),
                       reads=[Bexp.all(), ubr(c_)], writes=[ps[b].all()])

            def stM(n):
                Ec, Es = tEs[n % 2][:, 0, :], tEs[n % 2][:, 1, :]
                tr = tErs[n % 2]
                bre, bim = bk(n)
                dve(lambda e: e.tensor_tensor(out=mmv[0], in0=ps[bre].t[:], in1=Ec, op=ALU.mult), reads=[ps[bre].all(), tr], writes=[mm_r[0]])
                dve(lambda e: e.tensor_tensor(out=stmp[0], in0=ps[bim].t[:], in1=Es, op=ALU.mult), reads=[ps[bim].all(), tr], writes=[stmp_r[0]])
                dve(lambda e: e.tensor_tensor(out=mmv[1], in0=ps[bim].t[:], in1=Ec, op=ALU.mult), reads=[ps[bim].all(), tr], writes=[mm_r[1]])
                dve(lambda e: e.tensor_tensor(out=stmp[1], in0=ps[bre].t[:], in1=Es, op=ALU.mult), reads=[ps[bre].all(), tr], writes=[stmp_r[1]])

            def stA(n):
                S.op("pool", lambda e: e.tensor_tensor(out=mmv[0], in0=mmv[0], in1=stmp[0], op=ALU.subtract), [mm_r[0], stmp_r[0]], [mm_r[0]])
                S.op("pool", lambda e: e.tensor_tensor(out=mmv[1], in0=mmv[1], in1=stmp[1], op=ALU.add), [mm_r[1], stmp_r[1]], [mm_r[1]])

            def stS(n):
                for ri in range(2):
                    dve(lambda e, ri=ri, p=n: e.tensor_tensor_scan(out=wwv[ri], data0=rr.t[:, p:p + 1].to_broadcast([128, 512]), data1=mmv[ri],
                                                                  initial=wini.t[:, ri, p:p + 1], op0=ALU.mult, op1=ALU.add),
                        reads=[mm_r[ri], rr.all(), wini.all()], writes=[ww_r[ri]])
                for ri in range(2):
                    act(lambda e, ri=ri, p=n: e.activation(out=wlast.t[:, ri, p:p + 1], in_=wwv[ri][:, 511:512], func=AF.Identity),
                        reads=[ww_r[ri]], writes=[wlast.all()])
                act(lambda e: e.activation(out=nwv, in_=wwv[1], func=AF.Identity, scale=-1.0), reads=[ww_r[1]], writes=[nw_r])
                act(lambda e: e.activation(out=wrbv, in_=wwv[0], func=AF.Identity), reads=[ww_r[0]], writes=[wrb_r])
                act(lambda e, n=n: e.activation(out=ecbv, in_=tEs[n % 2][:, 0, :], func=AF.Identity), reads=[tErs[n % 2]], writes=[ecb_r])
                act(lambda e, n=n: e.activation(out=snbv, in_=tEs[n % 2][:, 1, :], func=AF.Identity), reads=[tErs[n % 2]], writes=[snb_r])

            def stZ(n):
                c_, j_ = n // 4, n % 4
                Ec, Es = tEs[n % 2][:, 0, :], tEs[n % 2][:, 1, :]
                tr = tErs[n % 2]
                dve(lambda e: e.tensor_tensor(out=zzv[0], in0=wrbv, in1=ecbv, op=ALU.mult), reads=[wrb_r, ecb_r], writes=[zz_r[0]])
                dve(lambda e: e.tensor_tensor(out=zzv[1], in0=wwv[1], in1=Es, op=ALU.mult), reads=[ww_r[1], tr], writes=[zz_r[1]])
                dve(lambda e: e.tensor_tensor(out=zzv[2], in0=wrbv, in1=snbv, op=ALU.mult), reads=[wrb_r, snb_r], writes=[zz_r[2]])
                dve(lambda e: e.tensor_tensor(out=zzv[3], in0=nwv, in1=ecbv, op=ALU.mult), reads=[nw_r, ecb_r], writes=[zz_r[3]])
                yb = ybank[n % 2]
                for zi, cm in ((0, "cre_e"), (1, "cre_e"), (2, "cim_e"), (3, "cim_e")):
                    pe(lambda e, zi=zi, cm=cm, yb=yb, p=n: e.matmul(ps[yb].t[0:32, :], cb[cm].t[:, p, :], zzv[zi], start=(zi == 0), stop=False),
                       reads=[cb[cm].all(), zz_r[zi]], writes=[ps[yb].all()])
                pe(lambda e, yb=yb, c_=c_, j_=j_: e.matmul(ps[yb].t[0:32, :], cb["d_e"].t[32 * j_:32 * j_ + 32, c_, :], ubv[32 * j_:32 * j_ + 32, c_, :],
                                                           start=False, stop=True, tile_position=(32 * j_, 0)),
                   reads=[cb["d_e"].all(), ubr(c_)], writes=[ps[yb].all()])
                act(lambda e, yb=yb, c_=c_, j_=j_: e.activation(out=gbv[32 * j_:32 * j_ + 32, c_, :], in_=ps[yb].t[0:32, :], func=AF.Gelu),
                    reads=[ps[yb].all()], writes=[Qr(c_)])

            stL_dma(0)
            stL_pe(0)
            for n in range(64):
                if n + 1 < 64:
                    stL_pe(n + 1)
                stM(n)
                stA(n)
                if n >= 1:
                    stZ(n - 1)
                if n + 1 < 64:
                    stL_dma(n + 1)
                stS(n)
            stZ(63)
            FR = [car.all(), ctmp.all(), wlast.all(), c511.all(), s511.all()]
            wl_re, wl_im = wlast.t[:, 0, :], wlast.t[:, 1, :]
            TT = lambda o, a, b, op: dve(lambda e: e.tensor_tensor(out=o, in0=a, in1=b, op=op), reads=FR, writes=FR)
            TT(ctmp.t[:, 0, :], c511.t[:], wl_re, ALU.mult)
            TT(ctmp.t[:, 1, :], s511.t[:], wl_im, ALU.mult)
            TT(cre, ctmp.t[:, 0, :], ctmp.t[:, 1, :], ALU.subtract)
            TT(ctmp.t[:, 2, :], s511.t[:], wl_re, ALU.mult)
            TT(ctmp.t[:, 3, :], c511.t[:], wl_im, ALU.mult)
            TT(cim, ctmp.t[:, 2, :], ctmp.t[:, 3, :], ALU.add)

        tWs = [uf(X0 + i * 4096, 1024).rearrange("p (c t) -> p c t", c=2) for i in range(2)]
        tWrs = [ur(X0 + i * 4096, 4096) for i in range(2)]
        jkv = [uf(i * 2048, 512) for i in range(4)]; jk_r = [ur(i * 2048, 2048) for i in range(4)]

        ada_slab = [16]

        def ada_full_slab():
            if ada_slab[0] < 48:
                sl = ada_slab[0]
                ada_slab[0] += 1
                for _ in ada_all([sl], [pnext() for _ in range(4)]):
                    pass

        def ssm_pre():
            sl0 = ada_slab[0]
            ada_slab[0] = min(48, sl0 + 4)
            ada_it = ada_all(range(sl0, ada_slab[0]), [3, 4, 5, 6])
            bl = [0, 1, 2, 7]
            for pi_ in range(64):
                c_, j_ = pi_ // 4, pi_ % 4
                tWv, tW_r = tWs[pi_ % 2], tWrs[pi_ % 2]
                S.dma("sp", tWv, wtabs[pi_].rearrange("c p t -> p c t"), reads=[("wtabs", pi_, pi_ + 1)], writes=[tW_r], key="tabr%d" % (pi_ % 2))
                bre, bim = bl[(2 * pi_) % 4], bl[(2 * pi_ + 1) % 4]
                for ri, b in ((0, bre), (1, bim)):
                    pe(lambda e, ri=ri, b=b, c_=c_, j_=j_: e.matmul(ps[b].t[:], Bexp.t[32 * j_:32 * j_ + 32, c_, ri, :],
                                                                  ubv[32 * j_:32 * j_ + 32, c_, :], start=True, stop=True,
                                                                  tile_position=(32 * j_, 0)),
                       reads=[Bexp.all(), ubr(c_)], writes=[ps[b].all()])
                for k, (bank, wi) in enumerate(((bre, 0), (bim, 1), (bim, 0), (bre, 1))):
                    dve(lambda e, k=k, bank=bank, wi=wi, tWv=tWv: e.tensor_tensor(out=jkv[k], in0=ps[bank].t[:], in1=tWv[:, wi, :], op=ALU.mult),
                        reads=[ps[bank].all(), tW_r], writes=[jk_r[k]])
                    act(lambda e, k=k, p=pi_: e.activation(out=jkv[k], in_=jkv[k], func=AF.Identity, accum_out=accS.t[:, k, p:p + 1]),
                        reads=[jk_r[k]], writes=[jk_r[k], accS.all()])
                if pi_ % 2 == 1:
                    next(ada_it, None)
            for _ in ada_it:
                pass
            CR = [car.all(), ctmp.all(), accS.all(), Gre.all(), Gim.all()]
            cre, cim = car.t[:, 0, :], car.t[:, 1, :]
            c0, c1, c2, c3 = [ctmp.t[:, i, :] for i in range(4)]
            TT = lambda o, a, b, op: dve(lambda e: e.tensor_tensor(out=o, in0=a, in1=b, op=op), reads=CR, writes=CR)
            TT(c0, accS.t[:, 0, :], accS.t[:, 1, :], ALU.subtract)
            TT(c1, accS.t[:, 2, :], accS.t[:, 3, :], ALU.add)
            TT(c2, Gre.t[:], cre, ALU.mult)
            TT(c0, c0, c2, ALU.add)
            TT(c2, Gim.t[:], cim, ALU.mult)
            TT(c0, c0, c2, ALU.subtract)
            TT(c3, Gre.t[:], cim, ALU.mult)
            TT(c1, c1, c3, ALU.add)
            TT(c3, Gim.t[:], cre, ALU.mult)
            TT(c1, c1, c3, ALU.add)
            dve(lambda e: e.tensor_copy(cre, c0), reads=CR, writes=CR)
            dve(lambda e: e.tensor_copy(cim, c1), reads=CR, writes=CR)

        G0 = X0
        gatev = [uf(G0 + 12288 + i * 2048, 512) for i in range(2)]; gate_r = [ur(G0 + 12288 + i * 2048, 2048) for i in range(2)]
        prodv = [uf(G0 + 16384 + i * 2048, 512) for i in range(2)]; prod_r = [ur(G0 + 16384 + i * 2048, 2048) for i in range(2)]

        def glu_and_norm():
            rhs = lambda kc: (gbv[:, kc, :], Qr(kc))
            cnt = [0]
            for slab in range(4):
                def epi(j, b, slab=slab):
                    ob = slab * 4 + j
                    q = cnt[0] % 2
                    i = cnt[0]
                    cnt[0] += 1
                    act(lambda e: e.activation(out=gatev[q], in_=ps[b].t[:], func=AF.Sigmoid, bias=sm["b_gluT"].t[:, ob:ob + 1]),
                        reads=[ps[b].all(), sm["b_gluT"].all()], writes=[gate_r[q]])
                    dve(lambda e: e.tensor_tensor(out=prodv[q], in0=gbv[:, ob, :], in1=gatev[q], op=ALU.mult),
                        reads=[Qr(ob), gate_r[q]], writes=[prod_r[q]])
                    act(lambda e: e.activation(out=sqv[q], in_=prodv[q], func=AF.Square), reads=[prod_r[q]], writes=[sqr[q]])
                    pe(lambda e: e.matmul(ps[SSB].t[:], ones.t[:], sqv[q], start=(i == 0), stop=(i == 15)),
                       reads=[ones.all(), sqr[q]], writes=[ps[SSB].all()])
                    dve(lambda e: e.tensor_copy(hb.t[:, 16 + ob, :], prodv[q]), reads=[prod_r[q]], writes=[hbr(16 + ob)])
                group_mm(dr["w_glu"], 4, slab * 512, rhs, epi)
            finish_stats(2048)
            for ob in range(16):
                dve(lambda e, ob=ob: e.scalar_tensor_tensor(out=hb.t[:, 16 + ob, :], in0=hb.t[:, 16 + ob, :], scalar=sm["ssm_gT"].t[:, ob:ob + 1],
                                                         in1=rstd, op0=ALU.mult, op1=ALU.mult),
                    reads=[hbr(16 + ob), rstd_r, sm["ssm_gT"].all()], writes=[hbr(16 + ob)])

        def resid_epi(gate, st):
            def epi_factory(slab):
                def epi(j, b):
                    ob = slab * 4 + j
                    dve(lambda e: e.scalar_tensor_tensor(out=acc.t[:, ob, :], in0=ps[b].t[:], scalar=gate[:, ob:ob + 1], in1=acc.t[:, ob, :],
                                                         op0=ALU.mult, op1=ALU.add),
                        reads=[ps[b].all(), mods.all(), accr(ob)], writes=[accr(ob)])
                    st.add(acc.t[:, ob, :], accr(ob))
                return epi
            return epi_factory

        def out_proj():
            rhs = lambda kc: (hb.t[:, kc, :], hbr(kc))
            st = Stats()
            ef = resid_epi(gt1, st)
            for slab in range(8):
                group_mm(dr["w_out"], 8, slab * 512, rhs, ef(slab))
            return st

        hidv = [ubf(i * 4096, 2048).rearrange("p (a t) -> p a t", a=4) for i in range(2)]
        hid_r = [ur(i * 4096, 4096) for i in range(2)]
        rlv = [uf(8192 + i * 2048, 512) for i in range(2)]; rl_r = [ur(8192 + i * 2048, 2048) for i in range(2)]

        def ffn():
            rhs = lambda kc: (hb.t[:, kc, :], hbr(kc))
            cnt = [0]
            st = Stats()
            for f in range(32):
                hq = f % 2

                def epi(j, b, hq=hq):
                    q = cnt[0] % 2
                    cnt[0] += 1
                    act(lambda e: e.activation(out=rlv[q], in_=ps[b].t[:], func=AF.Relu), reads=[ps[b].all()], writes=[rl_r[q]])
                    dve(lambda e: e.scalar_tensor_tensor(out=hidv[hq][:, j, :], in0=ps[b].t[:], scalar=0.0, in1=rlv[q], op0=ALU.max, op1=ALU.mult),
                        reads=[ps[b].all(), rl_r[q]], writes=[ur(hq * 4096 + j * 1024, 1024)])
                group_mm(dr["w_ff1"], 8, f * 512, rhs, epi)
                for slab in range(8):
                    s = wtile(dr["w_ff2"], f * 512, slab * 512)
                    for j in range(4):
                        ob = slab * 4 + j
                        b = pnext()
                        for a in range(4):
                            pe(lambda e, b=b, s=s, a=a, j=j, hq=hq: e.matmul(ps[b].t[:], wr.t[:, s, a, j * 128:(j + 1) * 128], hidv[hq][:, a, :],
                                                                          start=(a == 0), stop=(a == 3)),
                               reads=[wreg(s), hid_r[hq]], writes=[ps[b].all()])
                        dve(lambda e, b=b, ob=ob: e.scalar_tensor_tensor(out=acc.t[:, ob, :], in0=ps[b].t[:], scalar=gt2[:, ob:ob + 1], in1=acc.t[:, ob, :],
                                                                     op0=ALU.mult, op1=ALU.add),
                            reads=[ps[b].all(), mods.all(), accr(ob)], writes=[accr(ob)])
                        if f == 31:
                            st.add(acc.t[:, ob, :], accr(ob))
            return st

        def final_out(row0, st):
            st.finish(D)
            for c in range(KC):
                dve(lambda e, c=c: e.scalar_tensor_tensor(out=acc.t[:, c, :], in0=acc.t[:, c, :], scalar=sm["fing"].t[:, c:c + 1], in1=rstd,
                                                       op0=ALU.mult, op1=ALU.mult),
                    reads=[accr(c), rstd_r, sm["fing"].all()], writes=[accr(c)])
            cnt = 0
            for blk in range(4):
                for half in range(2):
                    q = cnt % 2
                    cnt += 1
                    for c4 in range(4):
                        b = pnext()
                        for cc in range(4):
                            c = half * 16 + c4 * 4 + cc
                            pe(lambda e, b=b, cc=cc, c=c, blk=blk: e.transpose(ps[b].t[:, cc * 128:(cc + 1) * 128], acc.t[:, c, blk * 128:(blk + 1) * 128], sm["identf"].t[:]),
                               reads=[accr(c), sm["identf"].all()], writes=[ps[b].all()])
                        if c4 % 2 == 0:
                            dve(lambda e, b=b, q=q, c4=c4: e.tensor_copy(xs[q][:, c4 * 512:(c4 + 1) * 512], ps[b].t[:]), reads=[ps[b].all()],
                                writes=[ur(q * 8192 + c4 * 2048, 2048)])
                        else:
                            act(lambda e, b=b, q=q, c4=c4: e.activation(out=xs[q][:, c4 * 512:(c4 + 1) * 512], in_=ps[b].t[:], func=AF.Identity), reads=[ps[b].all()],
                                writes=[ur(q * 8192 + c4 * 2048, 2048)])
                    S.dma("sp", out_d[row0 + blk * 128: row0 + (blk + 1) * 128, half * 2048:(half + 1) * 2048], xs[q], reads=[xsr[q]], key="os%d" % q)

        nt = NTILE if not debug else debug.get("_ntile", [NTILE])[0]
        npre = NPRE if not debug else debug.get("_npre", [NPRE])[0]
        e_cnt = [0]
        for ti in range(NPRE - npre, NPRE):
            load_x(ti * T)
            norm_mod(gs1.t, sh1)
            want_kv = (ti == NPRE - 1)
            in_proj(ti, True, want_kv)
            if want_kv:
                shift_halo()
            ssm_pre()
        while ada_slab[0] < 48:
            ada_full_slab()
        dve(lambda e: e.scalar_tensor_tensor(out=gs2.t[:], in0=sc2, scalar=1.0, in1=sm["n2g"].t[:], op0=ALU.add, op1=ALU.mult),
            reads=[mods.all(), sm["n2g"].all()], writes=[gs2.all()])
        dump("mods", mods.t[:], [mods.all()])
        dve(lambda e: e.tensor_scalar(car.t[:], car.t[:], sm["flag"].t[:, 0:1], None, op0=ALU.mult), reads=[car.all(), sm["flag"].all()], writes=[car.all()])
        for i in range(nt):
            ti = NPRE + i
            load_x(ti * T)
            norm_mod(gs1.t, sh1)
            if i == 0:
                dump("x0", acc.t[:, 0:2, :], [acc.all()])
                dump("h1", hb.t[:, 0:2, :], [hb.all()])
                dump("Bexp", Bexp.t[:, 0, :, :], [Bexp.all()])
                dump("rr", rr.t[:], [rr.all()])
                dump("phi", phi.t[:], [phi.all()])
            in_proj(ti, False, True)
            if i == 0:
                dump("q0", Qv[:, 0:2, :], [Qr(0), Qr(1)])
                dump("k0", Kb.t[:, 0:2, :], [Kb.all()])
                dump("ub0", ubv[:, 0:2, :], [ubr(0), ubr(1)])
                dump("va", Va.t[:, 1, :, :], [Va.all()])
            attention(i == 0)
            if i == 0:
                dump("att", hb.t[:, 0:2, :], [hb.all()])
            shift_halo()
            ssm()
            if i == 0:
                dump("gb0", gbv[:, 0:2, :], [Qr(0), Qr(1)])
                dump("car", car.t[:], [car.all()])
            glu_and_norm()
            if i == 0:
                dump("ssm", hb.t[:, 16:18, :], [hb.all()])
            st2 = out_proj()
            if i == 0:
                dump("x1", acc.t[:, 0:2, :], [acc.all()])
            norm_mod(gs2.t, sh2, st2)
            if i == 0:
                dump("h2", hb.t[:, 0:2, :], [hb.all()])
            st3 = ffn()
            if i == 0:
                dump("x2", acc.t[:, 0:2, :], [acc.all()])
            final_out(i * T, st3)
        S.emit()
        build.stats = S.stats
    return nc


def _prep_shared(inp):
    f = np.float32
    sh = {}
    idx = []
    for g in range(8):
        for half in range(2):
            for j in range(4):
                h = 4 * g + j
                idx.extend(range(h * 64 + half * 32, h * 64 + half * 32 + 32))
    for a in range(4):
        for half in range(2):
            for j in range(4):
                idx.extend(range(2048 + a * 64 + half * 32, 2048 + a * 64 + half * 32 + 32))
    idx.extend(range(2560, 4608))
    idx.extend(range(2304, 2560))
    idx = np.asarray(idx)
    assert idx.size == WINP
    sh["w_in"] = np.ascontiguousarray(inp["w_in"][0][:, idx])
    sh["w_ada"] = np.ascontiguousarray(inp["w_ada"][0])
    sh["w_glu"] = np.ascontiguousarray(inp["w_glu"][0])
    sh["w_out"] = np.ascontiguousarray(inp["w_out"][0])
    sh["w_ff1"] = np.ascontiguousarray(inp["w_ff1"][0])
    sh["w_ff2"] = np.ascontiguousarray(inp["w_ff2"][0])
    col = lambda v: np.ascontiguousarray(np.asarray(v, f).reshape(-1, 128).T)
    sh["b_adaT"] = col(inp["b_ada"][0])
    sh["n1g"] = col(inp["norm1_g"][0]); sh["n2g"] = col(inp["norm2_g"][0]); sh["fing"] = col(inp["final_g"])
    sh["sinks_bc"] = np.ascontiguousarray(np.broadcast_to(np.asarray(inp["sinks"][0], f)[None, :], (128, 32)))
    sh["b_gluT"] = col(inp["b_glu"][0]); sh["ssm_gT"] = col(inp["ssm_out_g"][0]); sh["attn_gT"] = col(inp["attn_out_g"][0])
    sh["identf"] = np.eye(128, dtype=f); sh["identb"] = np.eye(128, dtype=f)
    sh["iota"] = np.ascontiguousarray(np.broadcast_to(np.arange(512, dtype=f)[None, :], (128, 512)))
    lre, lim, lst = [np.asarray(inp[k][0], f) for k in ("ssm_lam_re", "ssm_lam_im", "ssm_log_step")]
    bre, bim = np.asarray(inp["ssm_b_re"][0], f), np.asarray(inp["ssm_b_im"][0], f)
    cre, cim = np.asarray(inp["ssm_c_re"][0], f), np.asarray(inp["ssm_c_im"][0], f)
    dsk = np.asarray(inp["ssm_d"][0], f)
    q = np.arange(128)
    pi_ = np.arange(64)
    Gp = 2 * pi_[None, :] + (q // 64)[:, None]
    Pp = np.broadcast_to((q % 64)[:, None], (128, 64))
    sh["lamre_p"] = np.ascontiguousarray(lre[Gp, Pp]); sh["lamim_p"] = np.ascontiguousarray(lim[Gp, Pp])
    sh["lstep_p"] = np.ascontiguousarray(lst[Gp])
    r = np.arange(128)
    cprime = np.arange(16)
    colx = np.arange(128)
    G = (8 * cprime[None, :] + 2 * (r // 32)[:, None] + ((r % 32) // 16)[:, None])[:, :, None]
    P = (colx % 64)[None, None, :]
    H = (r % 16)[:, None, None]
    same = ((colx // 64)[None, None, :] == ((r % 32) // 16)[:, None, None])
    Gb = np.broadcast_to(G, (128, 16, 128)); Pb = np.broadcast_to(P, (128, 16, 128)); Hb = np.broadcast_to(H, (128, 16, 128))
    sh["lamre_e"] = np.ascontiguousarray(lre[Gb, Pb]).reshape(128, 2048)
    sh["lamim_e"] = np.ascontiguousarray(lim[Gb, Pb]).reshape(128, 2048)
    sh["lstep_e"] = np.ascontiguousarray(lst[Gb]).reshape(128, 2048)
    sh["bre_e"] = np.where(same, bre[Gb, Pb, Hb], f(0)).astype(f).reshape(128, 2048)
    sh["bim_e"] = np.where(same, bim[Gb, Pb, Hb], f(0)).astype(f).reshape(128, 2048)
    c32 = np.arange(32)
    Gc = np.broadcast_to((2 * pi_[None, :, None] + (q // 64)[:, None, None]), (128, 64, 32))
    Hc = np.broadcast_to((c32 % 16)[None, None, :], (128, 64, 32))
    Pc = np.broadcast_to((q % 64)[:, None, None], (128, 64, 32))
    samec = ((c32 // 16)[None, None, :] == (q // 64)[:, None, None])
    sh["cre_e"] = np.where(samec, cre[Gc, Hc, Pc], f(0)).astype(f)
    sh["cim_e"] = np.where(samec, cim[Gc, Hc, Pc], f(0)).astype(f)
    ch = (8 * cprime[None, :] + 2 * (r // 32)[:, None]) * 16 + (r % 32)[:, None]
    dd = np.where((c32[None, None, :] == (r % 32)[:, None, None]), dsk[ch][:, :, None], f(0)).astype(f)
    sh["d_e"] = np.ascontiguousarray(dd)
    j = np.arange(128)[:, None]; i = np.arange(128)[None, :]
    mprev = (j > i).astype(f); mcur = (j <= i).astype(f)
    sh["mask_n"] = np.ascontiguousarray(np.concatenate([mprev, mcur], axis=1))
    return sh, (mprev, mcur)


def _rope_tables(base):
    f = np.float32
    half = 32
    inv_freq = (f(10000.0) ** (-(np.arange(half, dtype=f) / f(half)))).astype(f)
    pos = (np.arange(4096, dtype=np.int64) + base).astype(f)
    ang = (pos[None, :] * inv_freq[np.arange(128) % 32][:, None]).astype(f)
    return np.cos(ang).astype(f), np.sin(ang).astype(f)


_NC_CACHE = {}


def kernel(**inputs):
    inp = {k: np.asarray(v) for k, v in inputs.items()}
    f = np.float32
    sh, (mprev, mcur) = _prep_shared(inp)
    x = np.asarray(inp["x"], f)
    c = np.asarray(inp["c"], f)
    in_maps = []
    for core in range(8):
        b, half = core // 2, core % 2
        m = dict(sh)
        if half == 0:
            xcat = np.concatenate([np.zeros((2048, D), f), x[b, 0:2048]], axis=0)
        else:
            xcat = x[b]
        m["xcat"] = np.ascontiguousarray(xcat)
        m["cT"] = np.ascontiguousarray(c[b].reshape(32, 128).T)
        m["flag"] = np.full((128, 1), float(half), f)
        m["mask_f"] = np.ascontiguousarray(np.concatenate([mprev if half == 1 else np.zeros_like(mprev), mcur], axis=1))
        rc, rs = _rope_tables(half * 2048 - 2048)
        m["ropec"], m["ropes"] = rc, rs
        in_maps.append(m)
    if "nc" not in _NC_CACHE:
        _NC_CACHE["nc"] = build()
    nc = _NC_CACHE["nc"]
    res = run_bass_kernel_spmd(nc, in_maps, core_ids=list(range(8)))
    out = np.empty((4, 4096, D), f)
    for core in range(8):
        b, half = core // 2, core % 2
        out[b, half * 2048:(half + 1) * 2048] = res.results[core]["out"]
    return out
```

```python
import contextlib
import math
import numpy as np
import concourse.bass as bass
import concourse.mybir as mybir
from concourse.bass_utils import run_bass_kernel_spmd

F32 = mybir.dt.float32
BF16 = mybir.dt.bfloat16
AF = mybir.ActivationFunctionType
ALU = mybir.AluOpType

D = 4096
KC = 32
T = 512
NTILE = 4
NPRE = 4
DFF = 16384
EPS = 1e-6
WINP = 5376
NW = 4
MAGIC = 12582912.0
TWO_PI = 2.0 * math.pi


class _Op:
    __slots__ = ("eng", "fn", "deps", "is_dma", "sem", "semval", "needs_inc", "incval")

    def __init__(self, eng, fn):
        self.eng = eng
        self.fn = fn
        self.deps = []
        self.is_dma = False
        self.sem = None
        self.semval = 0
        self.needs_inc = False
        self.incval = 0


class Sched:
    ENGS = ("pe", "act", "dve", "pool", "sp")

    def __init__(self, nc, es):
        self.nc = nc
        self.es = es
        self.ops = []
        self.W = {}
        self.R = {}
        self.dsems = {}
        self.dcount = {}

    def _deps(self, op, reads, writes):
        deps = op.deps
        for (k, lo, hi) in reads:
            for (l2, h2, o2) in self.W.get(k, ()):
                if l2 < hi and lo < h2:
                    deps.append(o2)
            self.R.setdefault(k, []).append((lo, hi, op))
        for (k, lo, hi) in writes:
            wl = self.W.get(k, [])
            rl = self.R.get(k, [])
            for (l2, h2, o2) in wl:
                if l2 < hi and lo < h2:
                    deps.append(o2)
            for (l2, h2, o2) in rl:
                if l2 < hi and lo < h2 and o2 is not op:
                    deps.append(o2)
            self.W[k] = [t for t in wl if not (lo <= t[0] and t[1] <= hi)] + [(lo, hi, op)]
            self.R[k] = [t for t in rl if not (lo <= t[0] and t[1] <= hi) or t[2] is op]

    def op(self, eng, fn, reads=(), writes=()):
        o = _Op(eng, fn)
        self._deps(o, reads, writes)
        self.ops.append(o)
        return o

    def dma(self, eng, out, in_, reads=(), writes=(), key=None):
        o = _Op(eng, None)
        o.is_dma = True
        if key not in self.dsems:
            self.dsems[key] = self.es.enter_context(self.nc.semaphore("d_" + key))
            self.dcount[key] = 0
        self.dcount[key] += 16
        o.sem = self.dsems[key]
        o.semval = self.dcount[key]
        o.fn = lambda e, out=out, in_=in_: e.dma_start(out=out, in_=in_)
        self._deps(o, reads, writes)
        self.ops.append(o)
        return o

    def emit(self):
        nc = self.nc
        esem = {e: self.es.enter_context(nc.semaphore("e_" + e)) for e in self.ENGS}
        fin = _Op("sp", None)
        last = {}
        for o in self.ops:
            if o.is_dma:
                last[("d", id(o.sem))] = o
            else:
                last[("e", o.eng)] = o
        fin.deps = list(last.values())
        self.ops.append(fin)
        idx = {id(o): i for i, o in enumerate(self.ops)}
        for o in self.ops:
            best = {}
            for d in o.deps:
                k = ("d", id(d.sem)) if d.is_dma else ("e", d.eng)
                if k not in best or idx[id(d)] > idx[id(best[k])]:
                    best[k] = d
            o.deps = list(best.values())
        for o in self.ops:
            for d in o.deps:
                if not d.is_dma:
                    if d.eng == "pe" and o.eng == "pe" and not o.is_dma:
                        continue
                    d.needs_inc = True
        cnt = {e: 0 for e in self.ENGS}
        for o in self.ops:
            if (not o.is_dma) and o.needs_inc:
                cnt[o.eng] += 1
                o.incval = cnt[o.eng]
        per = {e: [] for e in self.ENGS}
        for o in self.ops:
            per[o.eng].append(o)
        self.stats = {e: len(per[e]) for e in self.ENGS}
        self.stats["incs"] = dict(cnt)

        def run(engname, e):
            waited = {}
            for o in per[engname]:
                for d in o.deps:
                    if d.is_dma:
                        s, v = d.sem, d.semval
                    else:
                        if d.eng == "pe" and engname == "pe" and not o.is_dma:
                            continue
                        s, v = esem[d.eng], d.incval
                    kk = id(s)
                    if waited.get(kk, 0) >= v:
                        continue
                    waited[kk] = v
                    e.wait_ge(s, v)
                if o.fn is None:
                    continue
                ins = o.fn(e)
                if o.is_dma:
                    ins.then_inc(o.sem, 16)
                elif o.needs_inc:
                    ins.then_inc(esem[engname], 1)

        with nc.Block() as block:
            @block.tensor
            def _(e):
                run("pe", e)

            @block.scalar
            def _(e):
                run("act", e)

            @block.vector
            def _(e):
                run("dve", e)

            @block.gpsimd
            def _(e):
                run("pool", e)

            @block.sync
            def _(e):
                run("sp", e)


class Buf:
    def __init__(self, nc, es, name, shape, dtype, psum=False):
        self.name = name
        if psum:
            self.t = es.enter_context(nc.psum_tensor("p_" + name, shape, dtype))
        else:
            self.t = es.enter_context(nc.sbuf_tensor("s_" + name, shape, dtype))
        n = 1
        for s in shape[1:]:
            n *= s
        self.n = n

    def all(self):
        return (self.name, 0, self.n)

    def r(self, lo, hi):
        return (self.name, lo, hi)


SMALL_IN = [
    ("cT", [128, 32]), ("b_adaT", [128, 192]), ("n1g", [128, 32]), ("n2g", [128, 32]),
    ("fing", [128, 32]), ("sinks_bc", [128, 32]), ("flag", [128, 1]),
    ("lamre_p", [128, 64]), ("lamim_p", [128, 64]), ("lstep_p", [128, 64]),
    ("b_gluT", [128, 16]), ("ssm_gT", [128, 16]), ("attn_gT", [128, 16]), ("identf", [128, 128]),
]
CAST_IN = [
    ("cre_e", [128, 64, 32]), ("cim_e", [128, 64, 32]), ("d_e", [128, 16, 32]),
    ("mask_n", [128, 256]), ("mask_f", [128, 256]), ("identb", [128, 128]),
]
BIG_IN = [
    ("lamre_e", [128, 2048]), ("lamim_e", [128, 2048]), ("lstep_e", [128, 2048]),
    ("bre_e", [128, 2048]), ("bim_e", [128, 2048]),
]


def build(debug=None):
    nc = bass.Bass("TRN2", target_bir_lowering=False)
    dr = {}

    def din(name, shape):
        dr[name] = nc.dram_tensor(name, shape, F32, kind="ExternalInput").ap()

    din("xcat", [4096, D])
    for n, s in SMALL_IN + CAST_IN + BIG_IN:
        din(n, s)
    din("iota", [128, 512])
    din("ropec", [128, 4096])
    din("ropes", [128, 4096])
    din("w_ada", [D, 6 * D])
    din("w_in", [D, WINP])
    din("w_glu", [2048, 2048])
    din("w_out", [D, D])
    din("w_ff1", [D, DFF])
    din("w_ff2", [DFF, D])
    out_d = nc.dram_tensor("out", [2048, D], F32, kind="ExternalOutput").ap()
    tabs = nc.dram_tensor("tabs", [64, 2, 128, 512], F32, kind="Internal").ap()
    wtabs = nc.dram_tensor("wtabs", [64, 2, 128, 512], F32, kind="Internal").ap()

    with contextlib.ExitStack() as es:
        S = Sched(nc, es)
        B = lambda name, shape, dt=F32, psum=False: Buf(nc, es, name, shape, dt, psum)
        acc = B("acc", [128, KC, T])
        hb = B("hb", [128, KC, T], BF16)
        wr = B("wr", [128, NW, 4, 512], BF16)
        Kb = B("Kb", [128, 8, 640], BF16)
        Va = B("Va", [128, 5, 4, 65], BF16)
        U = B("U", [128, 13312])
        ps = [B("ps%d" % i, [128, 512], F32, psum=True) for i in range(8)]
        sm = {n: B(n, s) for n, s in SMALL_IN}
        cb = {n: B(n, s, BF16) for n, s in CAST_IN}
        Bexp = B("Bexp", [128, 16, 2, 128], BF16)
        mods = B("mods", [128, 192])
        gs1 = B("gs1", [128, 32]); gs2 = B("gs2", [128, 32])
        cact = B("cact", [128, 32], BF16)
        esink = B("esink", [128, 32])
        ones = B("ones", [128, 128])
        rr = B("rr", [128, 64]); cos1 = B("cos1", [128, 64]); sin1 = B("sin1", [128, 64]); phi = B("phi", [128, 64])
        car = B("car", [128, 2, 64]); wini = B("wini", [128, 2, 64]); ctmp = B("ctmp", [128, 4, 64])
        halfpi = B("halfpi", [128, 1])
        magic = B("magic", [128, 1]); magic2 = B("magic2", [128, 1])
        sml = B("sml", [128, 16])

        def uf(lo_b, n):
            return U.t[:, lo_b // 4: lo_b // 4 + n]

        def ubf(lo_b, n):
            return U.t[:, lo_b // 4: lo_b // 4 + n // 2].bitcast(BF16)

        def ur(lo_b, nbytes):
            return ("U", lo_b // 4, (lo_b + nbytes) // 4)

        K1 = 1024
        Qv = ubf(0, 16 * 512).rearrange("p (c t) -> p c t", c=16)
        gbv = Qv
        ubv = ubf(16 * K1, 16 * 512).rearrange("p (c t) -> p c t", c=16)

        def Qr(c):
            return ur(c * 1024, 1024)

        def ubr(c):
            return ur(16 * K1 + c * 1024, 1024)

        X0 = 32 * K1

        pctr = [0]

        def pnext():
            b = pctr[0] % 7
            pctr[0] += 1
            return b

        wctr = [0]

        def wtile(Wd, r0, c0, ncols=512):
            s = wctr[0] % NW
            wctr[0] += 1
            S.dma("pool", wr.t[:, s, :, 0:ncols],
                  Wd[r0:r0 + 512, c0:c0 + ncols].rearrange("(a p) n -> p a n", p=128),
                  writes=[wr.r(s * 2048, (s + 1) * 2048)], key="w%d" % s)
            return s

        def wreg(s):
            return wr.r(s * 2048, (s + 1) * 2048)

        def dve(fn, reads=(), writes=()):
            return S.op("dve", fn, reads, writes)

        def act(fn, reads=(), writes=()):
            return S.op("act", fn, reads, writes)

        def pe(fn, reads=(), writes=()):
            return S.op("pe", fn, reads, writes)

        def group_mm(Wd, nk4, c0, rhs, epi, nj=4):
            banks = [pnext() for _ in range(nj)]
            nk = nk4 * 4
            for k4 in range(nk4):
                s = wtile(Wd, k4 * 512, c0, nj * 128)
                for a in range(4):
                    kc = k4 * 4 + a
                    rap, rreg = rhs(kc)
                    for j in range(nj):
                        pe(lambda e, b=banks[j], s=s, a=a, j=j, rap=rap, kc=kc:
                           e.matmul(ps[b].t[:], wr.t[:, s, a, j * 128:(j + 1) * 128], rap,
                                    start=(kc == 0), stop=(kc == nk - 1)),
                           reads=[wreg(s), rreg], writes=[ps[banks[j]].all()])
            for j in range(nj):
                epi(j, banks[j])

        def dump(name, ap, reads):
            if debug and name in debug:
                dd = nc.dram_tensor("dbg_" + name, list(ap.shape), ap.dtype, kind="ExternalOutput").ap()
                S.dma("sp", dd, ap, reads=reads, key="dbg_" + name)

        for n, s in SMALL_IN:
            S.dma("sp", sm[n].t[:], dr[n], writes=[sm[n].all()], key="ld_" + n)
        for n, s in CAST_IN:
            S.dma("pool", cb[n].t[:], dr[n], writes=[cb[n].all()], key="ld_" + n)
        dve(lambda e: e.memset(ones.t[:], 1.0), writes=[ones.all()])
        dve(lambda e: e.memset(halfpi.t[:], math.pi / 2), writes=[halfpi.all()])
        dve(lambda e: e.memset(magic.t[:], MAGIC), writes=[magic.all()])
        dve(lambda e: e.memset(magic2.t[:], -MAGIC), writes=[magic2.all()])
        dve(lambda e: e.memset(Va.t[:], 0.0), writes=[Va.all()])
        dve(lambda e: e.memset(Va.t[:, :, :, 64:65], 1.0), writes=[Va.all()])
        dve(lambda e: e.memset(Kb.t[:], 0.0), writes=[Kb.all()])
        dve(lambda e: e.memset(car.t[:], 0.0), writes=[car.all()])
        act(lambda e: e.activation(out=cact.t[:], in_=sm["cT"].t[:], func=AF.Silu),
            reads=[sm["cT"].all()], writes=[cact.all()])
        act(lambda e: e.activation(out=esink.t[:], in_=sm["sinks_bc"].t[:], func=AF.Exp),
            reads=[sm["sinks_bc"].all()], writes=[esink.all()])

        def ada_all(slabs, banks):
            for slab in slabs:
                for k4 in range(8):
                    s = wtile(dr["w_ada"], k4 * 512, slab * 512)
                    for a in range(4):
                        kc = k4 * 4 + a
                        for j in range(4):
                            pe(lambda e, s=s, a=a, j=j, kc=kc, b=banks[j]:
                               e.matmul(ps[b].t[:, 0:1], wr.t[:, s, a, j * 128:(j + 1) * 128],
                                        cact.t[:, kc:kc + 1], start=(kc == 0), stop=(kc == 31)),
                               reads=[wreg(s), cact.all()], writes=[ps[banks[j]].all()])
                    if k4 == 7:
                        for j in range(4):
                            col = 4 * slab + j
                            dve(lambda e, b=banks[j], col=col: e.tensor_tensor(out=mods.t[:, col:col + 1], in0=ps[b].t[:, 0:1],
                                                                              in1=sm["b_adaT"].t[:, col:col + 1], op=ALU.add),
                                reads=[ps[banks[j]].all(), sm["b_adaT"].all()], writes=[mods.r(col, col + 1)])
                    yield

        sh1 = mods.t[:, 0:32]; sc1 = mods.t[:, 32:64]; gt1 = mods.t[:, 64:96]
        sh2 = mods.t[:, 96:128]; sc2 = mods.t[:, 128:160]; gt2 = mods.t[:, 160:192]

        def sincos(ang, n, tmpA, tmpB, sin_out, cos_out, rd, wrs):
            act(lambda e: e.activation(out=tmpA, in_=ang, func=AF.Identity, scale=1.0 / TWO_PI, bias=magic.t[:]), reads=rd + [magic.all()], writes=wrs)
            act(lambda e: e.activation(out=tmpB, in_=tmpA, func=AF.Identity, scale=1.0, bias=magic2.t[:]), reads=wrs + [magic2.all()], writes=wrs)
            dve(lambda e: e.scalar_tensor_tensor(out=ang, in0=tmpB, scalar=-TWO_PI, in1=ang, op0=ALU.mult, op1=ALU.add), reads=rd + wrs, writes=rd)
            dve(lambda e: e.tensor_scalar(ang, ang, 3.1415925, -3.1415925, op0=ALU.min, op1=ALU.max), reads=rd, writes=rd)
            act(lambda e: e.activation(out=sin_out, in_=ang, func=AF.Sin), reads=rd, writes=wrs)
            dve(lambda e: e.scalar_tensor_tensor(out=tmpA, in0=ang, scalar=-1.0, in1=ang, op0=ALU.mult, op1=ALU.max), reads=rd, writes=wrs)
            act(lambda e: e.activation(out=cos_out, in_=tmpA, func=AF.Sin, bias=halfpi.t[:], scale=-1.0),
                reads=wrs + [halfpi.all()], writes=wrs)

        def sincos_g(ang, tmpA, tmpB, sin_out, cos_out, rd, wrs):
            act(lambda e: e.activation(out=tmpA, in_=ang, func=AF.Identity, scale=1.0 / TWO_PI, bias=magic.t[:]), reads=rd + [magic.all()], writes=wrs)
            yield
            act(lambda e: e.activation(out=tmpB, in_=tmpA, func=AF.Identity, scale=1.0, bias=magic2.t[:]), reads=wrs + [magic2.all()], writes=wrs)
            yield
            dve(lambda e: e.scalar_tensor_tensor(out=ang, in0=tmpB, scalar=-TWO_PI, in1=ang, op0=ALU.mult, op1=ALU.add), reads=rd + wrs, writes=rd)
            yield
            dve(lambda e: e.tensor_scalar(ang, ang, 3.1415925, -3.1415925, op0=ALU.min, op1=ALU.max), reads=rd, writes=rd)
            yield
            act(lambda e: e.activation(out=sin_out, in_=ang, func=AF.Sin), reads=rd, writes=wrs)
            yield
            dve(lambda e: e.scalar_tensor_tensor(out=tmpA, in0=ang, scalar=-1.0, in1=ang, op0=ALU.mult, op1=ALU.max), reads=rd, writes=wrs)
            yield
            act(lambda e: e.activation(out=cos_out, in_=tmpA, func=AF.Sin, bias=halfpi.t[:], scale=-1.0),
                reads=wrs + [halfpi.all()], writes=wrs)
            yield

        av = acc.t[:].rearrange("p c t -> p (c t)")

        def a_(i):
            return av[:, i * 2048:(i + 1) * 2048]

        AR = [acc.all()]
        for i, n in enumerate(["lamre_e", "lamim_e", "lstep_e", "bre_e", "bim_e"]):
            S.dma("sp", a_(i), dr[n], writes=AR, key="ld_big")
        LRE, LIM, LST, BRE, BIM, T5, T6, T7 = [a_(i) for i in range(8)]
        act(lambda e: e.activation(out=LST, in_=LST, func=AF.Exp), reads=AR, writes=AR)
        dve(lambda e: e.tensor_tensor(out=T5, in0=LIM, in1=LST, op=ALU.mult), reads=AR, writes=AR)
        dve(lambda e: e.tensor_tensor(out=LST, in0=LRE, in1=LST, op=ALU.mult), reads=AR, writes=AR)
        act(lambda e: e.activation(out=LST, in_=LST, func=AF.Exp), reads=AR, writes=AR)
        hbf = hb.t[:].rearrange("p c t -> p (c t)").bitcast(F32)
        HR = [hb.all()]
        sincos(T5, 2048, hbf[:, 0:2048], hbf[:, 2048:4096], T6, T7, AR, HR + AR)
        dve(lambda e: e.tensor_tensor(out=T6, in0=T6, in1=LST, op=ALU.mult), reads=AR, writes=AR)
        dve(lambda e: e.tensor_tensor(out=T7, in0=T7, in1=LST, op=ALU.mult), reads=AR, writes=AR)
        dve(lambda e: e.tensor_scalar(T7, T7, -1.0, None, op0=ALU.add), reads=AR, writes=AR)
        h0, h1, h2, h3 = [hbf[:, i * 2048:(i + 1) * 2048] for i in range(4)]
        dve(lambda e: e.tensor_tensor(out=h0, in0=LRE, in1=LRE, op=ALU.mult), reads=AR, writes=HR)
        dve(lambda e: e.tensor_tensor(out=h1, in0=LIM, in1=LIM, op=ALU.mult), reads=AR, writes=HR)
        dve(lambda e: e.tensor_tensor(out=h0, in0=h0, in1=h1, op=ALU.add), reads=HR, writes=HR)
        dve(lambda e: e.reciprocal(h0, h0), reads=HR, writes=HR)
        dve(lambda e: e.tensor_tensor(out=h1, in0=T7, in1=LRE, op=ALU.mult), reads=AR, writes=HR)
        dve(lambda e: e.tensor_tensor(out=h3, in0=T6, in1=LIM, op=ALU.mult), reads=AR, writes=HR)
        dve(lambda e: e.tensor_tensor(out=h1, in0=h1, in1=h3, op=ALU.add), reads=HR, writes=HR)
        dve(lambda e: e.tensor_tensor(out=h1, in0=h1, in1=h0, op=ALU.mult), reads=HR, writes=HR)
        dve(lambda e: e.tensor_tensor(out=h2, in0=T6, in1=LRE, op=ALU.mult), reads=AR, writes=HR)
        dve(lambda e: e.tensor_tensor(out=h3, in0=T7, in1=LIM, op=ALU.mult), reads=AR, writes=HR)
        dve(lambda e: e.tensor_tensor(out=h2, in0=h2, in1=h3, op=ALU.subtract), reads=HR, writes=HR)
        dve(lambda e: e.tensor_tensor(out=h2, in0=h2, in1=h0, op=ALU.mult), reads=HR, writes=HR)
        dve(lambda e: e.tensor_tensor(out=T5, in0=h1, in1=BRE, op=ALU.mult), reads=AR + HR, writes=AR)
        dve(lambda e: e.tensor_tensor(out=h3, in0=h2, in1=BIM, op=ALU.mult), reads=AR + HR, writes=HR)
        dve(lambda e: e.tensor_tensor(out=Bexp.t[:, :, 0, :], in0=T5.rearrange("p (c n) -> p c n", c=16),
                                      in1=h3.rearrange("p (c n) -> p c n", c=16), op=ALU.subtract),
            reads=AR + HR, writes=[Bexp.all()])
        dve(lambda e: e.tensor_tensor(out=T5, in0=h1, in1=BIM, op=ALU.mult), reads=AR + HR, writes=AR)
        dve(lambda e: e.tensor_tensor(out=h3, in0=h2, in1=BRE, op=ALU.mult), reads=AR + HR, writes=HR)
        dve(lambda e: e.tensor_tensor(out=Bexp.t[:, :, 1, :], in0=T5.rearrange("p (c n) -> p c n", c=16),
                                      in1=h3.rearrange("p (c n) -> p c n", c=16), op=ALU.add),
            reads=AR + HR, writes=[Bexp.all()])

        lnr = B("lnr", [128, 64]); nphi = B("nphi", [128, 64]); phi511 = B("phi511", [128, 64])
        nlnr = B("nlnr", [128, 64]); lnr511 = B("lnr511", [128, 64]); Gre = B("Gre", [128, 64]); Gim = B("Gim", [128, 64])
        accS = B("accS", [128, 4, 64])
        PR = [rr.all(), cos1.all(), sin1.all(), phi.all(), ctmp.all(), lnr.all(), nphi.all(), phi511.all(), nlnr.all(), lnr511.all(), Gre.all(), Gim.all()]
        st_p = ctmp.t[:, 0, :]
        act(lambda e: e.activation(out=st_p, in_=sm["lstep_p"].t[:], func=AF.Exp), reads=[sm["lstep_p"].all()], writes=PR)
        dve(lambda e: e.tensor_tensor(out=lnr.t[:], in0=sm["lamre_p"].t[:], in1=st_p, op=ALU.mult), reads=PR + [sm["lamre_p"].all()], writes=PR)
        act(lambda e: e.activation(out=rr.t[:], in_=lnr.t[:], func=AF.Exp), reads=PR, writes=PR)
        dve(lambda e: e.tensor_tensor(out=phi.t[:], in0=sm["lamim_p"].t[:], in1=st_p, op=ALU.mult), reads=PR + [sm["lamim_p"].all()], writes=PR)
        sincos(phi.t[:], 64, ctmp.t[:, 1, :], ctmp.t[:, 2, :], sin1.t[:], cos1.t[:], PR, PR)

        dve(lambda e: e.tensor_scalar(nphi.t[:], phi.t[:], -1.0, None, op0=ALU.mult), reads=PR, writes=PR)
        dve(lambda e: e.tensor_scalar(phi511.t[:], phi.t[:], 511.0, None, op0=ALU.mult), reads=PR, writes=PR)
        dve(lambda e: e.tensor_scalar(nlnr.t[:], lnr.t[:], -1.0, None, op0=ALU.mult), reads=PR, writes=PR)
        dve(lambda e: e.tensor_scalar(lnr511.t[:], lnr.t[:], 511.0, None, op0=ALU.mult), reads=PR, writes=PR)
        dve(lambda e: e.tensor_scalar(ctmp.t[:, 0, :], phi.t[:], 512.0, None, op0=ALU.mult), reads=PR, writes=PR)
        sincos(ctmp.t[:, 0, :], 64, ctmp.t[:, 1, :], ctmp.t[:, 2, :], Gim.t[:], Gre.t[:], PR, PR)
        act(lambda e: e.activation(out=ctmp.t[:, 3, :], in_=lnr.t[:], func=AF.Exp, scale=512.0), reads=PR, writes=PR)
        dve(lambda e: e.tensor_tensor(out=Gre.t[:], in0=Gre.t[:], in1=ctmp.t[:, 3, :], op=ALU.mult), reads=PR, writes=PR)
        dve(lambda e: e.tensor_tensor(out=Gim.t[:], in0=Gim.t[:], in1=ctmp.t[:, 3, :], op=ALU.mult), reads=PR, writes=PR)

        c511 = B("c511", [128, 64]); s511 = B("s511", [128, 64]); wlast = B("wlast", [128, 2, 64])
        PR = PR + [c511.all(), s511.all()]
        dve(lambda e: e.tensor_scalar(ctmp.t[:, 0, :], phi.t[:], 511.0, None, op0=ALU.mult), reads=PR, writes=PR)
        sincos(ctmp.t[:, 0, :], 64, ctmp.t[:, 1, :], ctmp.t[:, 2, :], s511.t[:], c511.t[:], PR, PR)
        iotv = uf(X0 + 12288, 512); iot_r = ur(X0 + 12288, 2048)
        revv = uf(X0 + 14336, 512); rev_r = ur(X0 + 14336, 2048)
        S.dma("sp", iotv, dr["iota"], writes=[iot_r], key="ld_iota")
        dve(lambda e: e.tensor_scalar(revv, iotv, -1.0, 511.0, op0=ALU.mult, op1=ALU.add), reads=[iot_r], writes=[rev_r])
        NBT = 4
        tbb = [av[:, i * 2048:(i + 1) * 2048] for i in range(6)]
        v3 = lambda ap: ap.rearrange("p (q t) -> p q t", q=NBT)
        bc_t = lambda ap: ap.unsqueeze(1).to_broadcast([128, NBT, 512])
        hbs = [hbf[:, i * 2048:(i + 1) * 2048] for i in range(4)]

        def gen_E_batch(bt):
            p0 = bt * NBT
            bc_p = lambda buf: buf.t[:, p0:p0 + NBT].unsqueeze(2).to_broadcast([128, NBT, 512])
            dve(lambda e, a=bc_t(iotv), b=bc_p(nphi): e.tensor_tensor(out=v3(hbs[0]), in0=a, in1=b, op=ALU.mult), reads=[iot_r] + PR, writes=HR)
            yield
            yield from sincos_g(hbs[0], hbs[1], hbs[2], hbs[3], hbs[2], HR, HR)
            S.dma("sp", tabs[p0:p0 + NBT, 1].rearrange("q p t -> p q t"), v3(hbs[3]), reads=HR, writes=[("tabs", p0, p0 + NBT)], key="tabw")
            S.dma("sp", tabs[p0:p0 + NBT, 0].rearrange("q p t -> p q t"), v3(hbs[2]), reads=HR, writes=[("tabs", p0, p0 + NBT)], key="tabw")
            yield

        def gen_W_batch(bt):
            p0 = bt * NBT
            bc_p = lambda buf: buf.t[:, p0:p0 + NBT].unsqueeze(2).to_broadcast([128, NBT, 512])
            dve(lambda e, a=bc_t(revv), b=bc_p(phi): e.tensor_tensor(out=v3(tbb[0]), in0=a, in1=b, op=ALU.mult), reads=[rev_r] + PR, writes=AR)
            yield
            dve(lambda e, a=bc_t(revv), b=bc_p(lnr): e.tensor_tensor(out=v3(tbb[3]), in0=a, in1=b, op=ALU.mult), reads=[rev_r] + PR, writes=AR)
            yield
            act(lambda e: e.activation(out=tbb[3], in_=tbb[3], func=AF.Exp), reads=AR, writes=AR)
            yield
            yield from sincos_g(tbb[0], tbb[1], tbb[2], tbb[4], tbb[5], AR, AR)
            dve(lambda e: e.tensor_tensor(out=tbb[1], in0=tbb[5], in1=tbb[3], op=ALU.mult), reads=AR, writes=AR)
            yield
            dve(lambda e: e.tensor_tensor(out=tbb[2], in0=tbb[4], in1=tbb[3], op=ALU.mult), reads=AR, writes=AR)
            yield
            S.dma("sp", wtabs[p0:p0 + NBT, 0].rearrange("q p t -> p q t"), v3(tbb[1]), reads=AR, writes=[("wtabs", p0, p0 + NBT)], key="wtabw")
            S.dma("sp", wtabs[p0:p0 + NBT, 1].rearrange("q p t -> p q t"), v3(tbb[2]), reads=AR, writes=[("wtabs", p0, p0 + NBT)], key="wtabw")
            yield

        def interleave(*gens):
            gens = list(gens)
            while gens:
                for g_ in list(gens):
                    try:
                        next(g_)
                    except StopIteration:
                        gens.remove(g_)

        for i in range(16):
            for _ in ada_all(range(i, i + 1), [0, 1, 2, 3]):
                pass
            interleave(gen_W_batch(i), gen_E_batch(i))
        pctr[0] = 4
        dve(lambda e: e.scalar_tensor_tensor(out=gs1.t[:], in0=sc1, scalar=1.0, in1=sm["n1g"].t[:], op0=ALU.add, op1=ALU.mult),
            reads=[mods.all(), sm["n1g"].all()], writes=[gs1.all()])

        sqv = [uf(X0 + i * 2048, 512) for i in range(2)]
        sqr = [ur(X0 + i * 2048, 2048) for i in range(2)]
        rstd = uf(X0 + 4096, 512); rstd_r = ur(X0 + 4096, 2048)
        ntv = [uf(X0 + 6144 + i * 2048, 512) for i in range(2)]
        ntr = [ur(X0 + 6144 + i * 2048, 2048) for i in range(2)]
        s2v = uf(X0 + 10240, 512); s2_r = ur(X0 + 10240, 2048)
        sq4 = [uf(X0 + 12288 + i * 2048, 512) for i in range(4)]; sq4_r = [ur(X0 + 12288 + i * 2048, 2048) for i in range(4)]
        sqc = [0]

        class Stats:
            def __init__(self):
                self.i = 0

            def add(self, ap, reg):
                q = sqc[0] % 4
                sqc[0] += 1
                if self.i == 0:
                    act(lambda e: e.activation(out=s2v, in_=ap, func=AF.Square), reads=[reg], writes=[s2_r])
                else:
                    act(lambda e: e.activation(out=sq4[q], in_=ap, func=AF.Square), reads=[reg], writes=[sq4_r[q]])
                    dve(lambda e: e.tensor_tensor(out=s2v, in0=s2v, in1=sq4[q], op=ALU.add), reads=[s2_r, sq4_r[q]], writes=[s2_r])
                self.i += 1

            def finish(self, nfeat):
                pe(lambda e: e.matmul(ps[SSB].t[:], ones.t[:], s2v, start=True, stop=True), reads=[ones.all(), s2_r], writes=[ps[SSB].all()])
                finish_stats(nfeat)

        SSB = 7

        def accr(c):
            return acc.r(c * T, (c + 1) * T)

        def hbr(c):
            return hb.r(c * T, (c + 1) * T)

        def rms_stats(chunks, nfeat):
            n = len(chunks)
            for i, (ap, reg) in enumerate(chunks):
                q = i % 2
                act(lambda e, ap=ap, q=q: e.activation(out=sqv[q], in_=ap, func=AF.Square), reads=[reg], writes=[sqr[q]])
                pe(lambda e, q=q, i=i: e.matmul(ps[SSB].t[:], ones.t[:], sqv[q], start=(i == 0), stop=(i == n - 1)),
                   reads=[ones.all(), sqr[q]], writes=[ps[SSB].all()])
            finish_stats(nfeat)

        def finish_stats(nfeat):
            dve(lambda e: e.tensor_scalar(rstd, ps[SSB].t[:], 1.0 / nfeat, EPS, op0=ALU.mult, op1=ALU.add),
                reads=[ps[SSB].all()], writes=[rstd_r])
            act(lambda e: e.activation(out=rstd, in_=rstd, func=AF.Sqrt), reads=[rstd_r], writes=[rstd_r])
            dve(lambda e: e.reciprocal(rstd, rstd), reads=[rstd_r], writes=[rstd_r])

        def norm_mod(gs, sh, st=None):
            if st is None:
                rms_stats([(acc.t[:, c, :], accr(c)) for c in range(KC)], D)
            else:
                st.finish(D)
            for c in range(KC):
                q = c % 2
                dve(lambda e, c=c, q=q: e.tensor_tensor(out=ntv[q], in0=acc.t[:, c, :], in1=rstd, op=ALU.mult),
                    reads=[accr(c), rstd_r], writes=[ntr[q]])
                act(lambda e, c=c, q=q: e.activation(out=hb.t[:, c, :], in_=ntv[q], func=AF.Identity,
                                                    bias=sh[:, c:c + 1], scale=gs[:, c:c + 1]),
                    reads=[ntr[q], mods.all(), gs1.all(), gs2.all()], writes=[hbr(c)])

        xs = [uf(i * 8192, 2048) for i in range(2)]
        xsr = [ur(i * 8192, 8192) for i in range(2)]

        def load_x(row0):
            cnt = 0
            for blk in range(4):
                for half in range(2):
                    q = cnt % 2
                    cnt += 1
                    S.dma("sp", xs[q], dr["xcat"][row0 + blk * 128: row0 + (blk + 1) * 128, half * 2048:(half + 1) * 2048],
                          writes=[xsr[q]], key="xs%d" % q)
                    for c4 in range(4):
                        b = pnext()
                        for cc in range(4):
                            pe(lambda e, b=b, q=q, c4=c4, cc=cc:
                               e.transpose(ps[b].t[:, cc * 128:(cc + 1) * 128], xs[q][:, (c4 * 4 + cc) * 128:(c4 * 4 + cc + 1) * 128], sm["identf"].t[:]),
                               reads=[xsr[q], sm["identf"].all()], writes=[ps[b].all()])
                        c0 = half * 16 + c4 * 4
                        fn = lambda e, b=b, c0=c0, blk=blk: e.tensor_copy(
                            acc.t[:, c0:c0 + 4, blk * 128:(blk + 1) * 128], ps[b].t[:].rearrange("p (c t) -> p c t", c=4))
                        fn2 = lambda e, b=b, c0=c0, blk=blk: e.activation(
                            out=acc.t[:, c0:c0 + 4, blk * 128:(blk + 1) * 128], in_=ps[b].t[:].rearrange("p (c t) -> p c t", c=4), func=AF.Identity)
                        S.op("dve" if c4 % 2 == 0 else "act", fn if c4 % 2 == 0 else fn2,
                             reads=[ps[b].all()], writes=[acc.r(c0 * T, (c0 + 4) * T)])

        rcv = uf(X0, 512); rsv = uf(X0 + 2048, 512)
        rc_r = ur(X0, 4096)
        t4v = [uf(X0 + 4096 + i * 2048, 512) for i in range(4)]
        t4r = ur(X0 + 4096, 8192)

        def rope_epi(ba, bb, outA, outB, regA, regB):
            dve(lambda e: e.tensor_tensor(out=t4v[0], in0=ps[ba].t[:], in1=rcv, op=ALU.mult), reads=[ps[ba].all(), rc_r], writes=[t4r])
            dve(lambda e: e.tensor_tensor(out=t4v[1], in0=ps[bb].t[:], in1=rsv, op=ALU.mult), reads=[ps[bb].all(), rc_r], writes=[t4r])
            dve(lambda e: e.tensor_tensor(out=t4v[2], in0=ps[bb].t[:], in1=rcv, op=ALU.mult), reads=[ps[bb].all(), rc_r], writes=[t4r])
            dve(lambda e: e.tensor_tensor(out=t4v[3], in0=ps[ba].t[:], in1=rsv, op=ALU.mult), reads=[ps[ba].all(), rc_r], writes=[t4r])
            dve(lambda e: e.tensor_tensor(out=outA, in0=t4v[0], in1=t4v[1], op=ALU.subtract), reads=[t4r], writes=[regA])
            dve(lambda e: e.tensor_tensor(out=outB, in0=t4v[2], in1=t4v[3], op=ALU.add), reads=[t4r], writes=[regB])

        def in_proj(ti, pre, want_kv):
            col0 = ti * T
            S.dma("sp", rcv, dr["ropec"][:, col0:col0 + T], writes=[rc_r], key="rope")
            S.dma("sp", rsv, dr["ropes"][:, col0:col0 + T], writes=[rc_r], key="rope")
            rhs = lambda kc: (hb.t[:, kc, :], hbr(kc))
            if not pre:
                for slab in range(4):
                    banks = []
                    group_mm(dr["w_in"], 8, slab * 512, rhs, lambda j, b: banks.append(b))
                    for pr in range(2):
                        g = slab * 2 + pr
                        rope_epi(banks[2 * pr], banks[2 * pr + 1], Qv[:, 2 * g, :], Qv[:, 2 * g + 1, :], Qr(2 * g), Qr(2 * g + 1))
            if want_kv:
                for slab in range(2):
                    banks = []
                    group_mm(dr["w_in"], 8, 2048 + slab * 512, rhs, lambda j, b: banks.append(b))
                    for pr in range(2):
                        a = slab * 2 + pr
                        rope_epi(banks[2 * pr], banks[2 * pr + 1], Kb.t[:, 2 * a, 128:640], Kb.t[:, 2 * a + 1, 128:640],
                                 Kb.r((2 * a) * 640 + 128, (2 * a + 1) * 640), Kb.r((2 * a + 1) * 640 + 128, (2 * a + 2) * 640))
            for slab in range(4):
                def epi(j, b, slab=slab):
                    c = slab * 4 + j
                    act(lambda e: e.activation(out=ubv[:, c, :], in_=ps[b].t[:], func=AF.Identity), reads=[ps[b].all()], writes=[ubr(c)])
                group_mm(dr["w_in"], 8, 3072 + slab * 512, rhs, epi)
                if pre:
                    ada_full_slab()
            if want_kv:
                banks = [pnext() for _ in range(4)]
                for k4 in range(8):
                    s = wtile(dr["w_in"], k4 * 512, 5120, 256)
                    for a in range(4):
                        kc = k4 * 4 + a
                        for blk in range(4):
                            pe(lambda e, b=banks[blk], s=s, a=a, kc=kc, blk=blk:
                               e.matmul(ps[b].t[:, 0:256], hb.t[:, kc, blk * 128:(blk + 1) * 128], wr.t[:, s, a, 0:256],
                                        start=(kc == 0), stop=(kc == 31)),
                               reads=[wreg(s), hbr(kc)], writes=[ps[banks[blk]].all()])
                for blk in range(4):
                    b = banks[blk]
                    act(lambda e, b=b, blk=blk: e.activation(out=Va.t[:, 1 + blk, :, 0:64], in_=ps[b].t[:, 0:256].rearrange("p (h d) -> p h d", h=4), func=AF.Identity),
                        reads=[ps[b].all()], writes=[Va.r((1 + blk) * 260, (2 + blk) * 260)])

        def shift_halo():
            dve(lambda e: e.tensor_copy(Kb.t[:, :, 0:128], Kb.t[:, :, 512:640]), reads=[Kb.all()], writes=[Kb.all()])
            dve(lambda e: e.tensor_copy(Va.t[:, 0, :, :], Va.t[:, 4, :, :]), reads=[Va.all()], writes=[Va.all()])

        A0 = X0
        attok = uf(A0, 2048); attok_r = ur(A0, 8192)
        atn = ubf(A0 + 8192, 2048); atn_r = ur(A0 + 8192, 4096)
        NEB = 4
        ebv = [ubf(A0 + 12288 + i * 512, 256) for i in range(NEB)]
        ebr = [ur(A0 + 12288 + i * 512, 512) for i in range(NEB)]
        emv = [ubf(A0 + 14336 + i * 512, 256) for i in range(NEB)]
        emr = [ur(A0 + 14336 + i * 512, 512) for i in range(NEB)]
        atj = atn; atj_r = atn_r

        def attention(first_tile):
            for qb in range(4):
                msk = cb["mask_f"] if (first_tile and qb == 0) else cb["mask_n"]
                heads = [(g, j) for g in range(8) for j in range(4)]
                st = {}
                pob = {}

                def stageA(n, qb=qb):
                    g, j = heads[n]
                    a = g // 2
                    q = n % NEB
                    sb = pnext()
                    st[n] = (sb, q)
                    for kb in range(2):
                        kcol = (qb + kb) * 128
                        for ab in range(2):
                            pe(lambda e, sb=sb, kb=kb, ab=ab, kcol=kcol, g=g, j=j, a=a:
                               e.matmul(ps[sb].t[:, kb * 128:(kb + 1) * 128],
                                        Kb.t[32 * j:32 * j + 32, 2 * a + ab, kcol:kcol + 128],
                                        Qv[32 * j:32 * j + 32, 2 * g + ab, qb * 128:(qb + 1) * 128],
                                        start=(ab == 0), stop=(ab == 1), tile_position=(32 * j, 0)),
                               reads=[Kb.all(), Qr(2 * g + ab)], writes=[ps[sb].all()])
                    act(lambda e, sb=sb, q=q: e.activation(out=ebv[q], in_=ps[sb].t[:, 0:256], func=AF.Exp, scale=0.125),
                        reads=[ps[sb].all()], writes=[ebr[q]])

                def stageM(n, msk=msk):
                    q = st[n][1]
                    dve(lambda e, q=q: e.tensor_tensor(out=emv[q], in0=ebv[q], in1=msk.t[:], op=ALU.mult),
                        reads=[ebr[q], msk.all()], writes=[emr[q]])

                def stageP(n, qb=qb):
                    g, j = heads[n]
                    a = g // 2
                    q = st[n][1]
                    if j == 0:
                        pob[g] = pnext()
                    po = pob[g]
                    for kb in range(2):
                        pe(lambda e, po=po, q=q, kb=kb, j=j, a=a:
                           e.matmul(ps[po].t[:, j * 65:(j + 1) * 65], emv[q][:, kb * 128:(kb + 1) * 128],
                                    Va.t[:, qb + kb, a, :], start=(kb == 0), stop=(kb == 1)),
                           reads=[emr[q], Va.all()], writes=[ps[po].all()])
                    if j == 3:
                        pov = ps[po].t[:, 0:260].rearrange("p (h d) -> p h d", d=65)
                        SR = [sml.all()]
                        dve(lambda e, pov=pov, g=g: e.tensor_tensor(out=sml.t[:, 0:4], in0=pov[:, :, 64], in1=esink.t[:, 4 * g:4 * g + 4], op=ALU.add),
                            reads=[ps[po].all(), esink.all()], writes=SR)
                        dve(lambda e: e.reciprocal(sml.t[:, 4:8], sml.t[:, 0:4]), reads=SR, writes=SR)
                        dve(lambda e, pov=pov, g=g: e.tensor_tensor(
                            out=attok[:, g * 256:(g + 1) * 256].rearrange("p (h d) -> p h d", d=64), in0=pov[:, :, 0:64],
                            in1=sml.t[:, 4:8].unsqueeze(2).to_broadcast([128, 4, 64]), op=ALU.mult),
                            reads=[ps[po].all()] + SR, writes=[attok_r])

                for n in range(32 + 3):
                    if n < 32:
                        stageA(n)
                    if 0 <= n - 2 < 32:
                        stageM(n - 2)
                    if 0 <= n - 3 < 32:
                        stageP(n - 3)
                SR = [sml.all()]
                act(lambda e: e.activation(out=atj, in_=attok, func=AF.Square, accum_out=sml.t[:, 8:9]), reads=[attok_r], writes=[atj_r] + SR)
                dve(lambda e: e.tensor_scalar(sml.t[:, 9:10], sml.t[:, 8:9], 1.0 / 2048, EPS, op0=ALU.mult, op1=ALU.add), reads=SR, writes=SR)
                act(lambda e: e.activation(out=sml.t[:, 9:10], in_=sml.t[:, 9:10], func=AF.Sqrt), reads=SR, writes=SR)
                dve(lambda e: e.reciprocal(sml.t[:, 10:11], sml.t[:, 9:10]), reads=SR, writes=SR)
                dve(lambda e: e.tensor_scalar(atn, attok, sml.t[:, 10:11], None, op0=ALU.mult), reads=[attok_r] + SR, writes=[atn_r])
                for c4 in range(4):
                    b = pnext()
                    pb = ps[b].t[:].bitcast(BF16)
                    for cc in range(4):
                        c = c4 * 4 + cc
                        pe(lambda e, pb=pb, cc=cc, c=c: e.transpose(pb[:, cc * 128:(cc + 1) * 128], atn[:, c * 128:(c + 1) * 128], cb["identb"].t[:]),
                           reads=[atn_r, cb["identb"].all()], writes=[ps[b].all()])
                    for cc in range(4):
                        c = c4 * 4 + cc
                        act(lambda e, pb=pb, cc=cc, c=c, qb=qb: e.activation(out=hb.t[:, c, qb * 128:(qb + 1) * 128], in_=pb[:, cc * 128:(cc + 1) * 128],
                                                                   func=AF.Identity, scale=sm["attn_gT"].t[:, c:c + 1]),
                            reads=[ps[b].all(), sm["attn_gT"].all()], writes=[hb.r(c * T + qb * 128, c * T + (qb + 1) * 128)])

        S0 = X0
        tEs = [uf(S0 + i * 4096, 1024).rearrange("p (c t) -> p c t", c=2) for i in range(2)]
        tErs = [ur(S0 + i * 4096, 4096) for i in range(2)]
        stmp = [uf(S0 + 8192 + i * 2048, 512) for i in range(2)]; stmp_r = [ur(S0 + 8192 + i * 2048, 2048) for i in range(2)]
        mmv = [uf(S0 + 12288 + i * 2048, 512) for i in range(2)]; mm_r = [ur(S0 + 12288 + i * 2048, 2048) for i in range(2)]
        wwv = [uf(S0 + 16384 + i * 2048, 512) for i in range(2)]; ww_r = [ur(S0 + 16384 + i * 2048, 2048) for i in range(2)]
        zzv = [Kb.t[:, a_, 128:640] for a_ in range(4)]; zz_r = [Kb.r(a_ * 640 + 128, (a_ + 1) * 640) for a_ in range(4)]
        nwv = Kb.t[:, 4, 128:640]; nw_r = Kb.r(4 * 640 + 128, 5 * 640)
        wrbv = Kb.t[:, 5, 128:640]; wrb_r = Kb.r(5 * 640 + 128, 6 * 640)
        ecbv = Kb.t[:, 6, 128:640]; ecb_r = Kb.r(6 * 640 + 128, 7 * 640)
        snbv = Kb.t[:, 7, 128:640]; snb_r = Kb.r(7 * 640 + 128, 8 * 640)
        wibv = hb.t[:, 30, :]; wib_r = hbr(30)
        necbv = hb.t[:, 31, :]; necb_r = hbr(31)

        def ssm():
            CR = [car.all(), wini.all(), ctmp.all(), cos1.all(), sin1.all()]
            cre, cim = car.t[:, 0, :], car.t[:, 1, :]
            dve(lambda e: e.tensor_tensor(out=ctmp.t[:, 0, :], in0=cos1.t[:], in1=cre, op=ALU.mult), reads=CR, writes=CR)
            dve(lambda e: e.tensor_tensor(out=ctmp.t[:, 1, :], in0=sin1.t[:], in1=cim, op=ALU.mult), reads=CR, writes=CR)
            dve(lambda e: e.tensor_tensor(out=wini.t[:, 0, :], in0=ctmp.t[:, 0, :], in1=ctmp.t[:, 1, :], op=ALU.subtract), reads=CR, writes=CR)
            dve(lambda e: e.tensor_tensor(out=ctmp.t[:, 2, :], in0=sin1.t[:], in1=cre, op=ALU.mult), reads=CR, writes=CR)
            dve(lambda e: e.tensor_tensor(out=ctmp.t[:, 3, :], in0=cos1.t[:], in1=cim, op=ALU.mult), reads=CR, writes=CR)
            dve(lambda e: e.tensor_tensor(out=wini.t[:, 1, :], in0=ctmp.t[:, 2, :], in1=ctmp.t[:, 3, :], op=ALU.add), reads=CR, writes=CR)
            bbank = [0, 1, 2, 3]
            ybank = [4, 5]

            def bk(n):
                return bbank[(2 * n) % 4], bbank[(2 * n + 1) % 4]

            def stL_dma(n):
                S.dma("sp", tEs[n % 2], tabs[n].rearrange("c p t -> p c t"), reads=[("tabs", n, n + 1)], writes=[tErs[n % 2]], key="tabr%d" % (n % 2))

            def stL_pe(n):
                c_, j_ = n // 4, n % 4
                for ri, b in zip((0, 1), bk(n)):
                    pe(lambda e, ri=ri, b=b, c_=c_, j_=j_: e.matmul(ps[b].t[:], Bexp.t[32 * j_:32 * j_ + 32, c_, ri, :],
                                                                  ubv[32 * j_:32 * j_ + 32, c_, :], start=True, stop=True,
                                                                  tile_position=(32 * j_, 0)),
                       reads=[Bexp.all(), ubr(c_)], writes=[ps[b].all()])

            def stM(n):
                Ec, Es = tEs[n % 2][:, 0, :], tEs[n % 2][:, 1, :]
                tr = tErs[n % 2]
                bre, bim = bk(n)
                dve(lambda e: e.tensor_tensor(out=mmv[0], in0=ps[bre].t[:], in1=Ec, op=ALU.mult), reads=[ps[bre].all(), tr], writes=[mm_r[0]])
                dve(lambda e: e.tensor_tensor(out=stmp[0], in0=ps[bim].t[:], in1=Es, op=ALU.mult), reads=[ps[bim].all(), tr], writes=[stmp_r[0]])
                dve(lambda e: e.tensor_tensor(out=mmv[1], in0=ps[bim].t[:], in1=Ec, op=ALU.mult), reads=[ps[bim].all(), tr], writes=[mm_r[1]])
                dve(lambda e: e.tensor_tensor(out=stmp[1], in0=ps[bre].t[:], in1=Es, op=ALU.mult), reads=[ps[bre].all(), tr], writes=[stmp_r[1]])

            def stA(n):
                S.op("pool", lambda e: e.tensor_tensor(out=mmv[0], in0=mmv[0], in1=stmp[0], op=ALU.subtract), [mm_r[0], stmp_r[0]], [mm_r[0]])
                S.op("pool", lambda e: e.tensor_tensor(out=mmv[1], in0=mmv[1], in1=stmp[1], op=ALU.add), [mm_r[1], stmp_r[1]], [mm_r[1]])

            def stS(n):
                for ri in range(2):
                    dve(lambda e, ri=ri, p=n: e.tensor_tensor_scan(out=wwv[ri], data0=rr.t[:, p:p + 1].to_broadcast([128, 512]), data1=mmv[ri],
                                                                  initial=wini.t[:, ri, p:p + 1], op0=ALU.mult, op1=ALU.add),
                        reads=[mm_r[ri], rr.all(), wini.all()], writes=[ww_r[ri]])
                for ri in range(2):
                    act(lambda e, ri=ri, p=n: e.activation(out=wlast.t[:, ri, p:p + 1], in_=wwv[ri][:, 511:512], func=AF.Identity),
                        reads=[ww_r[ri]], writes=[wlast.all()])
                act(lambda e: e.activation(out=wrbv, in_=wwv[0], func=AF.Identity), reads=[ww_r[0]], writes=[wrb_r])
                act(lambda e, n=n: e.activation(out=ecbv, in_=tEs[n % 2][:, 0, :], func=AF.Identity), reads=[tErs[n % 2]], writes=[ecb_r])
                act(lambda e, n=n: e.activation(out=snbv, in_=tEs[n % 2][:, 1, :], func=AF.Identity), reads=[tErs[n % 2]], writes=[snb_r])
                act(lambda e: e.activation(out=wibv, in_=wwv[1], func=AF.Identity), reads=[ww_r[1]], writes=[wib_r])
                act(lambda e, n=n: e.activation(out=necbv, in_=tEs[n % 2][:, 0, :], func=AF.Identity, scale=-1.0), reads=[tErs[n % 2]], writes=[necb_r])

            def stZ(n):
                c_, j_ = n // 4, n % 4
                Ec, Es = tEs[n % 2][:, 0, :], tEs[n % 2][:, 1, :]
                tr = tErs[n % 2]
                dve(lambda e: e.tensor_tensor(out=zzv[0], in0=wrbv, in1=ecbv, op=ALU.mult), reads=[wrb_r, ecb_r], writes=[zz_r[0]])
                dve(lambda e: e.tensor_tensor(out=zzv[1], in0=wibv, in1=snbv, op=ALU.mult), reads=[wib_r, snb_r], writes=[zz_r[1]])
                dve(lambda e: e.tensor_tensor(out=zzv[2], in0=wrbv, in1=snbv, op=ALU.mult), reads=[wrb_r, snb_r], writes=[zz_r[2]])
                dve(lambda e: e.tensor_tensor(out=zzv[3], in0=wibv, in1=necbv, op=ALU.mult), reads=[wib_r, necb_r], writes=[zz_r[3]])
                yb = ybank[n % 2]
                for zi, cm in ((0, "cre_e"), (1, "cre_e"), (2, "cim_e"), (3, "cim_e")):
                    pe(lambda e, zi=zi, cm=cm, yb=yb, p=n: e.matmul(ps[yb].t[0:32, :], cb[cm].t[:, p, :], zzv[zi], start=(zi == 0), stop=False),
                       reads=[cb[cm].all(), zz_r[zi]], writes=[ps[yb].all()])
                pe(lambda e, yb=yb, c_=c_, j_=j_: e.matmul(ps[yb].t[0:32, :], cb["d_e"].t[32 * j_:32 * j_ + 32, c_, :], ubv[32 * j_:32 * j_ + 32, c_, :],
                                                           start=False, stop=True, tile_position=(32 * j_, 0)),
                   reads=[cb["d_e"].all(), ubr(c_)], writes=[ps[yb].all()])
                act(lambda e, yb=yb, c_=c_, j_=j_: e.activation(out=gbv[32 * j_:32 * j_ + 32, c_, :], in_=ps[yb].t[0:32, :], func=AF.Gelu),
                    reads=[ps[yb].all()], writes=[Qr(c_)])

            stL_dma(0)
            stL_pe(0)
            for n in range(64):
                if n + 1 < 64:
                    stL_pe(n + 1)
                stM(n)
                stA(n)
                if n >= 1:
                    stZ(n - 1)
                if n + 1 < 64:
                    stL_dma(n + 1)
                stS(n)
            stZ(63)
            FR = [car.all(), ctmp.all(), wlast.all(), c511.all(), s511.all()]
            wl_re, wl_im = wlast.t[:, 0, :], wlast.t[:, 1, :]
            TT = lambda o, a, b, op: dve(lambda e: e.tensor_tensor(out=o, in0=a, in1=b, op=op), reads=FR, writes=FR)
            TT(ctmp.t[:, 0, :], c511.t[:], wl_re, ALU.mult)
            TT(ctmp.t[:, 1, :], s511.t[:], wl_im, ALU.mult)
            TT(cre, ctmp.t[:, 0, :], ctmp.t[:, 1, :], ALU.subtract)
            TT(ctmp.t[:, 2, :], s511.t[:], wl_re, ALU.mult)
            TT(ctmp.t[:, 3, :], c511.t[:], wl_im, ALU.mult)
            TT(cim, ctmp.t[:, 2, :], ctmp.t[:, 3, :], ALU.add)

        tWs = [uf(X0 + i * 4096, 1024).rearrange("p (c t) -> p c t", c=2) for i in range(2)]
        tWrs = [ur(X0 + i * 4096, 4096) for i in range(2)]
        jkv = [uf(i * 2048, 512) for i in range(4)]; jk_r = [ur(i * 2048, 2048) for i in range(4)]

        ada_slab = [16]

        def ada_full_slab():
            if ada_slab[0] < 48:
                sl = ada_slab[0]
                ada_slab[0] += 1
                for _ in ada_all([sl], [pnext() for _ in range(4)]):
                    pass

        def ssm_pre():
            sl0 = ada_slab[0]
            ada_slab[0] = min(48, sl0 + 4)
            ada_it = ada_all(range(sl0, ada_slab[0]), [3, 4, 5, 6])
            bl = [0, 1, 2, 7]
            for pi_ in range(64):
                c_, j_ = pi_ // 4, pi_ % 4
                tWv, tW_r = tWs[pi_ % 2], tWrs[pi_ % 2]
                S.dma("sp", tWv, wtabs[pi_].rearrange("c p t -> p c t"), reads=[("wtabs", pi_, pi_ + 1)], writes=[tW_r], key="tabr%d" % (pi_ % 2))
                bre, bim = bl[(2 * pi_) % 4], bl[(2 * pi_ + 1) % 4]
                for ri, b in ((0, bre), (1, bim)):
                    pe(lambda e, ri=ri, b=b, c_=c_, j_=j_: e.matmul(ps[b].t[:], Bexp.t[32 * j_:32 * j_ + 32, c_, ri, :],
                                                                  ubv[32 * j_:32 * j_ + 32, c_, :], start=True, stop=True,
                                                                  tile_position=(32 * j_, 0)),
                       reads=[Bexp.all(), ubr(c_)], writes=[ps[b].all()])
                for k, (bank, wi) in enumerate(((bre, 0), (bim, 1), (bim, 0), (bre, 1))):
                    dve(lambda e, k=k, bank=bank, wi=wi, tWv=tWv: e.tensor_tensor(out=jkv[k], in0=ps[bank].t[:], in1=tWv[:, wi, :], op=ALU.mult),
                        reads=[ps[bank].all(), tW_r], writes=[jk_r[k]])
                    act(lambda e, k=k, p=pi_: e.activation(out=jkv[k], in_=jkv[k], func=AF.Identity, accum_out=accS.t[:, k, p:p + 1]),
                        reads=[jk_r[k]], writes=[jk_r[k], accS.all()])
                if pi_ % 2 == 1:
                    next(ada_it, None)
            for _ in ada_it:
                pass
            CR = [car.all(), ctmp.all(), accS.all(), Gre.all(), Gim.all()]
            cre, cim = car.t[:, 0, :], car.t[:, 1, :]
            c0, c1, c2, c3 = [ctmp.t[:, i, :] for i in range(4)]
            TT = lambda o, a, b, op: dve(lambda e: e.tensor_tensor(out=o, in0=a, in1=b, op=op), reads=CR, writes=CR)
            TT(c0, accS.t[:, 0, :], accS.t[:, 1, :], ALU.subtract)
            TT(c1, accS.t[:, 2, :], accS.t[:, 3, :], ALU.add)
            TT(c2, Gre.t[:], cre, ALU.mult)
            TT(c0, c0, c2, ALU.add)
            TT(c2, Gim.t[:], cim, ALU.mult)
            TT(c0, c0, c2, ALU.subtract)
            TT(c3, Gre.t[:], cim, ALU.mult)
            TT(c1, c1, c3, ALU.add)
            TT(c3, Gim.t[:], cre, ALU.mult)
            TT(c1, c1, c3, ALU.add)
            dve(lambda e: e.tensor_copy(cre, c0), reads=CR, writes=CR)
            dve(lambda e: e.tensor_copy(cim, c1), reads=CR, writes=CR)

        G0 = X0
        gatev = [uf(G0 + 12288 + i * 2048, 512) for i in range(2)]; gate_r = [ur(G0 + 12288 + i * 2048, 2048) for i in range(2)]
        prodv = [uf(G0 + 16384 + i * 2048, 512) for i in range(2)]; prod_r = [ur(G0 + 16384 + i * 2048, 2048) for i in range(2)]

        def glu_and_norm():
            rhs = lambda kc: (gbv[:, kc, :], Qr(kc))
            cnt = [0]
            for slab in range(4):
                def epi(j, b, slab=slab):
                    ob = slab * 4 + j
                    q = cnt[0] % 2
                    i = cnt[0]
                    cnt[0] += 1
                    act(lambda e: e.activation(out=gatev[q], in_=ps[b].t[:], func=AF.Sigmoid, bias=sm["b_gluT"].t[:, ob:ob + 1]),
                        reads=[ps[b].all(), sm["b_gluT"].all()], writes=[gate_r[q]])
                    dve(lambda e: e.tensor_tensor(out=prodv[q], in0=gbv[:, ob, :], in1=gatev[q], op=ALU.mult),
                        reads=[Qr(ob), gate_r[q]], writes=[prod_r[q]])
                    act(lambda e: e.activation(out=sqv[q], in_=prodv[q], func=AF.Square), reads=[prod_r[q]], writes=[sqr[q]])
                    pe(lambda e: e.matmul(ps[SSB].t[:], ones.t[:], sqv[q], start=(i == 0), stop=(i == 15)),
                       reads=[ones.all(), sqr[q]], writes=[ps[SSB].all()])
                    dve(lambda e: e.tensor_copy(hb.t[:, 16 + ob, :], prodv[q]), reads=[prod_r[q]], writes=[hbr(16 + ob)])
                group_mm(dr["w_glu"], 4, slab * 512, rhs, epi)
            finish_stats(2048)
            for ob in range(16):
                dve(lambda e, ob=ob: e.scalar_tensor_tensor(out=hb.t[:, 16 + ob, :], in0=hb.t[:, 16 + ob, :], scalar=sm["ssm_gT"].t[:, ob:ob + 1],
                                                         in1=rstd, op0=ALU.mult, op1=ALU.mult),
                    reads=[hbr(16 + ob), rstd_r, sm["ssm_gT"].all()], writes=[hbr(16 + ob)])

        def resid_epi(gate, st):
            def epi_factory(slab):
                def epi(j, b):
                    ob = slab * 4 + j
                    dve(lambda e: e.scalar_tensor_tensor(out=acc.t[:, ob, :], in0=ps[b].t[:], scalar=gate[:, ob:ob + 1], in1=acc.t[:, ob, :],
                                                         op0=ALU.mult, op1=ALU.add),
                        reads=[ps[b].all(), mods.all(), accr(ob)], writes=[accr(ob)])
                    st.add(acc.t[:, ob, :], accr(ob))
                return epi
            return epi_factory

        def out_proj():
            rhs = lambda kc: (hb.t[:, kc, :], hbr(kc))
            st = Stats()
            ef = resid_epi(gt1, st)
            for slab in range(8):
                group_mm(dr["w_out"], 8, slab * 512, rhs, ef(slab))
            return st

        hidv = [ubf(i * 4096, 2048).rearrange("p (a t) -> p a t", a=4) for i in range(2)]
        hid_r = [ur(i * 4096, 4096) for i in range(2)]
        rlv = [uf(8192 + i * 2048, 512) for i in range(2)]; rl_r = [ur(8192 + i * 2048, 2048) for i in range(2)]

        def ffn():
            rhs = lambda kc: (hb.t[:, kc, :], hbr(kc))
            cnt = [0]
            st = Stats()
            for f in range(32):
                hq = f % 2

                def epi(j, b, hq=hq):
                    q = cnt[0] % 2
                    cnt[0] += 1
                    act(lambda e: e.activation(out=rlv[q], in_=ps[b].t[:], func=AF.Relu), reads=[ps[b].all()], writes=[rl_r[q]])
                    dve(lambda e: e.scalar_tensor_tensor(out=hidv[hq][:, j, :], in0=ps[b].t[:], scalar=0.0, in1=rlv[q], op0=ALU.max, op1=ALU.mult),
                        reads=[ps[b].all(), rl_r[q]], writes=[ur(hq * 4096 + j * 1024, 1024)])
                group_mm(dr["w_ff1"], 8, f * 512, rhs, epi)
                for slab in range(8):
                    s = wtile(dr["w_ff2"], f * 512, slab * 512)
                    for j in range(4):
                        ob = slab * 4 + j
                        b = pnext()
                        for a in range(4):
                            pe(lambda e, b=b, s=s, a=a, j=j, hq=hq: e.matmul(ps[b].t[:], wr.t[:, s, a, j * 128:(j + 1) * 128], hidv[hq][:, a, :],
                                                                          start=(a == 0), stop=(a == 3)),
                               reads=[wreg(s), hid_r[hq]], writes=[ps[b].all()])
                        dve(lambda e, b=b, ob=ob: e.scalar_tensor_tensor(out=acc.t[:, ob, :], in0=ps[b].t[:], scalar=gt2[:, ob:ob + 1], in1=acc.t[:, ob, :],
                                                                     op0=ALU.mult, op1=ALU.add),
                            reads=[ps[b].all(), mods.all(), accr(ob)], writes=[accr(ob)])
                        if f == 31:
                            st.add(acc.t[:, ob, :], accr(ob))
            return st

        def final_out(row0, st):
            st.finish(D)
            for c in range(KC):
                dve(lambda e, c=c: e.scalar_tensor_tensor(out=acc.t[:, c, :], in0=acc.t[:, c, :], scalar=sm["fing"].t[:, c:c + 1], in1=rstd,
                                                       op0=ALU.mult, op1=ALU.mult),
                    reads=[accr(c), rstd_r, sm["fing"].all()], writes=[accr(c)])
            cnt = 0
            for blk in range(4):
                for half in range(2):
                    q = cnt % 2
                    cnt += 1
                    for c4 in range(4):
                        b = pnext()
                        for cc in range(4):
                            c = half * 16 + c4 * 4 + cc
                            pe(lambda e, b=b, cc=cc, c=c, blk=blk: e.transpose(ps[b].t[:, cc * 128:(cc + 1) * 128], acc.t[:, c, blk * 128:(blk + 1) * 128], sm["identf"].t[:]),
                               reads=[accr(c), sm["identf"].all()], writes=[ps[b].all()])
                        if c4 % 2 == 0:
                            dve(lambda e, b=b, q=q, c4=c4: e.tensor_copy(xs[q][:, c4 * 512:(c4 + 1) * 512], ps[b].t[:]), reads=[ps[b].all()],
                                writes=[ur(q * 8192 + c4 * 2048, 2048)])
                        else:
                            act(lambda e, b=b, q=q, c4=c4: e.activation(out=xs[q][:, c4 * 512:(c4 + 1) * 512], in_=ps[b].t[:], func=AF.Identity), reads=[ps[b].all()],
                                writes=[ur(q * 8192 + c4 * 2048, 2048)])
                    S.dma("sp", out_d[row0 + blk * 128: row0 + (blk + 1) * 128, half * 2048:(half + 1) * 2048], xs[q], reads=[xsr[q]], key="os%d" % q)

        nt = NTILE if not debug else debug.get("_ntile", [NTILE])[0]
        npre = NPRE if not debug else debug.get("_npre", [NPRE])[0]
        e_cnt = [0]
        for ti in range(NPRE - npre, NPRE):
            load_x(ti * T)
            norm_mod(gs1.t, sh1)
            want_kv = (ti == NPRE - 1)
            in_proj(ti, True, want_kv)
            if want_kv:
                shift_halo()
            ssm_pre()
        while ada_slab[0] < 48:
            ada_full_slab()
        dve(lambda e: e.scalar_tensor_tensor(out=gs2.t[:], in0=sc2, scalar=1.0, in1=sm["n2g"].t[:], op0=ALU.add, op1=ALU.mult),
            reads=[mods.all(), sm["n2g"].all()], writes=[gs2.all()])
        dump("mods", mods.t[:], [mods.all()])
        dve(lambda e: e.tensor_scalar(car.t[:], car.t[:], sm["flag"].t[:, 0:1], None, op0=ALU.mult), reads=[car.all(), sm["flag"].all()], writes=[car.all()])
        for i in range(nt):
            ti = NPRE + i
            load_x(ti * T)
            norm_mod(gs1.t, sh1)
            if i == 0:
                dump("x0", acc.t[:, 0:2, :], [acc.all()])
                dump("h1", hb.t[:, 0:2, :], [hb.all()])
                dump("Bexp", Bexp.t[:, 0, :, :], [Bexp.all()])
                dump("rr", rr.t[:], [rr.all()])
                dump("phi", phi.t[:], [phi.all()])
            in_proj(ti, False, True)
            if i == 0:
                dump("q0", Qv[:, 0:2, :], [Qr(0), Qr(1)])
                dump("k0", Kb.t[:, 0:2, :], [Kb.all()])
                dump("ub0", ubv[:, 0:2, :], [ubr(0), ubr(1)])
                dump("va", Va.t[:, 1, :, :], [Va.all()])
            attention(i == 0)
            if i == 0:
                dump("att", hb.t[:, 0:2, :], [hb.all()])
            shift_halo()
            ssm()
            if i == 0:
                dump("gb0", gbv[:, 0:2, :], [Qr(0), Qr(1)])
                dump("car", car.t[:], [car.all()])
            glu_and_norm()
            if i == 0:
                dump("ssm", hb.t[:, 16:18, :], [hb.all()])
            st2 = out_proj()
            if i == 0:
                dump("x1", acc.t[:, 0:2, :], [acc.all()])
            norm_mod(gs2.t, sh2, st2)
            if i == 0:
                dump("h2", hb.t[:, 0:2, :], [hb.all()])
            st3 = ffn()
            if i == 0:
                dump("x2", acc.t[:, 0:2, :], [acc.all()])
            final_out(i * T, st3)
        S.emit()
        build.stats = S.stats
    return nc


def _prep_shared(inp):
    f = np.float32
    sh = {}
    idx = []
    for g in range(8):
        for half in range(2):
            for j in range(4):
                h = 4 * g + j
                idx.extend(range(h * 64 + half * 32, h * 64 + half * 32 + 32))
    for a in range(4):
        for half in range(2):
            for j in range(4):
                idx.extend(range(2048 + a * 64 + half * 32, 2048 + a * 64 + half * 32 + 32))
    idx.extend(range(2560, 4608))
    idx.extend(range(2304, 2560))
    idx = np.asarray(idx)
    assert idx.size == WINP
    sh["w_in"] = np.ascontiguousarray(inp["w_in"][0][:, idx])
    sh["w_ada"] = np.ascontiguousarray(inp["w_ada"][0])
    sh["w_glu"] = np.ascontiguousarray(inp["w_glu"][0])
    sh["w_out"] = np.ascontiguousarray(inp["w_out"][0])
    sh["w_ff1"] = np.ascontiguousarray(inp["w_ff1"][0])
    sh["w_ff2"] = np.ascontiguousarray(inp["w_ff2"][0])
    col = lambda v: np.ascontiguousarray(np.asarray(v, f).reshape(-1, 128).T)
    sh["b_adaT"] = col(inp["b_ada"][0])
    sh["n1g"] = col(inp["norm1_g"][0]); sh["n2g"] = col(inp["norm2_g"][0]); sh["fing"] = col(inp["final_g"])
    sh["sinks_bc"] = np.ascontiguousarray(np.broadcast_to(np.asarray(inp["sinks"][0], f)[None, :], (128, 32)))
    sh["b_gluT"] = col(inp["b_glu"][0]); sh["ssm_gT"] = col(inp["ssm_out_g"][0]); sh["attn_gT"] = col(inp["attn_out_g"][0])
    sh["identf"] = np.eye(128, dtype=f); sh["identb"] = np.eye(128, dtype=f)
    sh["iota"] = np.ascontiguousarray(np.broadcast_to(np.arange(512, dtype=f)[None, :], (128, 512)))
    lre, lim, lst = [np.asarray(inp[k][0], f) for k in ("ssm_lam_re", "ssm_lam_im", "ssm_log_step")]
    bre, bim = np.asarray(inp["ssm_b_re"][0], f), np.asarray(inp["ssm_b_im"][0], f)
    cre, cim = np.asarray(inp["ssm_c_re"][0], f), np.asarray(inp["ssm_c_im"][0], f)
    dsk = np.asarray(inp["ssm_d"][0], f)
    q = np.arange(128)
    pi_ = np.arange(64)
    Gp = 2 * pi_[None, :] + (q // 64)[:, None]
    Pp = np.broadcast_to((q % 64)[:, None], (128, 64))
    sh["lamre_p"] = np.ascontiguousarray(lre[Gp, Pp]); sh["lamim_p"] = np.ascontiguousarray(lim[Gp, Pp])
    sh["lstep_p"] = np.ascontiguousarray(lst[Gp])
    r = np.arange(128)
    cprime = np.arange(16)
    colx = np.arange(128)
    G = (8 * cprime[None, :] + 2 * (r // 32)[:, None] + ((r % 32) // 16)[:, None])[:, :, None]
    P = (colx % 64)[None, None, :]
    H = (r % 16)[:, None, None]
    same = ((colx // 64)[None, None, :] == ((r % 32) // 16)[:, None, None])
    Gb = np.broadcast_to(G, (128, 16, 128)); Pb = np.broadcast_to(P, (128, 16, 128)); Hb = np.broadcast_to(H, (128, 16, 128))
    sh["lamre_e"] = np.ascontiguousarray(lre[Gb, Pb]).reshape(128, 2048)
    sh["lamim_e"] = np.ascontiguousarray(lim[Gb, Pb]).reshape(128, 2048)
    sh["lstep_e"] = np.ascontiguousarray(lst[Gb]).reshape(128, 2048)
    sh["bre_e"] = np.where(same, bre[Gb, Pb, Hb], f(0)).astype(f).reshape(128, 2048)
    sh["bim_e"] = np.where(same, bim[Gb, Pb, Hb], f(0)).astype(f).reshape(128, 2048)
    c32 = np.arange(32)
    Gc = np.broadcast_to((2 * pi_[None, :, None] + (q // 64)[:, None, None]), (128, 64, 32))
    Hc = np.broadcast_to((c32 % 16)[None, None, :], (128, 64, 32))
    Pc = np.broadcast_to((q % 64)[:, None, None], (128, 64, 32))
    samec = ((c32 // 16)[None, None, :] == (q // 64)[:, None, None])
    sh["cre_e"] = np.where(samec, cre[Gc, Hc, Pc], f(0)).astype(f)
    sh["cim_e"] = np.where(samec, cim[Gc, Hc, Pc], f(0)).astype(f)
    ch = (8 * cprime[None, :] + 2 * (r // 32)[:, None]) * 16 + (r % 32)[:, None]
    dd = np.where((c32[None, None, :] == (r % 32)[:, None, None]), dsk[ch][:, :, None], f(0)).astype(f)
    sh["d_e"] = np.ascontiguousarray(dd)
    j = np.arange(128)[:, None]; i = np.arange(128)[None, :]
    mprev = (j > i).astype(f); mcur = (j <= i).astype(f)
    sh["mask_n"] = np.ascontiguousarray(np.concatenate([mprev, mcur], axis=1))
    return sh, (mprev, mcur)


def _rope_tables(base):
    f = np.float32
    half = 32
    inv_freq = (f(10000.0) ** (-(np.arange(half, dtype=f) / f(half)))).astype(f)
    pos = (np.arange(4096, dtype=np.int64) + base).astype(f)
    ang = (pos[None, :] * inv_freq[np.arange(128) % 32][:, None]).astype(f)
    return np.cos(ang).astype(f), np.sin(ang).astype(f)


_NC_CACHE = {}


def kernel(**inputs):
    inp = {k: np.asarray(v) for k, v in inputs.items()}
    f = np.float32
    sh, (mprev, mcur) = _prep_shared(inp)
    x = np.asarray(inp["x"], f)
    c = np.asarray(inp["c"], f)
    in_maps = []
    for core in range(8):
        b, half = core // 2, core % 2
        m = dict(sh)
        if half == 0:
            xcat = np.concatenate([np.zeros((2048, D), f), x[b, 0:2048]], axis=0)
        else:
            xcat = x[b]
        m["xcat"] = np.ascontiguousarray(xcat)
        m["cT"] = np.ascontiguousarray(c[b].reshape(32, 128).T)
        m["flag"] = np.full((128, 1), float(half), f)
        m["mask_f"] = np.ascontiguousarray(np.concatenate([mprev if half == 1 else np.zeros_like(mprev), mcur], axis=1))
        rc, rs = _rope_tables(half * 2048 - 2048)
        m["ropec"], m["ropes"] = rc, rs
        in_maps.append(m)
    if "nc" not in _NC_CACHE:
        _NC_CACHE["nc"] = build()
    nc = _NC_CACHE["nc"]
    res = run_bass_kernel_spmd(nc, in_maps, core_ids=list(range(8)))
    out = np.empty((4, 4096, D), f)
    for core in range(8):
        b, half = core // 2, core % 2
        out[b, half * 2048:(half + 1) * 2048] = res.results[core]["out"]
    return out
```

```python
import contextlib
import math
import numpy as np
import concourse.bass as bass
import concourse.mybir as mybir
from concourse.bass_utils import run_bass_kernel_spmd

F32 = mybir.dt.float32
BF16 = mybir.dt.bfloat16
AF = mybir.ActivationFunctionType
ALU = mybir.AluOpType

D = 4096
KC = 32
T = 512
NTILE = 4
NPRE = 4
DFF = 16384
EPS = 1e-6
WINP = 5376
NW = 4
MAGIC = 12582912.0
TWO_PI = 2.0 * math.pi


class _Op:
    __slots__ = ("eng", "fn", "deps", "is_dma", "sem", "semval", "needs_inc", "incval")

    def __init__(self, eng, fn):
        self.eng = eng
        self.fn = fn
        self.deps = []
        self.is_dma = False
        self.sem = None
        self.semval = 0
        self.needs_inc = False
        self.incval = 0


class Sched:
    ENGS = ("pe", "act", "dve", "pool", "sp")

    def __init__(self, nc, es):
        self.nc = nc
        self.es = es
        self.ops = []
        self.W = {}
        self.R = {}
        self.dsems = {}
        self.dcount = {}

    def _deps(self, op, reads, writes):
        deps = op.deps
        for (k, lo, hi) in reads:
            for (l2, h2, o2) in self.W.get(k, ()):
                if l2 < hi and lo < h2:
                    deps.append(o2)
            self.R.setdefault(k, []).append((lo, hi, op))
        for (k, lo, hi) in writes:
            wl = self.W.get(k, [])
            rl = self.R.get(k, [])
            for (l2, h2, o2) in wl:
                if l2 < hi and lo < h2:
                    deps.append(o2)
            for (l2, h2, o2) in rl:
                if l2 < hi and lo < h2 and o2 is not op:
                    deps.append(o2)
            self.W[k] = [t for t in wl if not (lo <= t[0] and t[1] <= hi)] + [(lo, hi, op)]
            self.R[k] = [t for t in rl if not (lo <= t[0] and t[1] <= hi) or t[2] is op]

    def op(self, eng, fn, reads=(), writes=()):
        o = _Op(eng, fn)
        self._deps(o, reads, writes)
        self.ops.append(o)
        return o

    def dma(self, eng, out, in_, reads=(), writes=(), key=None):
        o = _Op(eng, None)
        o.is_dma = True
        if key not in self.dsems:
            self.dsems[key] = self.es.enter_context(self.nc.semaphore("d_" + key))
            self.dcount[key] = 0
        self.dcount[key] += 16
        o.sem = self.dsems[key]
        o.semval = self.dcount[key]
        o.fn = lambda e, out=out, in_=in_: e.dma_start(out=out, in_=in_)
        self._deps(o, reads, writes)
        self.ops.append(o)
        return o

    def emit(self):
        nc = self.nc
        esem = {e: self.es.enter_context(nc.semaphore("e_" + e)) for e in self.ENGS}
        fin = _Op("sp", None)
        last = {}
        for o in self.ops:
            if o.is_dma:
                last[("d", id(o.sem))] = o
            else:
                last[("e", o.eng)] = o
        fin.deps = list(last.values())
        self.ops.append(fin)
        idx = {id(o): i for i, o in enumerate(self.ops)}
        for o in self.ops:
            best = {}
            for d in o.deps:
                k = ("d", id(d.sem)) if d.is_dma else ("e", d.eng)
                if k not in best or idx[id(d)] > idx[id(best[k])]:
                    best[k] = d
            o.deps = list(best.values())
        for o in self.ops:
            for d in o.deps:
                if not d.is_dma:
                    if d.eng == "pe" and o.eng == "pe" and not o.is_dma:
                        continue
                    d.needs_inc = True
        cnt = {e: 0 for e in self.ENGS}
        for o in self.ops:
            if (not o.is_dma) and o.needs_inc:
                cnt[o.eng] += 1
                o.incval = cnt[o.eng]
        per = {e: [] for e in self.ENGS}
        for o in self.ops:
            per[o.eng].append(o)
        self.stats = {e: len(per[e]) for e in self.ENGS}
        self.stats["incs"] = dict(cnt)

        def run(engname, e):
            waited = {}
            for o in per[engname]:
                for d in o.deps:
                    if d.is_dma:
                        s, v = d.sem, d.semval
                    else:
                        if d.eng == "pe" and engname == "pe" and not o.is_dma:
                            continue
                        s, v = esem[d.eng], d.incval
                    kk = id(s)
                    if waited.get(kk, 0) >= v:
                        continue
                    waited[kk] = v
                    e.wait_ge(s, v)
                if o.fn is None:
                    continue
                ins = o.fn(e)
                if o.is_dma:
                    ins.then_inc(o.sem, 16)
                elif o.needs_inc:
                    ins.then_inc(esem[engname], 1)

        with nc.Block() as block:
            @block.tensor
            def _(e):
                run("pe", e)

            @block.scalar
            def _(e):
                run("act", e)

            @block.vector
            def _(e):
                run("dve", e)

            @block.gpsimd
            def _(e):
                run("pool", e)

            @block.sync
            def _(e):
                run("sp", e)


class Buf:
    def __init__(self, nc, es, name, shape, dtype, psum=False):
        self.name = name
        if psum:
            self.t = es.enter_context(nc.psum_tensor("p_" + name, shape, dtype))
        else:
            self.t = es.enter_context(nc.sbuf_tensor("s_" + name, shape, dtype))
        n = 1
        for s in shape[1:]:
            n *= s
        self.n = n

    def all(self):
        return (self.name, 0, self.n)

    def r(self, lo, hi):
        return (self.name, lo, hi)


SMALL_IN = [
    ("cT", [128, 32]), ("b_adaT", [128, 192]), ("n1g", [128, 32]), ("n2g", [128, 32]),
    ("fing", [128, 32]), ("sinks_bc", [128, 32]), ("flag", [128, 1]),
    ("lamre_p", [128, 64]), ("lamim_p", [128, 64]), ("lstep_p", [128, 64]),
    ("b_gluT", [128, 16]), ("ssm_gT", [128, 16]), ("attn_gT", [128, 16]), ("identf", [128, 128]),
]
CAST_IN = [
    ("cre_e", [128, 64, 32]), ("cim_e", [128, 64, 32]), ("d_e", [128, 16, 32]),
    ("mask_n", [128, 256]), ("mask_f", [128, 256]), ("identb", [128, 128]),
]
BIG_IN = [
    ("lamre_e", [128, 2048]), ("lamim_e", [128, 2048]), ("lstep_e", [128, 2048]),
    ("bre_e", [128, 2048]), ("bim_e", [128, 2048]),
]


def build(debug=None):
    nc = bass.Bass("TRN2", target_bir_lowering=False)
    dr = {}

    def din(name, shape):
        dr[name] = nc.dram_tensor(name, shape, F32, kind="ExternalInput").ap()

    din("xcat", [4096, D])
    for n, s in SMALL_IN + CAST_IN + BIG_IN:
        din(n, s)
    din("iota", [128, 512])
    din("ropec", [128, 4096])
    din("ropes", [128, 4096])
    din("w_ada", [D, 6 * D])
    din("w_in", [D, WINP])
    din("w_glu", [2048, 2048])
    din("w_out", [D, D])
    din("w_ff1", [D, DFF])
    din("w_ff2", [DFF, D])
    out_d = nc.dram_tensor("out", [2048, D], F32, kind="ExternalOutput").ap()
    tabs = nc.dram_tensor("tabs", [64, 2, 128, 512], F32, kind="Internal").ap()
    wtabs = nc.dram_tensor("wtabs", [64, 2, 128, 512], F32, kind="Internal").ap()

    with contextlib.ExitStack() as es:
        S = Sched(nc, es)
        B = lambda name, shape, dt=F32, psum=False: Buf(nc, es, name, shape, dt, psum)
        acc = B("acc", [128, KC, T])
        hb = B("hb", [128, KC, T], BF16)
        wr = B("wr", [128, NW, 4, 512], BF16)
        Kb = B("Kb", [128, 8, 640], BF16)
        Va = B("Va", [128, 5, 4, 65], BF16)
        U = B("U", [128, 13312])
        ps = [B("ps%d" % i, [128, 512], F32, psum=True) for i in range(8)]
        sm = {n: B(n, s) for n, s in SMALL_IN}
        cb = {n: B(n, s, BF16) for n, s in CAST_IN}
        Bexp = B("Bexp", [128, 16, 2, 128], BF16)
        mods = B("mods", [128, 192])
        gs1 = B("gs1", [128, 32]); gs2 = B("gs2", [128, 32])
        cact = B("cact", [128, 32], BF16)
        esink = B("esink", [128, 32])
        ones = B("ones", [128, 128])
        rr = B("rr", [128, 64]); cos1 = B("cos1", [128, 64]); sin1 = B("sin1", [128, 64]); phi = B("phi", [128, 64])
        car = B("car", [128, 2, 64]); wini = B("wini", [128, 2, 64]); ctmp = B("ctmp", [128, 4, 64])
        halfpi = B("halfpi", [128, 1])
        magic = B("magic", [128, 1]); magic2 = B("magic2", [128, 1])
        sml = B("sml", [128, 16])

        def uf(lo_b, n):
            return U.t[:, lo_b // 4: lo_b // 4 + n]

        def ubf(lo_b, n):
            return U.t[:, lo_b // 4: lo_b // 4 + n // 2].bitcast(BF16)

        def ur(lo_b, nbytes):
            return ("U", lo_b // 4, (lo_b + nbytes) // 4)

        K1 = 1024
        Qv = ubf(0, 16 * 512).rearrange("p (c t) -> p c t", c=16)
        gbv = Qv
        ubv = ubf(16 * K1, 16 * 512).rearrange("p (c t) -> p c t", c=16)

        def Qr(c):
            return ur(c * 1024, 1024)

        def ubr(c):
            return ur(16 * K1 + c * 1024, 1024)

        X0 = 32 * K1

        pctr = [0]

        def pnext():
            b = pctr[0] % 7
            pctr[0] += 1
            return b

        wctr = [0]

        def wtile(Wd, r0, c0, ncols=512):
            s = wctr[0] % NW
            wctr[0] += 1
            S.dma("pool", wr.t[:, s, :, 0:ncols],
                  Wd[r0:r0 + 512, c0:c0 + ncols].rearrange("(a p) n -> p a n", p=128),
                  writes=[wr.r(s * 2048, (s + 1) * 2048)], key="w%d" % s)
            return s

        def wreg(s):
            return wr.r(s * 2048, (s + 1) * 2048)

        def dve(fn, reads=(), writes=()):
            return S.op("dve", fn, reads, writes)

        def act(fn, reads=(), writes=()):
            return S.op("act", fn, reads, writes)

        def pe(fn, reads=(), writes=()):
            return S.op("pe", fn, reads, writes)

        def group_mm(Wd, nk4, c0, rhs, epi, nj=4):
            banks = [pnext() for _ in range(nj)]
            nk = nk4 * 4
            for k4 in range(nk4):
                s = wtile(Wd, k4 * 512, c0, nj * 128)
                for a in range(4):
                    kc = k4 * 4 + a
                    rap, rreg = rhs(kc)
                    for j in range(nj):
                        pe(lambda e, b=banks[j], s=s, a=a, j=j, rap=rap, kc=kc:
                           e.matmul(ps[b].t[:], wr.t[:, s, a, j * 128:(j + 1) * 128], rap,
                                    start=(kc == 0), stop=(kc == nk - 1)),
                           reads=[wreg(s), rreg], writes=[ps[banks[j]].all()])
            for j in range(nj):
                epi(j, banks[j])

        def dump(name, ap, reads):
            if debug and name in debug:
                dd = nc.dram_tensor("dbg_" + name, list(ap.shape), ap.dtype, kind="ExternalOutput").ap()
                S.dma("sp", dd, ap, reads=reads, key="dbg_" + name)

        for n, s in SMALL_IN:
            S.dma("sp", sm[n].t[:], dr[n], writes=[sm[n].all()], key="ld_" + n)
        for n, s in CAST_IN:
            S.dma("pool", cb[n].t[:], dr[n], writes=[cb[n].all()], key="ld_" + n)
        dve(lambda e: e.memset(ones.t[:], 1.0), writes=[ones.all()])
        dve(lambda e: e.memset(halfpi.t[:], math.pi / 2), writes=[halfpi.all()])
        dve(lambda e: e.memset(magic.t[:], MAGIC), writes=[magic.all()])
        dve(lambda e: e.memset(magic2.t[:], -MAGIC), writes=[magic2.all()])
        dve(lambda e: e.memset(Va.t[:], 0.0), writes=[Va.all()])
        dve(lambda e: e.memset(Va.t[:, :, :, 64:65], 1.0), writes=[Va.all()])
        dve(lambda e: e.memset(Kb.t[:], 0.0), writes=[Kb.all()])
        dve(lambda e: e.memset(car.t[:], 0.0), writes=[car.all()])
        act(lambda e: e.activation(out=cact.t[:], in_=sm["cT"].t[:], func=AF.Silu),
            reads=[sm["cT"].all()], writes=[cact.all()])
        act(lambda e: e.activation(out=esink.t[:], in_=sm["sinks_bc"].t[:], func=AF.Exp),
            reads=[sm["sinks_bc"].all()], writes=[esink.all()])

        def ada_all(slabs, banks):
            for slab in slabs:
                for k4 in range(8):
                    s = wtile(dr["w_ada"], k4 * 512, slab * 512)
                    for a in range(4):
                        kc = k4 * 4 + a
                        for j in range(4):
                            pe(lambda e, s=s, a=a, j=j, kc=kc, b=banks[j]:
                               e.matmul(ps[b].t[:, 0:1], wr.t[:, s, a, j * 128:(j + 1) * 128],
                                        cact.t[:, kc:kc + 1], start=(kc == 0), stop=(kc == 31)),
                               reads=[wreg(s), cact.all()], writes=[ps[banks[j]].all()])
                    if k4 == 7:
                        for j in range(4):
                            col = 4 * slab + j
                            dve(lambda e, b=banks[j], col=col: e.tensor_tensor(out=mods.t[:, col:col + 1], in0=ps[b].t[:, 0:1],
                                                                              in1=sm["b_adaT"].t[:, col:col + 1], op=ALU.add),
                                reads=[ps[banks[j]].all(), sm["b_adaT"].all()], writes=[mods.r(col, col + 1)])
                    yield

        sh1 = mods.t[:, 0:32]; sc1 = mods.t[:, 32:64]; gt1 = mods.t[:, 64:96]
        sh2 = mods.t[:, 96:128]; sc2 = mods.t[:, 128:160]; gt2 = mods.t[:, 160:192]

        def sincos(ang, n, tmpA, tmpB, sin_out, cos_out, rd, wrs):
            act(lambda e: e.activation(out=tmpA, in_=ang, func=AF.Identity, scale=1.0 / TWO_PI, bias=magic.t[:]), reads=rd + [magic.all()], writes=wrs)
            act(lambda e: e.activation(out=tmpB, in_=tmpA, func=AF.Identity, scale=1.0, bias=magic2.t[:]), reads=wrs + [magic2.all()], writes=wrs)
            dve(lambda e: e.scalar_tensor_tensor(out=ang, in0=tmpB, scalar=-TWO_PI, in1=ang, op0=ALU.mult, op1=ALU.add), reads=rd + wrs, writes=rd)
            dve(lambda e: e.tensor_scalar(ang, ang, 3.1415925, -3.1415925, op0=ALU.min, op1=ALU.max), reads=rd, writes=rd)
            act(lambda e: e.activation(out=sin_out, in_=ang, func=AF.Sin), reads=rd, writes=wrs)
            dve(lambda e: e.scalar_tensor_tensor(out=tmpA, in0=ang, scalar=-1.0, in1=ang, op0=ALU.mult, op1=ALU.max), reads=rd, writes=wrs)
            act(lambda e: e.activation(out=cos_out, in_=tmpA, func=AF.Sin, bias=halfpi.t[:], scale=-1.0),
                reads=wrs + [halfpi.all()], writes=wrs)

        def sincos_g(ang, tmpA, tmpB, sin_out, cos_out, rd, wrs):
            act(lambda e: e.activation(out=tmpA, in_=ang, func=AF.Identity, scale=1.0 / TWO_PI, bias=magic.t[:]), reads=rd + [magic.all()], writes=wrs)
            yield
            act(lambda e: e.activation(out=tmpB, in_=tmpA, func=AF.Identity, scale=1.0, bias=magic2.t[:]), reads=wrs + [magic2.all()], writes=wrs)
            yield
            dve(lambda e: e.scalar_tensor_tensor(out=ang, in0=tmpB, scalar=-TWO_PI, in1=ang, op0=ALU.mult, op1=ALU.add), reads=rd + wrs, writes=rd)
            yield
            dve(lambda e: e.tensor_scalar(ang, ang, 3.1415925, -3.1415925, op0=ALU.min, op1=ALU.max), reads=rd, writes=rd)
            yield
            act(lambda e: e.activation(out=sin_out, in_=ang, func=AF.Sin), reads=rd, writes=wrs)
            yield
            dve(lambda e: e.scalar_tensor_tensor(out=tmpA, in0=ang, scalar=-1.0, in1=ang, op0=ALU.mult, op1=ALU.max), reads=rd, writes=wrs)
            yield
            act(lambda e: e.activation(out=cos_out, in_=tmpA, func=AF.Sin, bias=halfpi.t[:], scale=-1.0),
                reads=wrs + [halfpi.all()], writes=wrs)
            yield

        av = acc.t[:].rearrange("p c t -> p (c t)")

        def a_(i):
            return av[:, i * 2048:(i + 1) * 2048]

        AR = [acc.all()]
        for i, n in enumerate(["lamre_e", "lamim_e", "lstep_e", "bre_e", "bim_e"]):
            S.dma("sp", a_(i), dr[n], writes=AR, key="ld_big")
        LRE, LIM, LST, BRE, BIM, T5, T6, T7 = [a_(i) for i in range(8)]
        act(lambda e: e.activation(out=LST, in_=LST, func=AF.Exp), reads=AR, writes=AR)
        dve(lambda e: e.tensor_tensor(out=T5, in0=LIM, in1=LST, op=ALU.mult), reads=AR, writes=AR)
        dve(lambda e: e.tensor_tensor(out=LST, in0=LRE, in1=LST, op=ALU.mult), reads=AR, writes=AR)
        act(lambda e: e.activation(out=LST, in_=LST, func=AF.Exp), reads=AR, writes=AR)
        hbf = hb.t[:].rearrange("p c t -> p (c t)").bitcast(F32)
        HR = [hb.all()]
        sincos(T5, 2048, hbf[:, 0:2048], hbf[:, 2048:4096], T6, T7, AR, HR + AR)
        dve(lambda e: e.tensor_tensor(out=T6, in0=T6, in1=LST, op=ALU.mult), reads=AR, writes=AR)
        dve(lambda e: e.tensor_tensor(out=T7, in0=T7, in1=LST, op=ALU.mult), reads=AR, writes=AR)
        dve(lambda e: e.tensor_scalar(T7, T7, -1.0, None, op0=ALU.add), reads=AR, writes=AR)
        h0, h1, h2, h3 = [hbf[:, i * 2048:(i + 1) * 2048] for i in range(4)]
        dve(lambda e: e.tensor_tensor(out=h0, in0=LRE, in1=LRE, op=ALU.mult), reads=AR, writes=HR)
        dve(lambda e: e.tensor_tensor(out=h1, in0=LIM, in1=LIM, op=ALU.mult), reads=AR, writes=HR)
        dve(lambda e: e.tensor_tensor(out=h0, in0=h0, in1=h1, op=ALU.add), reads=HR, writes=HR)
        dve(lambda e: e.reciprocal(h0, h0), reads=HR, writes=HR)
        dve(lambda e: e.tensor_tensor(out=h1, in0=T7, in1=LRE, op=ALU.mult), reads=AR, writes=HR)
        dve(lambda e: e.tensor_tensor(out=h3, in0=T6, in1=LIM, op=ALU.mult), reads=AR, writes=HR)
        dve(lambda e: e.tensor_tensor(out=h1, in0=h1, in1=h3, op=ALU.add), reads=HR, writes=HR)
        dve(lambda e: e.tensor_tensor(out=h1, in0=h1, in1=h0, op=ALU.mult), reads=HR, writes=HR)
        dve(lambda e: e.tensor_tensor(out=h2, in0=T6, in1=LRE, op=ALU.mult), reads=AR, writes=HR)
        dve(lambda e: e.tensor_tensor(out=h3, in0=T7, in1=LIM, op=ALU.mult), reads=AR, writes=HR)
        dve(lambda e: e.tensor_tensor(out=h2, in0=h2, in1=h3, op=ALU.subtract), reads=HR, writes=HR)
        dve(lambda e: e.tensor_tensor(out=h2, in0=h2, in1=h0, op=ALU.mult), reads=HR, writes=HR)
        dve(lambda e: e.tensor_tensor(out=T5, in0=h1, in1=BRE, op=ALU.mult), reads=AR + HR, writes=AR)
        dve(lambda e: e.tensor_tensor(out=h3, in0=h2, in1=BIM, op=ALU.mult), reads=AR + HR, writes=HR)
        dve(lambda e: e.tensor_tensor(out=Bexp.t[:, :, 0, :], in0=T5.rearrange("p (c n) -> p c n", c=16),
                                      in1=h3.rearrange("p (c n) -> p c n", c=16), op=ALU.subtract),
            reads=AR + HR, writes=[Bexp.all()])
        dve(lambda e: e.tensor_tensor(out=T5, in0=h1, in1=BIM, op=ALU.mult), reads=AR + HR, writes=AR)
        dve(lambda e: e.tensor_tensor(out=h3, in0=h2, in1=BRE, op=ALU.mult), reads=AR + HR, writes=HR)
        dve(lambda e: e.tensor_tensor(out=Bexp.t[:, :, 1, :], in0=T5.rearrange("p (c n) -> p c n", c=16),
                                      in1=h3.rearrange("p (c n) -> p c n", c=16), op=ALU.add),
            reads=AR + HR, writes=[Bexp.all()])

        lnr = B("lnr", [128, 64]); nphi = B("nphi", [128, 64]); phi511 = B("phi511", [128, 64])
        nlnr = B("nlnr", [128, 64]); lnr511 = B("lnr511", [128, 64]); Gre = B("Gre", [128, 64]); Gim = B("Gim", [128, 64])
        accS = B("accS", [128, 4, 64])
        PR = [rr.all(), cos1.all(), sin1.all(), phi.all(), ctmp.all(), lnr.all(), nphi.all(), phi511.all(), nlnr.all(), lnr511.all(), Gre.all(), Gim.all()]
        st_p = ctmp.t[:, 0, :]
        act(lambda e: e.activation(out=st_p, in_=sm["lstep_p"].t[:], func=AF.Exp), reads=[sm["lstep_p"].all()], writes=PR)
        dve(lambda e: e.tensor_tensor(out=lnr.t[:], in0=sm["lamre_p"].t[:], in1=st_p, op=ALU.mult), reads=PR + [sm["lamre_p"].all()], writes=PR)
        act(lambda e: e.activation(out=rr.t[:], in_=lnr.t[:], func=AF.Exp), reads=PR, writes=PR)
        dve(lambda e: e.tensor_tensor(out=phi.t[:], in0=sm["lamim_p"].t[:], in1=st_p, op=ALU.mult), reads=PR + [sm["lamim_p"].all()], writes=PR)
        sincos(phi.t[:], 64, ctmp.t[:, 1, :], ctmp.t[:, 2, :], sin1.t[:], cos1.t[:], PR, PR)

        dve(lambda e: e.tensor_scalar(nphi.t[:], phi.t[:], -1.0, None, op0=ALU.mult), reads=PR, writes=PR)
        dve(lambda e: e.tensor_scalar(phi511.t[:], phi.t[:], 511.0, None, op0=ALU.mult), reads=PR, writes=PR)
        dve(lambda e: e.tensor_scalar(nlnr.t[:], lnr.t[:], -1.0, None, op0=ALU.mult), reads=PR, writes=PR)
        dve(lambda e: e.tensor_scalar(lnr511.t[:], lnr.t[:], 511.0, None, op0=ALU.mult), reads=PR, writes=PR)
        dve(lambda e: e.tensor_scalar(ctmp.t[:, 0, :], phi.t[:], 512.0, None, op0=ALU.mult), reads=PR, writes=PR)
        sincos(ctmp.t[:, 0, :], 64, ctmp.t[:, 1, :], ctmp.t[:, 2, :], Gim.t[:], Gre.t[:], PR, PR)
        act(lambda e: e.activation(out=ctmp.t[:, 3, :], in_=lnr.t[:], func=AF.Exp, scale=512.0), reads=PR, writes=PR)
        dve(lambda e: e.tensor_tensor(out=Gre.t[:], in0=Gre.t[:], in1=ctmp.t[:, 3, :], op=ALU.mult), reads=PR, writes=PR)
        dve(lambda e: e.tensor_tensor(out=Gim.t[:], in0=Gim.t[:], in1=ctmp.t[:, 3, :], op=ALU.mult), reads=PR, writes=PR)

        c511 = B("c511", [128, 64]); s511 = B("s511", [128, 64]); wlast = B("wlast", [128, 2, 64])
        PR = PR + [c511.all(), s511.all()]
        dve(lambda e: e.tensor_scalar(ctmp.t[:, 0, :], phi.t[:], 511.0, None, op0=ALU.mult), reads=PR, writes=PR)
        sincos(ctmp.t[:, 0, :], 64, ctmp.t[:, 1, :], ctmp.t[:, 2, :], s511.t[:], c511.t[:], PR, PR)
        iotv = uf(X0 + 12288, 512); iot_r = ur(X0 + 12288, 2048)
        revv = uf(X0 + 14336, 512); rev_r = ur(X0 + 14336, 2048)
        S.dma("sp", iotv, dr["iota"], writes=[iot_r], key="ld_iota")
        dve(lambda e: e.tensor_scalar(revv, iotv, -1.0, 511.0, op0=ALU.mult, op1=ALU.add), reads=[iot_r], writes=[rev_r])
        NBT = 4
        tbb = [av[:, i * 2048:(i + 1) * 2048] for i in range(6)]
        v3 = lambda ap: ap.rearrange("p (q t) -> p q t", q=NBT)
        bc_t = lambda ap: ap.unsqueeze(1).to_broadcast([128, NBT, 512])
        hbs = [hbf[:, i * 2048:(i + 1) * 2048] for i in range(4)]

        def gen_E_batch(bt):
            p0 = bt * NBT
            bc_p = lambda buf: buf.t[:, p0:p0 + NBT].unsqueeze(2).to_broadcast([128, NBT, 512])
            dve(lambda e, a=bc_t(iotv), b=bc_p(nphi): e.tensor_tensor(out=v3(hbs[0]), in0=a, in1=b, op=ALU.mult), reads=[iot_r] + PR, writes=HR)
            yield
            yield from sincos_g(hbs[0], hbs[1], hbs[2], hbs[3], hbs[2], HR, HR)
            S.dma("sp", tabs[p0:p0 + NBT, 1].rearrange("q p t -> p q t"), v3(hbs[3]), reads=HR, writes=[("tabs", p0, p0 + NBT)], key="tabw")
            S.dma("sp", tabs[p0:p0 + NBT, 0].rearrange("q p t -> p q t"), v3(hbs[2]), reads=HR, writes=[("tabs", p0, p0 + NBT)], key="tabw")
            yield

        def gen_W_batch(bt):
            p0 = bt * NBT
            bc_p = lambda buf: buf.t[:, p0:p0 + NBT].unsqueeze(2).to_broadcast([128, NBT, 512])
            dve(lambda e, a=bc_t(revv), b=bc_p(phi): e.tensor_tensor(out=v3(tbb[0]), in0=a, in1=b, op=ALU.mult), reads=[rev_r] + PR, writes=AR)
            yield
            dve(lambda e, a=bc_t(revv), b=bc_p(lnr): e.tensor_tensor(out=v3(tbb[3]), in0=a, in1=b, op=ALU.mult), reads=[rev_r] + PR, writes=AR)
            yield
            act(lambda e: e.activation(out=tbb[3], in_=tbb[3], func=AF.Exp), reads=AR, writes=AR)
            yield
            yield from sincos_g(tbb[0], tbb[1], tbb[2], tbb[4], tbb[5], AR, AR)
            dve(lambda e: e.tensor_tensor(out=tbb[1], in0=tbb[5], in1=tbb[3], op=ALU.mult), reads=AR, writes=AR)
            yield
            dve(lambda e: e.tensor_tensor(out=tbb[2], in0=tbb[4], in1=tbb[3], op=ALU.mult), reads=AR, writes=AR)
            yield
            S.dma("sp", wtabs[p0:p0 + NBT, 0].rearrange("q p t -> p q t"), v3(tbb[1]), reads=AR, writes=[("wtabs", p0, p0 + NBT)], key="wtabw")
            S.dma("sp", wtabs[p0:p0 + NBT, 1].rearrange("q p t -> p q t"), v3(tbb[2]), reads=AR, writes=[("wtabs", p0, p0 + NBT)], key="wtabw")
            yield

        def interleave(*gens):
            gens = list(gens)
            while gens:
                for g_ in list(gens):
                    try:
                        next(g_)
                    except StopIteration:
                        gens.remove(g_)

        for i in range(16):
            for _ in ada_all(range(i, i + 1), [0, 1, 2, 3]):
                pass
            interleave(gen_W_batch(i), gen_E_batch(i))
        pctr[0] = 4
        dve(lambda e: e.scalar_tensor_tensor(out=gs1.t[:], in0=sc1, scalar=1.0, in1=sm["n1g"].t[:], op0=ALU.add, op1=ALU.mult),
            reads=[mods.all(), sm["n1g"].all()], writes=[gs1.all()])

        sqv = [uf(X0 + i * 2048, 512) for i in range(2)]
        sqr = [ur(X0 + i * 2048, 2048) for i in range(2)]
        rstd = uf(X0 + 4096, 512); rstd_r = ur(X0 + 4096, 2048)
        ntv = [uf(X0 + 6144 + i * 2048, 512) for i in range(2)]
        ntr = [ur(X0 + 6144 + i * 2048, 2048) for i in range(2)]
        s2v = uf(X0 + 10240, 512); s2_r = ur(X0 + 10240, 2048)
        sq4 = [uf(X0 + 12288 + i * 2048, 512) for i in range(4)]; sq4_r = [ur(X0 + 12288 + i * 2048, 2048) for i in range(4)]
        sqc = [0]

        class Stats:
            def __init__(self):
                self.i = 0

            def add(self, ap, reg):
                q = sqc[0] % 4
                sqc[0] += 1
                if self.i == 0:
                    act(lambda e: e.activation(out=s2v, in_=ap, func=AF.Square), reads=[reg], writes=[s2_r])
                else:
                    act(lambda e: e.activation(out=sq4[q], in_=ap, func=AF.Square), reads=[reg], writes=[sq4_r[q]])
                    dve(lambda e: e.tensor_tensor(out=s2v, in0=s2v, in1=sq4[q], op=ALU.add), reads=[s2_r, sq4_r[q]], writes=[s2_r])
                self.i += 1

            def finish(self, nfeat):
                pe(lambda e: e.matmul(ps[SSB].t[:], ones.t[:], s2v, start=True, stop=True), reads=[ones.all(), s2_r], writes=[ps[SSB].all()])
                finish_stats(nfeat)

        SSB = 7

        def accr(c):
            return acc.r(c * T, (c + 1) * T)

        def hbr(c):
            return hb.r(c * T, (c + 1) * T)

        def rms_stats(chunks, nfeat):
            n = len(chunks)
            for i, (ap, reg) in enumerate(chunks):
                q = i % 2
                act(lambda e, ap=ap, q=q: e.activation(out=sqv[q], in_=ap, func=AF.Square), reads=[reg], writes=[sqr[q]])
                pe(lambda e, q=q, i=i: e.matmul(ps[SSB].t[:], ones.t[:], sqv[q], start=(i == 0), stop=(i == n - 1)),
                   reads=[ones.all(), sqr[q]], writes=[ps[SSB].all()])
            finish_stats(nfeat)

        def finish_stats(nfeat):
            dve(lambda e: e.tensor_scalar(rstd, ps[SSB].t[:], 1.0 / nfeat, EPS, op0=ALU.mult, op1=ALU.add),
                reads=[ps[SSB].all()], writes=[rstd_r])
            act(lambda e: e.activation(out=rstd, in_=rstd, func=AF.Sqrt), reads=[rstd_r], writes=[rstd_r])
            dve(lambda e: e.reciprocal(rstd, rstd), reads=[rstd_r], writes=[rstd_r])

        def norm_mod(gs, sh, st=None):
            if st is None:
                rms_stats([(acc.t[:, c, :], accr(c)) for c in range(KC)], D)
            else:
                st.finish(D)
            for c in range(KC):
                q = c % 2
                dve(lambda e, c=c, q=q: e.tensor_tensor(out=ntv[q], in0=acc.t[:, c, :], in1=rstd, op=ALU.mult),
                    reads=[accr(c), rstd_r], writes=[ntr[q]])
                act(lambda e, c=c, q=q: e.activation(out=hb.t[:, c, :], in_=ntv[q], func=AF.Identity,
                                                    bias=sh[:, c:c + 1], scale=gs[:, c:c + 1]),
                    reads=[ntr[q], mods.all(), gs1.all(), gs2.all()], writes=[hbr(c)])

        xs = [uf(i * 8192, 2048) for i in range(2)]
        xsr = [ur(i * 8192, 8192) for i in range(2)]

        def load_x(row0):
            cnt = 0
            for blk in range(4):
                for half in range(2):
                    q = cnt % 2
                    cnt += 1
                    S.dma("sp", xs[q], dr["xcat"][row0 + blk * 128: row0 + (blk + 1) * 128, half * 2048:(half + 1) * 2048],
                          writes=[xsr[q]], key="xs%d" % q)
                    for c4 in range(4):
                        b = pnext()
                        for cc in range(4):
                            pe(lambda e, b=b, q=q, c4=c4, cc=cc:
                               e.transpose(ps[b].t[:, cc * 128:(cc + 1) * 128], xs[q][:, (c4 * 4 + cc) * 128:(c4 * 4 + cc + 1) * 128], sm["identf"].t[:]),
                               reads=[xsr[q], sm["identf"].all()], writes=[ps[b].all()])
                        c0 = half * 16 + c4 * 4
                        fn = lambda e, b=b, c0=c0, blk=blk: e.tensor_copy(
                            acc.t[:, c0:c0 + 4, blk * 128:(blk + 1) * 128], ps[b].t[:].rearrange("p (c t) -> p c t", c=4))
                        fn2 = lambda e, b=b, c0=c0, blk=blk: e.activation(
                            out=acc.t[:, c0:c0 + 4, blk * 128:(blk + 1) * 128], in_=ps[b].t[:].rearrange("p (c t) -> p c t", c=4), func=AF.Identity)
                        S.op("dve" if c4 % 2 == 0 else "act", fn if c4 % 2 == 0 else fn2,
                             reads=[ps[b].all()], writes=[acc.r(c0 * T, (c0 + 4) * T)])

        rcv = uf(X0, 512); rsv = uf(X0 + 2048, 512)
        rc_r = ur(X0, 4096)
        t4v = [uf(X0 + 4096 + i * 2048, 512) for i in range(4)]
        t4r = ur(X0 + 4096, 8192)

        def rope_epi(ba, bb, outA, outB, regA, regB):
            dve(lambda e: e.tensor_tensor(out=t4v[0], in0=ps[ba].t[:], in1=rcv, op=ALU.mult), reads=[ps[ba].all(), rc_r], writes=[t4r])
            dve(lambda e: e.tensor_tensor(out=t4v[1], in0=ps[bb].t[:], in1=rsv, op=ALU.mult), reads=[ps[bb].all(), rc_r], writes=[t4r])
            dve(lambda e: e.tensor_tensor(out=t4v[2], in0=ps[bb].t[:], in1=rcv, op=ALU.mult), reads=[ps[bb].all(), rc_r], writes=[t4r])
            dve(lambda e: e.tensor_tensor(out=t4v[3], in0=ps[ba].t[:], in1=rsv, op=ALU.mult), reads=[ps[ba].all(), rc_r], writes=[t4r])
            dve(lambda e: e.tensor_tensor(out=outA, in0=t4v[0], in1=t4v[1], op=ALU.subtract), reads=[t4r], writes=[regA])
            dve(lambda e: e.tensor_tensor(out=outB, in0=t4v[2], in1=t4v[3], op=ALU.add), reads=[t4r], writes=[regB])

        def in_proj(ti, pre, want_kv):
            col0 = ti * T
            S.dma("sp", rcv, dr["ropec"][:, col0:col0 + T], writes=[rc_r], key="rope")
            S.dma("sp", rsv, dr["ropes"][:, col0:col0 + T], writes=[rc_r], key="rope")
            rhs = lambda kc: (hb.t[:, kc, :], hbr(kc))
            if not pre:
                for slab in range(4):
                    banks = []
                    group_mm(dr["w_in"], 8, slab * 512, rhs, lambda j, b: banks.append(b))
                    for pr in range(2):
                        g = slab * 2 + pr
                        rope_epi(banks[2 * pr], banks[2 * pr + 1], Qv[:, 2 * g, :], Qv[:, 2 * g + 1, :], Qr(2 * g), Qr(2 * g + 1))
            if want_kv:
                for slab in range(2):
                    banks = []
                    group_mm(dr["w_in"], 8, 2048 + slab * 512, rhs, lambda j, b: banks.append(b))
                    for pr in range(2):
                        a = slab * 2 + pr
                        rope_epi(banks[2 * pr], banks[2 * pr + 1], Kb.t[:, 2 * a, 128:640], Kb.t[:, 2 * a + 1, 128:640],
                                 Kb.r((2 * a) * 640 + 128, (2 * a + 1) * 640), Kb.r((2 * a + 1) * 640 + 128, (2 * a + 2) * 640))
            for slab in range(4):
                def epi(j, b, slab=slab):
                    c = slab * 4 + j
                    act(lambda e: e.activation(out=ubv[:, c, :], in_=ps[b].t[:], func=AF.Identity), reads=[ps[b].all()], writes=[ubr(c)])
                group_mm(dr["w_in"], 8, 3072 + slab * 512, rhs, epi)
                if pre:
                    ada_full_slab()
            if want_kv:
                banks = [pnext() for _ in range(4)]
                for k4 in range(8):
                    s = wtile(dr["w_in"], k4 * 512, 5120, 256)
                    for a in range(4):
                        kc = k4 * 4 + a
                        for blk in range(4):
                            pe(lambda e, b=banks[blk], s=s, a=a, kc=kc, blk=blk:
                               e.matmul(ps[b].t[:, 0:256], hb.t[:, kc, blk * 128:(blk + 1) * 128], wr.t[:, s, a, 0:256],
                                        start=(kc == 0), stop=(kc == 31)),
                               reads=[wreg(s), hbr(kc)], writes=[ps[banks[blk]].all()])
                for blk in range(4):
                    b = banks[blk]
                    act(lambda e, b=b, blk=blk: e.activation(out=Va.t[:, 1 + blk, :, 0:64], in_=ps[b].t[:, 0:256].rearrange("p (h d) -> p h d", h=4), func=AF.Identity),
                        reads=[ps[b].all()], writes=[Va.r((1 + blk) * 260, (2 + blk) * 260)])

        def shift_halo():
            dve(lambda e: e.tensor_copy(Kb.t[:, :, 0:128], Kb.t[:, :, 512:640]), reads=[Kb.all()], writes=[Kb.all()])
            dve(lambda e: e.tensor_copy(Va.t[:, 0, :, :], Va.t[:, 4, :, :]), reads=[Va.all()], writes=[Va.all()])

        A0 = X0
        attok = uf(A0, 2048); attok_r = ur(A0, 8192)
        atn = ubf(A0 + 8192, 2048); atn_r = ur(A0 + 8192, 4096)
        NEB = 4
        ebv = [ubf(A0 + 12288 + i * 512, 256) for i in range(NEB)]
        ebr = [ur(A0 + 12288 + i * 512, 512) for i in range(NEB)]
        emv = [ubf(A0 + 14336 + i * 512, 256) for i in range(NEB)]
        emr = [ur(A0 + 14336 + i * 512, 512) for i in range(NEB)]
        atj = atn; atj_r = atn_r

        def attention(first_tile):
            for qb in range(4):
                msk = cb["mask_f"] if (first_tile and qb == 0) else cb["mask_n"]
                heads = [(g, j) for g in range(8) for j in range(4)]
                st = {}
                pob = {}

                def stageA(n, qb=qb):
                    g, j = heads[n]
                    a = g // 2
                    q = n % NEB
                    sb = pnext()
                    st[n] = (sb, q)
                    for kb in range(2):
                        kcol = (qb + kb) * 128
                        for ab in range(2):
                            pe(lambda e, sb=sb, kb=kb, ab=ab, kcol=kcol, g=g, j=j, a=a:
                               e.matmul(ps[sb].t[:, kb * 128:(kb + 1) * 128],
                                        Kb.t[32 * j:32 * j + 32, 2 * a + ab, kcol:kcol + 128],
                                        Qv[32 * j:32 * j + 32, 2 * g + ab, qb * 128:(qb + 1) * 128],
                                        start=(ab == 0), stop=(ab == 1), tile_position=(32 * j, 0)),
                               reads=[Kb.all(), Qr(2 * g + ab)], writes=[ps[sb].all()])
                    act(lambda e, sb=sb, q=q: e.activation(out=ebv[q], in_=ps[sb].t[:, 0:256], func=AF.Exp, scale=0.125),
                        reads=[ps[sb].all()], writes=[ebr[q]])

                def stageM(n, msk=msk):
                    q = st[n][1]
                    dve(lambda e, q=q: e.tensor_tensor(out=emv[q], in0=ebv[q], in1=msk.t[:], op=ALU.mult),
                        reads=[ebr[q], msk.all()], writes=[emr[q]])

                def stageP(n, qb=qb):
                    g, j = heads[n]
                    a = g // 2
                    q = st[n][1]
                    if j == 0:
                        pob[g] = pnext()
                    po = pob[g]
                    for kb in range(2):
                        pe(lambda e, po=po, q=q, kb=kb, j=j, a=a:
                           e.matmul(ps[po].t[:, j * 65:(j + 1) * 65], emv[q][:, kb * 128:(kb + 1) * 128],
                                    Va.t[:, qb + kb, a, :], start=(kb == 0), stop=(kb == 1)),
                           reads=[emr[q], Va.all()], writes=[ps[po].all()])
                    if j == 3:
                        pov = ps[po].t[:, 0:260].rearrange("p (h d) -> p h d", d=65)
                        SR = [sml.all()]
                        dve(lambda e, pov=pov, g=g: e.tensor_tensor(out=sml.t[:, 0:4], in0=pov[:, :, 64], in1=esink.t[:, 4 * g:4 * g + 4], op=ALU.add),
                            reads=[ps[po].all(), esink.all()], writes=SR)
                        dve(lambda e: e.reciprocal(sml.t[:, 4:8], sml.t[:, 0:4]), reads=SR, writes=SR)
                        dve(lambda e, pov=pov, g=g: e.tensor_tensor(
                            out=attok[:, g * 256:(g + 1) * 256].rearrange("p (h d) -> p h d", d=64), in0=pov[:, :, 0:64],
                            in1=sml.t[:, 4:8].unsqueeze(2).to_broadcast([128, 4, 64]), op=ALU.mult),
                            reads=[ps[po].all()] + SR, writes=[attok_r])

                for n in range(32 + 3):
                    if n < 32:
                        stageA(n)
                    if 0 <= n - 2 < 32:
                        stageM(n - 2)
                    if 0 <= n - 3 < 32:
                        stageP(n - 3)
                SR = [sml.all()]
                act(lambda e: e.activation(out=atj, in_=attok, func=AF.Square, accum_out=sml.t[:, 8:9]), reads=[attok_r], writes=[atj_r] + SR)
                dve(lambda e: e.tensor_scalar(sml.t[:, 9:10], sml.t[:, 8:9], 1.0 / 2048, EPS, op0=ALU.mult, op1=ALU.add), reads=SR, writes=SR)
                act(lambda e: e.activation(out=sml.t[:, 9:10], in_=sml.t[:, 9:10], func=AF.Sqrt), reads=SR, writes=SR)
                dve(lambda e: e.reciprocal(sml.t[:, 10:11], sml.t[:, 9:10]), reads=SR, writes=SR)
                dve(lambda e: e.tensor_scalar(atn, attok, sml.t[:, 10:11], None, op0=ALU.mult), reads=[attok_r] + SR, writes=[atn_r])
                for c4 in range(4):
                    b = pnext()
                    pb = ps[b].t[:].bitcast(BF16)
                    for cc in range(4):
                        c = c4 * 4 + cc
                        pe(lambda e, pb=pb, cc=cc, c=c: e.transpose(pb[:, cc * 128:(cc + 1) * 128], atn[:, c * 128:(c + 1) * 128], cb["identb"].t[:]),
                           reads=[atn_r, cb["identb"].all()], writes=[ps[b].all()])
                    for cc in range(4):
                        c = c4 * 4 + cc
                        act(lambda e, pb=pb, cc=cc, c=c, qb=qb: e.activation(out=hb.t[:, c, qb * 128:(qb + 1) * 128], in_=pb[:, cc * 128:(cc + 1) * 128],
                                                                   func=AF.Identity, scale=sm["attn_gT"].t[:, c:c + 1]),
                            reads=[ps[b].all(), sm["attn_gT"].all()], writes=[hb.r(c * T + qb * 128, c * T + (qb + 1) * 128)])

        S0 = X0
        tEs = [uf(S0 + i * 4096, 1024).rearrange("p (c t) -> p c t", c=2) for i in range(2)]
        tErs = [ur(S0 + i * 4096, 4096) for i in range(2)]
        stmp = [uf(S0 + 8192 + i * 2048, 512) for i in range(2)]; stmp_r = [ur(S0 + 8192 + i * 2048, 2048) for i in range(2)]
        mmv = [uf(S0 + 12288 + i * 2048, 512) for i in range(2)]; mm_r = [ur(S0 + 12288 + i * 2048, 2048) for i in range(2)]
        wwv = [uf(S0 + 16384 + i * 2048, 512) for i in range(2)]; ww_r = [ur(S0 + 16384 + i * 2048, 2048) for i in range(2)]
        zzv = [Kb.t[:, a_, 128:640] for a_ in range(4)]; zz_r = [Kb.r(a_ * 640 + 128, (a_ + 1) * 640) for a_ in range(4)]
        nwv = Kb.t[:, 4, 128:640]; nw_r = Kb.r(4 * 640 + 128, 5 * 640)
        wrbv = Kb.t[:, 5, 128:640]; wrb_r = Kb.r(5 * 640 + 128, 6 * 640)
        ecbv = Kb.t[:, 6, 128:640]; ecb_r = Kb.r(6 * 640 + 128, 7 * 640)
        snbv = Kb.t[:, 7, 128:640]; snb_r = Kb.r(7 * 640 + 128, 8 * 640)
        wibv = hb.t[:, 30, :]; wib_r = hbr(30)
        necbv = hb.t[:, 31, :]; necb_r = hbr(31)

        def ssm():
            CR = [car.all(), wini.all(), ctmp.all(), cos1.all(), sin1.all()]
            cre, cim = car.t[:, 0, :], car.t[:, 1, :]
            dve(lambda e: e.tensor_tensor(out=ctmp.t[:, 0, :], in0=cos1.t[:], in1=cre, op=ALU.mult), reads=CR, writes=CR)
            dve(lambda e: e.tensor_tensor(out=ctmp.t[:, 1, :], in0=sin1.t[:], in1=cim, op=ALU.mult), reads=CR, writes=CR)
            dve(lambda e: e.tensor_tensor(out=wini.t[:, 0, :], in0=ctmp.t[:, 0, :], in1=ctmp.t[:, 1, :], op=ALU.subtract), reads=CR, writes=CR)
            dve(lambda e: e.tensor_tensor(out=ctmp.t[:, 2, :], in0=sin1.t[:], in1=cre, op=ALU.mult), reads=CR, writes=CR)
            dve(lambda e: e.tensor_tensor(out=ctmp.t[:, 3, :], in0=cos1.t[:], in1=cim, op=ALU.mult), reads=CR, writes=CR)
            dve(lambda e: e.tensor_tensor(out=wini.t[:, 1, :], in0=ctmp.t[:, 2, :], in1=ctmp.t[:, 3, :], op=ALU.add), reads=CR, writes=CR)
            bbank = [0, 1, 2, 3]
            ybank = [4, 5]

            def bk(n):
                return bbank[(2 * n) % 4], bbank[(2 * n + 1) % 4]

            def stL_dma(n):
                S.dma("sp", tEs[n % 2], tabs[n].rearrange("c p t -> p c t"), reads=[("tabs", n, n + 1)], writes=[tErs[n % 2]], key="tabr%d" % (n % 2))

            def stL_pe(n):
                c_, j_ = n // 4, n % 4
                for ri, b in zip((0, 1), bk(n)):
                    pe(lambda e, ri=ri, b=b, c_=c_, j_=j_: e.matmul(ps[b].t[:], Bexp.t[32 * j_:32 * j_ + 32, c_, ri, :],
                                                                  ubv[32 * j_:32 * j_ + 32, c_, :], start=True, stop=True,
                                                                  tile_position=(32 * j_, 0)),
                       reads=[Bexp.all(), ubr(c_)], writes=[ps[b].all()])

            def stM(n):
                Ec, Es = tEs[n % 2][:, 0, :], tEs[n % 2][:, 1, :]
                tr = tErs[n % 2]
                bre, bim = bk(n)
                dve(lambda e: e.tensor_tensor(out=mmv[0], in0=ps[bre].t[:], in1=Ec, op=ALU.mult), reads=[ps[bre].all(), tr], writes=[mm_r[0]])
                dve(lambda e: e.tensor_tensor(out=stmp[0], in0=ps[bim].t[:], in1=Es, op=ALU.mult), reads=[ps[bim].all(), tr], writes=[stmp_r[0]])
                dve(lambda e: e.tensor_tensor(out=mmv[1], in0=ps[bim].t[:], in1=Ec, op=ALU.mult), reads=[ps[bim].all(), tr], writes=[mm_r[1]])
                dve(lambda e: e.tensor_tensor(out=stmp[1], in0=ps[bre].t[:], in1=Es, op=ALU.mult), reads=[ps[bre].all(), tr], writes=[stmp_r[1]])

            def stA(n):
                S.op("pool", lambda e: e.tensor_tensor(out=mmv[0], in0=mmv[0], in1=stmp[0], op=ALU.subtract), [mm_r[0], stmp_r[0]], [mm_r[0]])
                S.op("pool", lambda e: e.tensor_tensor(out=mmv[1], in0=mmv[1], in1=stmp[1], op=ALU.add), [mm_r[1], stmp_r[1]], [mm_r[1]])

            def stS(n):
                act(lambda e, n=n: e.activation(out=ecbv, in_=tEs[n % 2][:, 0, :], func=AF.Identity), reads=[tErs[n % 2]], writes=[ecb_r])
                act(lambda e, n=n: e.activation(out=snbv, in_=tEs[n % 2][:, 1, :], func=AF.Identity), reads=[tErs[n % 2]], writes=[snb_r])
                act(lambda e, n=n: e.activation(out=necbv, in_=tEs[n % 2][:, 0, :], func=AF.Identity, scale=-1.0), reads=[tErs[n % 2]], writes=[necb_r])
                for ri in range(2):
                    dve(lambda e, ri=ri, p=n: e.tensor_tensor_scan(out=wwv[ri], data0=rr.t[:, p:p + 1].to_broadcast([128, 512]), data1=mmv[ri],
                                                                  initial=wini.t[:, ri, p:p + 1], op0=ALU.mult, op1=ALU.add),
                        reads=[mm_r[ri], rr.all(), wini.all()], writes=[ww_r[ri]])
                act(lambda e: e.activation(out=wrbv, in_=wwv[0], func=AF.Identity), reads=[ww_r[0]], writes=[wrb_r])
                act(lambda e: e.activation(out=wibv, in_=wwv[1], func=AF.Identity), reads=[ww_r[1]], writes=[wib_r])
                for ri in range(2):
                    act(lambda e, ri=ri, p=n: e.activation(out=wlast.t[:, ri, p:p + 1], in_=wwv[ri][:, 511:512], func=AF.Identity),
                        reads=[ww_r[ri]], writes=[wlast.all()])

            def stZ(n):
                c_, j_ = n // 4, n % 4
                Ec, Es = tEs[n % 2][:, 0, :], tEs[n % 2][:, 1, :]
                tr = tErs[n % 2]
                dve(lambda e: e.tensor_tensor(out=zzv[0], in0=wrbv, in1=ecbv, op=ALU.mult), reads=[wrb_r, ecb_r], writes=[zz_r[0]])
                dve(lambda e: e.tensor_tensor(out=zzv[1], in0=wibv, in1=snbv, op=ALU.mult), reads=[wib_r, snb_r], writes=[zz_r[1]])
                dve(lambda e: e.tensor_tensor(out=zzv[2], in0=wrbv, in1=snbv, op=ALU.mult), reads=[wrb_r, snb_r], writes=[zz_r[2]])
                dve(lambda e: e.tensor_tensor(out=zzv[3], in0=wibv, in1=necbv, op=ALU.mult), reads=[wib_r, necb_r], writes=[zz_r[3]])
                yb = ybank[n % 2]
                for zi, cm in ((0, "cre_e"), (1, "cre_e"), (2, "cim_e"), (3, "cim_e")):
                    pe(lambda e, zi=zi, cm=cm, yb=yb, p=n: e.matmul(ps[yb].t[0:32, :], cb[cm].t[:, p, :], zzv[zi], start=(zi == 0), stop=False),
                       reads=[cb[cm].all(), zz_r[zi]], writes=[ps[yb].all()])
                pe(lambda e, yb=yb, c_=c_, j_=j_: e.matmul(ps[yb].t[0:32, :], cb["d_e"].t[32 * j_:32 * j_ + 32, c_, :], ubv[32 * j_:32 * j_ + 32, c_, :],
                                                           start=False, stop=True, tile_position=(32 * j_, 0)),
                   reads=[cb["d_e"].all(), ubr(c_)], writes=[ps[yb].all()])
                act(lambda e, yb=yb, c_=c_, j_=j_: e.activation(out=gbv[32 * j_:32 * j_ + 32, c_, :], in_=ps[yb].t[0:32, :], func=AF.Gelu),
                    reads=[ps[yb].all()], writes=[Qr(c_)])

            stL_dma(0)
            stL_pe(0)
            for n in range(64):
                if n + 1 < 64:
                    stL_pe(n + 1)
                stM(n)
                stA(n)
                if n >= 1:
                    stZ(n - 1)
                if n + 1 < 64:
                    stL_dma(n + 1)
                stS(n)
            stZ(63)
            FR = [car.all(), ctmp.all(), wlast.all(), c511.all(), s511.all()]
            wl_re, wl_im = wlast.t[:, 0, :], wlast.t[:, 1, :]
            TT = lambda o, a, b, op: dve(lambda e: e.tensor_tensor(out=o, in0=a, in1=b, op=op), reads=FR, writes=FR)
            TT(ctmp.t[:, 0, :], c511.t[:], wl_re, ALU.mult)
            TT(ctmp.t[:, 1, :], s511.t[:], wl_im, ALU.mult)
            TT(cre, ctmp.t[:, 0, :], ctmp.t[:, 1, :], ALU.subtract)
            TT(ctmp.t[:, 2, :], s511.t[:], wl_re, ALU.mult)
            TT(ctmp.t[:, 3, :], c511.t[:], wl_im, ALU.mult)
            TT(cim, ctmp.t[:, 2, :], ctmp.t[:, 3, :], ALU.add)

        tWs = [uf(X0 + i * 4096, 1024).rearrange("p (c t) -> p c t", c=2) for i in range(2)]
        tWrs = [ur(X0 + i * 4096, 4096) for i in range(2)]
        jkv = [uf(i * 2048, 512) for i in range(4)]; jk_r = [ur(i * 2048, 2048) for i in range(4)]

        ada_slab = [16]

        def ada_full_slab():
            if ada_slab[0] < 48:
                sl = ada_slab[0]
                ada_slab[0] += 1
                for _ in ada_all([sl], [pnext() for _ in range(4)]):
                    pass

        def ssm_pre():
            sl0 = ada_slab[0]
            ada_slab[0] = min(48, sl0 + 4)
            ada_it = ada_all(range(sl0, ada_slab[0]), [3, 4, 5, 6])
            bl = [0, 1, 2, 7]
            for pi_ in range(64):
                c_, j_ = pi_ // 4, pi_ % 4
                tWv, tW_r = tWs[pi_ % 2], tWrs[pi_ % 2]
                S.dma("sp", tWv, wtabs[pi_].rearrange("c p t -> p c t"), reads=[("wtabs", pi_, pi_ + 1)], writes=[tW_r], key="tabr%d" % (pi_ % 2))
                bre, bim = bl[(2 * pi_) % 4], bl[(2 * pi_ + 1) % 4]
                for ri, b in ((0, bre), (1, bim)):
                    pe(lambda e, ri=ri, b=b, c_=c_, j_=j_: e.matmul(ps[b].t[:], Bexp.t[32 * j_:32 * j_ + 32, c_, ri, :],
                                                                  ubv[32 * j_:32 * j_ + 32, c_, :], start=True, stop=True,
                                                                  tile_position=(32 * j_, 0)),
                       reads=[Bexp.all(), ubr(c_)], writes=[ps[b].all()])
                for k, (bank, wi) in enumerate(((bre, 0), (bim, 1), (bim, 0), (bre, 1))):
                    dve(lambda e, k=k, bank=bank, wi=wi, tWv=tWv: e.tensor_tensor(out=jkv[k], in0=ps[bank].t[:], in1=tWv[:, wi, :], op=ALU.mult),
                        reads=[ps[bank].all(), tW_r], writes=[jk_r[k]])
                    act(lambda e, k=k, p=pi_: e.activation(out=jkv[k], in_=jkv[k], func=AF.Identity, accum_out=accS.t[:, k, p:p + 1]),
                        reads=[jk_r[k]], writes=[jk_r[k], accS.all()])
                if pi_ % 2 == 1:
                    next(ada_it, None)
            for _ in ada_it:
                pass
            CR = [car.all(), ctmp.all(), accS.all(), Gre.all(), Gim.all()]
            cre, cim = car.t[:, 0, :], car.t[:, 1, :]
            c0, c1, c2, c3 = [ctmp.t[:, i, :] for i in range(4)]
            TT = lambda o, a, b, op: dve(lambda e: e.tensor_tensor(out=o, in0=a, in1=b, op=op), reads=CR, writes=CR)
            TT(c0, accS.t[:, 0, :], accS.t[:, 1, :], ALU.subtract)
            TT(c1, accS.t[:, 2, :], accS.t[:, 3, :], ALU.add)
            TT(c2, Gre.t[:], cre, ALU.mult)
            TT(c0, c0, c2, ALU.add)
            TT(c2, Gim.t[:], cim, ALU.mult)
            TT(c0, c0, c2, ALU.subtract)
            TT(c3, Gre.t[:], cim, ALU.mult)
            TT(c1, c1, c3, ALU.add)
            TT(c3, Gim.t[:], cre, ALU.mult)
            TT(c1, c1, c3, ALU.add)
            dve(lambda e: e.tensor_copy(cre, c0), reads=CR, writes=CR)
            dve(lambda e: e.tensor_copy(cim, c1), reads=CR, writes=CR)

        G0 = X0
        gatev = [uf(G0 + 12288 + i * 2048, 512) for i in range(2)]; gate_r = [ur(G0 + 12288 + i * 2048, 2048) for i in range(2)]
        prodv = [uf(G0 + 16384 + i * 2048, 512) for i in range(2)]; prod_r = [ur(G0 + 16384 + i * 2048, 2048) for i in range(2)]

        def glu_and_norm():
            rhs = lambda kc: (gbv[:, kc, :], Qr(kc))
            cnt = [0]
            for slab in range(4):
                def epi(j, b, slab=slab):
                    ob = slab * 4 + j
                    q = cnt[0] % 2
                    i = cnt[0]
                    cnt[0] += 1
                    act(lambda e: e.activation(out=gatev[q], in_=ps[b].t[:], func=AF.Sigmoid, bias=sm["b_gluT"].t[:, ob:ob + 1]),
                        reads=[ps[b].all(), sm["b_gluT"].all()], writes=[gate_r[q]])
                    dve(lambda e: e.tensor_tensor(out=prodv[q], in0=gbv[:, ob, :], in1=gatev[q], op=ALU.mult),
                        reads=[Qr(ob), gate_r[q]], writes=[prod_r[q]])
                    act(lambda e: e.activation(out=sqv[q], in_=prodv[q], func=AF.Square), reads=[prod_r[q]], writes=[sqr[q]])
                    pe(lambda e: e.matmul(ps[SSB].t[:], ones.t[:], sqv[q], start=(i == 0), stop=(i == 15)),
                       reads=[ones.all(), sqr[q]], writes=[ps[SSB].all()])
                    dve(lambda e: e.tensor_copy(hb.t[:, 16 + ob, :], prodv[q]), reads=[prod_r[q]], writes=[hbr(16 + ob)])
                group_mm(dr["w_glu"], 4, slab * 512, rhs, epi)
            finish_stats(2048)
            for ob in range(16):
                dve(lambda e, ob=ob: e.scalar_tensor_tensor(out=hb.t[:, 16 + ob, :], in0=hb.t[:, 16 + ob, :], scalar=sm["ssm_gT"].t[:, ob:ob + 1],
                                                         in1=rstd, op0=ALU.mult, op1=ALU.mult),
                    reads=[hbr(16 + ob), rstd_r, sm["ssm_gT"].all()], writes=[hbr(16 + ob)])

        def resid_epi(gate, st):
            def epi_factory(slab):
                def epi(j, b):
                    ob = slab * 4 + j
                    dve(lambda e: e.scalar_tensor_tensor(out=acc.t[:, ob, :], in0=ps[b].t[:], scalar=gate[:, ob:ob + 1], in1=acc.t[:, ob, :],
                                                         op0=ALU.mult, op1=ALU.add),
                        reads=[ps[b].all(), mods.all(), accr(ob)], writes=[accr(ob)])
                    st.add(acc.t[:, ob, :], accr(ob))
                return epi
            return epi_factory

        def out_proj():
            rhs = lambda kc: (hb.t[:, kc, :], hbr(kc))
            st = Stats()
            ef = resid_epi(gt1, st)
            for slab in range(8):
                group_mm(dr["w_out"], 8, slab * 512, rhs, ef(slab))
            return st

        hidv = [ubf(i * 4096, 2048).rearrange("p (a t) -> p a t", a=4) for i in range(2)]
        hid_r = [ur(i * 4096, 4096) for i in range(2)]
        rlv = [uf(8192 + i * 2048, 512) for i in range(2)]; rl_r = [ur(8192 + i * 2048, 2048) for i in range(2)]

        def ffn():
            rhs = lambda kc: (hb.t[:, kc, :], hbr(kc))
            cnt = [0]
            st = Stats()
            for f in range(32):
                hq = f % 2

                def epi(j, b, hq=hq):
                    q = cnt[0] % 2
                    cnt[0] += 1
                    act(lambda e: e.activation(out=rlv[q], in_=ps[b].t[:], func=AF.Relu), reads=[ps[b].all()], writes=[rl_r[q]])
                    dve(lambda e: e.scalar_tensor_tensor(out=hidv[hq][:, j, :], in0=ps[b].t[:], scalar=0.0, in1=rlv[q], op0=ALU.max, op1=ALU.mult),
                        reads=[ps[b].all(), rl_r[q]], writes=[ur(hq * 4096 + j * 1024, 1024)])
                group_mm(dr["w_ff1"], 8, f * 512, rhs, epi)
                for slab in range(8):
                    s = wtile(dr["w_ff2"], f * 512, slab * 512)
                    for j in range(4):
                        ob = slab * 4 + j
                        b = pnext()
                        for a in range(4):
                            pe(lambda e, b=b, s=s, a=a, j=j, hq=hq: e.matmul(ps[b].t[:], wr.t[:, s, a, j * 128:(j + 1) * 128], hidv[hq][:, a, :],
                                                                          start=(a == 0), stop=(a == 3)),
                               reads=[wreg(s), hid_r[hq]], writes=[ps[b].all()])
                        dve(lambda e, b=b, ob=ob: e.scalar_tensor_tensor(out=acc.t[:, ob, :], in0=ps[b].t[:], scalar=gt2[:, ob:ob + 1], in1=acc.t[:, ob, :],
                                                                     op0=ALU.mult, op1=ALU.add),
                            reads=[ps[b].all(), mods.all(), accr(ob)], writes=[accr(ob)])
                        if f == 31:
                            st.add(acc.t[:, ob, :], accr(ob))
            return st

        def final_out(row0, st):
            st.finish(D)
            for c in range(KC):
                dve(lambda e, c=c: e.scalar_tensor_tensor(out=acc.t[:, c, :], in0=acc.t[:, c, :], scalar=sm["fing"].t[:, c:c + 1], in1=rstd,
                                                       op0=ALU.mult, op1=ALU.mult),
                    reads=[accr(c), rstd_r, sm["fing"].all()], writes=[accr(c)])
            cnt = 0
            for blk in range(4):
                for half in range(2):
                    q = cnt % 2
                    cnt += 1
                    for c4 in range(4):
                        b = pnext()
                        for cc in range(4):
                            c = half * 16 + c4 * 4 + cc
                            pe(lambda e, b=b, cc=cc, c=c, blk=blk: e.transpose(ps[b].t[:, cc * 128:(cc + 1) * 128], acc.t[:, c, blk * 128:(blk + 1) * 128], sm["identf"].t[:]),
                               reads=[accr(c), sm["identf"].all()], writes=[ps[b].all()])
                        if c4 % 2 == 0:
                            dve(lambda e, b=b, q=q, c4=c4: e.tensor_copy(xs[q][:, c4 * 512:(c4 + 1) * 512], ps[b].t[:]), reads=[ps[b].all()],
                                writes=[ur(q * 8192 + c4 * 2048, 2048)])
                        else:
                            act(lambda e, b=b, q=q, c4=c4: e.activation(out=xs[q][:, c4 * 512:(c4 + 1) * 512], in_=ps[b].t[:], func=AF.Identity), reads=[ps[b].all()],
                                writes=[ur(q * 8192 + c4 * 2048, 2048)])
                    S.dma("sp", out_d[row0 + blk * 128: row0 + (blk + 1) * 128, half * 2048:(half + 1) * 2048], xs[q], reads=[xsr[q]], key="os%d" % q)

        nt = NTILE if not debug else debug.get("_ntile", [NTILE])[0]
        npre = NPRE if not debug else debug.get("_npre", [NPRE])[0]
        e_cnt = [0]
        for ti in range(NPRE - npre, NPRE):
            load_x(ti * T)
            norm_mod(gs1.t, sh1)
            want_kv = (ti == NPRE - 1)
            in_proj(ti, True, want_kv)
            if want_kv:
                shift_halo()
            ssm_pre()
        while ada_slab[0] < 48:
            ada_full_slab()
        dve(lambda e: e.scalar_tensor_tensor(out=gs2.t[:], in0=sc2, scalar=1.0, in1=sm["n2g"].t[:], op0=ALU.add, op1=ALU.mult),
            reads=[mods.all(), sm["n2g"].all()], writes=[gs2.all()])
        dump("mods", mods.t[:], [mods.all()])
        dve(lambda e: e.tensor_scalar(car.t[:], car.t[:], sm["flag"].t[:, 0:1], None, op0=ALU.mult), reads=[car.all(), sm["flag"].all()], writes=[car.all()])
        for i in range(nt):
            ti = NPRE + i
            load_x(ti * T)
            norm_mod(gs1.t, sh1)
            if i == 0:
                dump("x0", acc.t[:, 0:2, :], [acc.all()])
                dump("h1", hb.t[:, 0:2, :], [hb.all()])
                dump("Bexp", Bexp.t[:, 0, :, :], [Bexp.all()])
                dump("rr", rr.t[:], [rr.all()])
                dump("phi", phi.t[:], [phi.all()])
            in_proj(ti, False, True)
            if i == 0:
                dump("q0", Qv[:, 0:2, :], [Qr(0), Qr(1)])
                dump("k0", Kb.t[:, 0:2, :], [Kb.all()])
                dump("ub0", ubv[:, 0:2, :], [ubr(0), ubr(1)])
                dump("va", Va.t[:, 1, :, :], [Va.all()])
            attention(i == 0)
            if i == 0:
                dump("att", hb.t[:, 0:2, :], [hb.all()])
            shift_halo()
            ssm()
            if i == 0:
                dump("gb0", gbv[:, 0:2, :], [Qr(0), Qr(1)])
                dump("car", car.t[:], [car.all()])
            glu_and_norm()
            if i == 0:
                dump("ssm", hb.t[:, 16:18, :], [hb.all()])
            st2 = out_proj()
            if i == 0:
                dump("x1", acc.t[:, 0:2, :], [acc.all()])
            norm_mod(gs2.t, sh2, st2)
            if i == 0:
                dump("h2", hb.t[:, 0:2, :], [hb.all()])
            st3 = ffn()
            if i == 0:
                dump("x2", acc.t[:, 0:2, :], [acc.all()])
            final_out(i * T, st3)
        S.emit()
        build.stats = S.stats
    return nc


def _prep_shared(inp):
    f = np.float32
    sh = {}
    idx = []
    for g in range(8):
        for half in range(2):
            for j in range(4):
                h = 4 * g + j
                idx.extend(range(h * 64 + half * 32, h * 64 + half * 32 + 32))
    for a in range(4):
        for half in range(2):
            for j in range(4):
                idx.extend(range(2048 + a * 64 + half * 32, 2048 + a * 64 + half * 32 + 32))
    idx.extend(range(2560, 4608))
    idx.extend(range(2304, 2560))
    idx = np.asarray(idx)
    assert idx.size == WINP
    sh["w_in"] = np.ascontiguousarray(inp["w_in"][0][:, idx])
    sh["w_ada"] = np.ascontiguousarray(inp["w_ada"][0])
    sh["w_glu"] = np.ascontiguousarray(inp["w_glu"][0])
    sh["w_out"] = np.ascontiguousarray(inp["w_out"][0])
    sh["w_ff1"] = np.ascontiguousarray(inp["w_ff1"][0])
    sh["w_ff2"] = np.ascontiguousarray(inp["w_ff2"][0])
    col = lambda v: np.ascontiguousarray(np.asarray(v, f).reshape(-1, 128).T)
    sh["b_adaT"] = col(inp["b_ada"][0])
    sh["n1g"] = col(inp["norm1_g"][0]); sh["n2g"] = col(inp["norm2_g"][0]); sh["fing"] = col(inp["final_g"])
    sh["sinks_bc"] = np.ascontiguousarray(np.broadcast_to(np.asarray(inp["sinks"][0], f)[None, :], (128, 32)))
    sh["b_gluT"] = col(inp["b_glu"][0]); sh["ssm_gT"] = col(inp["ssm_out_g"][0]); sh["attn_gT"] = col(inp["attn_out_g"][0])
    sh["identf"] = np.eye(128, dtype=f); sh["identb"] = np.eye(128, dtype=f)
    sh["iota"] = np.ascontiguousarray(np.broadcast_to(np.arange(512, dtype=f)[None, :], (128, 512)))
    lre, lim, lst = [np.asarray(inp[k][0], f) for k in ("ssm_lam_re", "ssm_lam_im", "ssm_log_step")]
    bre, bim = np.asarray(inp["ssm_b_re"][0], f), np.asarray(inp["ssm_b_im"][0], f)
    cre, cim = np.asarray(inp["ssm_c_re"][0], f), np.asarray(inp["ssm_c_im"][0], f)
    dsk = np.asarray(inp["ssm_d"][0], f)
    q = np.arange(128)
    pi_ = np.arange(64)
    Gp = 2 * pi_[None, :] + (q // 64)[:, None]
    Pp = np.broadcast_to((q % 64)[:, None], (128, 64))
    sh["lamre_p"] = np.ascontiguousarray(lre[Gp, Pp]); sh["lamim_p"] = np.ascontiguousarray(lim[Gp, Pp])
    sh["lstep_p"] = np.ascontiguousarray(lst[Gp])
    r = np.arange(128)
    cprime = np.arange(16)
    colx = np.arange(128)
    G = (8 * cprime[None, :] + 2 * (r // 32)[:, None] + ((r % 32) // 16)[:, None])[:, :, None]
    P = (colx % 64)[None, None, :]
    H = (r % 16)[:, None, None]
    same = ((colx // 64)[None, None, :] == ((r % 32) // 16)[:, None, None])
    Gb = np.broadcast_to(G, (128, 16, 128)); Pb = np.broadcast_to(P, (128, 16, 128)); Hb = np.broadcast_to(H, (128, 16, 128))
    sh["lamre_e"] = np.ascontiguousarray(lre[Gb, Pb]).reshape(128, 2048)
    sh["lamim_e"] = np.ascontiguousarray(lim[Gb, Pb]).reshape(128, 2048)
    sh["lstep_e"] = np.ascontiguousarray(lst[Gb]).reshape(128, 2048)
    sh["bre_e"] = np.where(same, bre[Gb, Pb, Hb], f(0)).astype(f).reshape(128, 2048)
    sh["bim_e"] = np.where(same, bim[Gb, Pb, Hb], f(0)).astype(f).reshape(128, 2048)
    c32 = np.arange(32)
    Gc = np.broadcast_to((2 * pi_[None, :, None] + (q // 64)[:, None, None]), (128, 64, 32))
    Hc = np.broadcast_to((c32 % 16)[None, None, :], (128, 64, 32))
    Pc = np.broadcast_to((q % 64)[:, None, None], (128, 64, 32))
    samec = ((c32 // 16)[None, None, :] == (q // 64)[:, None, None])
    sh["cre_e"] = np.where(samec, cre[Gc, Hc, Pc], f(0)).astype(f)
    sh["cim_e"] = np.where(samec, cim[Gc, Hc, Pc], f(0)).astype(f)
    ch = (8 * cprime[None, :] + 2 * (r // 32)[:, None]) * 16 + (r % 32)[:, None]
    dd = np.where((c32[None, None, :] == (r % 32)[:, None, None]), dsk[ch][:, :, None], f(0)).astype(f)
    sh["d_e"] = np.ascontiguousarray(dd)
    j = np.arange(128)[:, None]; i = np.arange(128)[None, :]
    mprev = (j > i).astype(f); mcur = (j <= i).astype(f)
    sh["mask_n"] = np.ascontiguousarray(np.concatenate([mprev, mcur], axis=1))
    return sh, (mprev, mcur)


def _rope_tables(base):
    f = np.float32
    half = 32
    inv_freq = (f(10000.0) ** (-(np.arange(half, dtype=f) / f(half)))).astype(f)
    pos = (np.arange(4096, dtype=np.int64) + base).astype(f)
    ang = (pos[None, :] * inv_freq[np.arange(128) % 32][:, None]).astype(f)
    return np.cos(ang).astype(f), np.sin(ang).astype(f)


_NC_CACHE = {}


def kernel(**inputs):
    inp = {k: np.asarray(v) for k, v in inputs.items()}
    f = np.float32
    sh, (mprev, mcur) = _prep_shared(inp)
    x = np.asarray(inp["x"], f)
    c = np.asarray(inp["c"], f)
    in_maps = []
    for core in range(8):
        b, half = core // 2, core % 2
        m = dict(sh)
        if half == 0:
            xcat = np.concatenate([np.zeros((2048, D), f), x[b, 0:2048]], axis=0)
        else:
            xcat = x[b]
        m["xcat"] = np.ascontiguousarray(xcat)
        m["cT"] = np.ascontiguousarray(c[b].reshape(32, 128).T)
        m["flag"] = np.full((128, 1), float(half), f)
        m["mask_f"] = np.ascontiguousarray(np.concatenate([mprev if half == 1 else np.zeros_like(mprev), mcur], axis=1))
        rc, rs = _rope_tables(half * 2048 - 2048)
        m["ropec"], m["ropes"] = rc, rs
        in_maps.append(m)
    if "nc" not in _NC_CACHE:
        _NC_CACHE["nc"] = build()
    nc = _NC_CACHE["nc"]
    res = run_bass_kernel_spmd(nc, in_maps, core_ids=list(range(8)))
    out = np.empty((4, 4096, D), f)
    for core in range(8):
        b, half = core // 2, core % 2
        out[b, half * 2048:(half + 1) * 2048] = res.results[core]["out"]
    return out
```
